# Optimizing a Trainium2 kernel written in Bass

```python
import jax, jax.numpy as jnp
from jax import lax
import numpy as np

D_MODEL = 2048
BATCH = 2
SEQ = 4096
DEPTH = 2

GRID_W = 64
CTX_LEN = 256
BRANCH = 1024
N_BRANCH = 3
EPS = 1e-6

RW_HEADS = 16
RW_HD = 64
RW_LORA = 64
RW_GN_EPS = 64e-5
RW_SHIFT = 3 * BRANCH + 4 * RW_LORA

AT_HEADS = 8
AT_KV = 2
AT_HD = 128
Q_BLOCK = 128
ROPE_THETA = 10000.0

ML_HEADS = 4
ML_DK = 128
ML_DV = 256
ML_CHUNK = 128
GATE_CAP = 15.0

IN_SIZES = (RW_SHIFT, BRANCH,
            AT_HEADS * AT_HD, AT_KV * AT_HD, AT_KV * AT_HD, BRANCH,
            ML_HEADS * ML_DK, ML_HEADS * ML_DK, ML_HEADS * ML_DV, BRANCH, 4 * ML_HEADS, BRANCH,
            N_BRANCH * D_MODEL)
D_IN = sum(IN_SIZES)

kernel_name = "hybrid_rwkv7_gqa_mlstm_prefix_dit_block"


def rms_norm(x, g):
    xf = x.astype(jnp.float32)
    y = xf * lax.rsqrt(jnp.mean(xf * xf, axis=-1, keepdims=True) + EPS)
    return (y * g.astype(jnp.float32)).astype(x.dtype)


def segment_neighbours(u, n_ctx):
    L = u.shape[1]
    pos = jnp.arange(L)
    zero = jnp.zeros((), u.dtype)
    prev = jnp.pad(u[:, :-1], ((0, 0), (1, 0), (0, 0)))
    nxt = jnp.pad(u[:, 1:], ((0, 0), (0, 1), (0, 0)))
    prev = jnp.where((pos == n_ctx)[None, :, None], zero, prev)
    nxt = jnp.where((pos == n_ctx - 1)[None, :, None], zero, nxt)
    return prev, nxt


def axial_rope(x, cos, sin):
    shp = x.shape
    q = AT_HD // 4
    xs = x.reshape(*shp[:-1], 2, 2, q)
    x1, x2 = xs[..., 0, :], xs[..., 1, :]
    expand = (1,) * (x.ndim - 3)
    c = cos.reshape(cos.shape[0], *expand, 2, q).astype(x.dtype)
    s = sin.reshape(sin.shape[0], *expand, 2, q).astype(x.dtype)
    return jnp.stack([x1 * c - x2 * s, x1 * s + x2 * c], axis=-2).reshape(shp)


def rwkv7_scan(r, w, a, b, k, v):
    Bsz, L, H, N = r.shape

    def step(S, inp):
        r_t, w_t, a_t, b_t, k_t, v_t = inp
        sa = jnp.einsum('bhvk,bhk->bhv', S, a_t)
        S = S * w_t[:, :, None, :] + sa[..., None] * b_t[:, :, None, :] + v_t[..., None] * k_t[:, :, None, :]
        return S, jnp.einsum('bhvk,bhk->bhv', S, r_t)

    S0 = jnp.zeros((Bsz, H, N, N), jnp.float32)
    xs = tuple(jnp.swapaxes(t, 0, 1) for t in (r, w, a, b, k, v))
    _, y = lax.scan(step, S0, xs)
    return jnp.swapaxes(y, 0, 1)


def rwkv7_branch(xr, xk, xv, xwd, xad, w_up, w0, a_up, a0, k_k, k_a, r_k, ln_w, ln_b, bwd):
    f32 = jnp.float32
    Bsz, L, _ = xr.shape
    heads = lambda t: t.astype(f32).reshape(Bsz, L, RW_HEADS, RW_HD)
    r, k, v = heads(xr), heads(xk), heads(xv)
    kk = k * k_k.astype(f32).reshape(RW_HEADS, RW_HD)
    kk = kk / jnp.maximum(jnp.linalg.norm(kk, axis=-1, keepdims=True), 1e-12)
    k_a = k_a.astype(f32).reshape(RW_HEADS, RW_HD)
    outs = []
    for d, order in ((0, None), (1, bwd)):
        wd = xwd[..., d * RW_LORA:(d + 1) * RW_LORA].astype(f32)
        ad = xad[..., d * RW_LORA:(d + 1) * RW_LORA].astype(f32)
        w_log = -jax.nn.softplus(-(w0[d].astype(f32) + jnp.tanh(wd) @ w_up[d].astype(f32))) - 0.5
        decay = heads(jnp.exp(-jnp.exp(w_log)))
        a = heads(jax.nn.sigmoid(a0[d].astype(f32) + ad @ a_up[d].astype(f32)))
        kd = k * (1.0 + (a - 1.0) * k_a)
        seqs = (r, decay, -kk, kk * a, kd, v)
        if order is not None:
            seqs = tuple(jnp.take(t, order, axis=1) for t in seqs)
        y_d = rwkv7_scan(*seqs)
        if order is not None:
            y_d = jnp.take(y_d, order, axis=1)
        outs.append(y_d)
    y = outs[0] + outs[1]
    mu = jnp.mean(y, axis=-1, keepdims=True)
    var = jnp.mean(jnp.square(y - mu), axis=-1, keepdims=True)
    y = (y - mu) * lax.rsqrt(var + RW_GN_EPS) * ln_w.astype(f32).reshape(RW_HEADS, RW_HD) \
        + ln_b.astype(f32).reshape(RW_HEADS, RW_HD)
    bonus = jnp.sum(r * k * r_k.astype(f32), axis=-1, keepdims=True) * v
    return (y + bonus).reshape(Bsz, L, BRANCH)


def gqa_branch(q, k, v, q_g, k_g, cos, sin, n_ctx):
    Bsz, L, _ = q.shape
    grp = AT_HEADS // AT_KV
    q = rms_norm(q.reshape(Bsz, L, AT_KV, grp, AT_HD), q_g)
    k = rms_norm(k.reshape(Bsz, L, AT_KV, AT_HD), k_g)
    v = v.reshape(Bsz, L, AT_KV, AT_HD)
    q = axial_rope(q, cos, sin) * (AT_HD ** -0.5)
    k = axial_rope(k, cos, sin)

    def attend(qb, keys, vals):
        s = jnp.einsum('bqgrd,bkgd->bgrqk', qb, keys).astype(jnp.float32)
        p = jax.nn.softmax(s, axis=-1).astype(vals.dtype)
        return jnp.einsum('bgrqk,bkgd->bqgrd', p, vals)

    out_c = attend(q[:, :n_ctx], k[:, :n_ctx], v[:, :n_ctx]).reshape(Bsz, n_ctx, BRANCH)
    n_lat = L - n_ctx
    nb = n_lat // Q_BLOCK
    ql = q[:, n_ctx:].reshape(Bsz, nb, Q_BLOCK, AT_KV, grp, AT_HD)
    ql = jnp.moveaxis(ql, 1, 0)
    out_l = lax.map(lambda blk: attend(blk, k, v), ql)
    out_l = jnp.moveaxis(out_l, 0, 1).reshape(Bsz, n_lat, BRANCH)
    return jnp.concatenate([out_c, out_l], axis=1)


def mlstm_chunk_scan(q, k, v, logi, logf):
    Bsz, H, L, _ = q.shape
    nc = L // ML_CHUNK
    chunks = lambda t: jnp.moveaxis(t.reshape(Bsz, H, nc, ML_CHUNK, *t.shape[3:]), 2, 0)
    tri = jnp.tril(jnp.ones((ML_CHUNK, ML_CHUNK), bool))

    def step(carry, inp):
        C, n, m = carry
        qc, kc, vc, li, lf = inp
        b = jnp.cumsum(lf, axis=-1)
        g = b[..., -1]
        dmat = jnp.where(tri, b[..., :, None] - b[..., None, :] + li[..., None, :], -jnp.inf)
        m_inter = b + m[..., None]
        m_t = jnp.maximum(m_inter, jnp.max(dmat, axis=-1))
        w_inter = jnp.exp(m_inter - m_t)
        s = jnp.einsum('bhtd,bhsd->bhts', qc, kc) * jnp.exp(dmat - m_t[..., None])
        num = w_inter[..., None] * jnp.einsum('bhtd,bhdv->bhtv', qc, C) + jnp.einsum('bhts,bhsv->bhtv', s, vc)
        den = w_inter * jnp.einsum('bhtd,bhd->bht', qc, n) + jnp.sum(s, axis=-1)
        h = num / jnp.maximum(jnp.abs(den), jnp.exp(-m_t))[..., None]
        loga = g[..., None] - b + li
        m_new = jnp.maximum(g + m, jnp.max(loga, axis=-1))
        carry_scale = jnp.exp(g + m - m_new)
        wa = jnp.exp(loga - m_new[..., None])
        C = carry_scale[..., None, None] * C + jnp.einsum('bhs,bhsd,bhsv->bhdv', wa, kc, vc)
        n = carry_scale[..., None] * n + jnp.einsum('bhs,bhsd->bhd', wa, kc)
        return (C, n, m_new), h

    init = (jnp.zeros((Bsz, H, ML_DK, ML_DV), jnp.float32),
            jnp.zeros((Bsz, H, ML_DK), jnp.float32),
            jnp.zeros((Bsz, H), jnp.float32))
    _, h = lax.scan(step, init, tuple(chunks(t) for t in (q, k, v, logi, logf)))
    return jnp.moveaxis(h, 0, 2).reshape(Bsz, H, L, ML_DV)


def mlstm_branch(q, k, v, o, gate_pre, gate_b, norm_g, bwd):
    f32 = jnp.float32
    Bsz, L, _ = q.shape
    to_heads = lambda t, dh: jnp.swapaxes(t.astype(f32).reshape(Bsz, L, ML_HEADS, dh), 1, 2)
    q = to_heads(q, ML_DK) * (ML_DK ** -0.5)
    k = to_heads(k, ML_DK)
    v = to_heads(v, ML_DV)
    pre = gate_pre.astype(f32).reshape(Bsz, L, 4, ML_HEADS) + gate_b.astype(f32)
    pre = GATE_CAP * jnp.tanh(pre / GATE_CAP)
    pre = jnp.transpose(pre, (0, 2, 3, 1))
    logi = pre[:, 0:2]
    logf = jax.nn.log_sigmoid(pre[:, 2:4])
    h_f = mlstm_chunk_scan(q, k, v, logi[:, 0], logf[:, 0])
    flip = lambda t: jnp.take(t, bwd, axis=2)
    h_b = flip(mlstm_chunk_scan(flip(q), flip(k), flip(v), flip(logi[:, 1]), flip(logf[:, 1])))
    h = jnp.swapaxes(h_f + h_b, 1, 2)
    h = rms_norm(h, norm_g.reshape(ML_HEADS, ML_DV)).reshape(Bsz, L, BRANCH)
    return jax.nn.sigmoid(o.astype(f32)) * h


def hybrid_layer(z, mod_c, mod_l, norm_g, w_in, shift_mu, rw_w_up, rw_w0, rw_a_up, rw_a0,
                 rw_k_k, rw_k_a, rw_r_k, rw_ln_w, rw_ln_b, at_q_g, at_k_g, ml_gate_b, ml_norm_g,
                 w_branch, w_out, cos, sin, bwd, n_ctx):
    Bsz, L, _ = z.shape
    sh_c, sc_c, gt_c = jnp.split(mod_c, 3, axis=-1)
    sh_l, sc_l, gt_l = jnp.split(mod_l, 3, axis=-1)
    z_c, z_l = z[:, :n_ctx], z[:, n_ctx:]
    h = jnp.concatenate([rms_norm(z_c, norm_g) * (1 + sc_c) + sh_c,
                         rms_norm(z_l, norm_g) * (1 + sc_l[:, None]) + sh_l[:, None]], axis=1)
    proj = h @ w_in
    split_idx = tuple(int(i) for i in np.cumsum(IN_SIZES)[:-1])
    (rw_s, rw_g, at_q, at_k, at_v, at_g, ml_q, ml_k, ml_v, ml_o, ml_if, ml_g, merge) = \
        jnp.split(proj, split_idx, axis=-1)

    prev, nxt = segment_neighbours(rw_s, n_ctx)
    rw_s = rw_s + shift_mu[0] * (prev - rw_s) + shift_mu[1] * (nxt - rw_s)
    xr, xk, xv, xwd, xad = jnp.split(rw_s, (BRANCH, 2 * BRANCH, 3 * BRANCH, 3 * BRANCH + 2 * RW_LORA), axis=-1)
    y_a = rwkv7_branch(xr, xk, xv, xwd, xad, rw_w_up, rw_w0, rw_a_up, rw_a0, rw_k_k, rw_k_a,
                       rw_r_k, rw_ln_w, rw_ln_b, bwd).astype(z.dtype) * jax.nn.silu(rw_g)
    y_b = gqa_branch(at_q, at_k, at_v, at_q_g, at_k_g, cos, sin, n_ctx) * jax.nn.silu(at_g)
    y_c = mlstm_branch(ml_q, ml_k, ml_v, ml_o, ml_if, ml_gate_b, ml_norm_g, bwd).astype(z.dtype) * jax.nn.silu(ml_g)

    ys = jnp.stack([y_a, y_b, y_c], axis=2)
    branch = jnp.einsum('blnc,ncd->blnd', ys, w_branch)
    gates = jax.nn.sigmoid(merge).reshape(Bsz, L, N_BRANCH, D_MODEL)
    out = jnp.sum(gates * branch, axis=2) @ w_out
    return jnp.concatenate([z_c + gt_c * out[:, :n_ctx],
                            z_l + gt_l[:, None] * out[:, n_ctx:]], axis=1)


def setup_inputs(seed: int = 0) -> dict:
    key = jax.random.key(seed)
    ks = iter(jax.random.split(key, 40))
    f32 = jnp.float32
    nrm = lambda shape, s: jax.random.normal(next(ks), shape, f32) * s
    D = D_MODEL
    x = nrm((BATCH, SEQ, D), 1.0)
    c = nrm((BATCH, D), 1.0)
    ctx = nrm((BATCH, CTX_LEN, D), 1.0)
    c_ctx = nrm((D,), 1.0)
    norm_g = 1.0 + nrm((DEPTH, D), 0.02)
    w_ada = nrm((DEPTH, D, 3 * D), 0.5 * D ** -0.5)
    b_ada = nrm((DEPTH, 3 * D), 0.01)
    w_in = nrm((DEPTH, D, D_IN), D ** -0.5)
    shift_mu = jax.random.uniform(next(ks), (DEPTH, 2, RW_SHIFT), f32, 0.1, 0.5)
    rw_w_up = nrm((DEPTH, 2, RW_LORA, BRANCH), 0.1 * RW_LORA ** -0.5)
    rw_w0 = jnp.linspace(-6.0, -1.0, BRANCH, dtype=f32)[None, None] + nrm((DEPTH, 2, BRANCH), 0.1)
    rw_a_up = nrm((DEPTH, 2, RW_LORA, BRANCH), 0.1 * RW_LORA ** -0.5)
    rw_a0 = nrm((DEPTH, 2, BRANCH), 0.1)
    rw_k_k = 0.85 + nrm((DEPTH, BRANCH), 0.02)
    rw_k_a = 1.0 + nrm((DEPTH, BRANCH), 0.02)
    rw_r_k = nrm((DEPTH, RW_HEADS, RW_HD), 0.1)
    rw_ln_w = 1.0 + nrm((DEPTH, BRANCH), 0.02)
    rw_ln_b = nrm((DEPTH, BRANCH), 0.01)
    at_q_g = 1.0 + nrm((DEPTH, AT_HD), 0.02)
    at_k_g = 1.0 + nrm((DEPTH, AT_HD), 0.02)
    ml_gate_b = jnp.concatenate(
        [nrm((DEPTH, 2, ML_HEADS), 0.1),
         jnp.linspace(3.0, 6.0, ML_HEADS, dtype=f32)[None, None] + nrm((DEPTH, 2, ML_HEADS), 0.1)], axis=1)
    ml_norm_g = 1.0 + nrm((DEPTH, BRANCH), 0.02)
    w_branch = nrm((DEPTH, N_BRANCH, BRANCH, D), BRANCH ** -0.5)
    w_out = nrm((DEPTH, D, D), D ** -0.5)
    final_g = 1.0 + nrm((D,), 0.02)
    return {"x": x, "c": c, "ctx": ctx, "c_ctx": c_ctx, "norm_g": norm_g, "w_ada": w_ada,
            "b_ada": b_ada, "w_in": w_in, "shift_mu": shift_mu, "rw_w_up": rw_w_up, "rw_w0": rw_w0,
            "rw_a_up": rw_a_up, "rw_a0": rw_a0, "rw_k_k": rw_k_k, "rw_k_a": rw_k_a, "rw_r_k": rw_r_k,
            "rw_ln_w": rw_ln_w, "rw_ln_b": rw_ln_b, "at_q_g": at_q_g, "at_k_g": at_k_g,
            "ml_gate_b": ml_gate_b, "ml_norm_g": ml_norm_g, "w_branch": w_branch, "w_out": w_out,
            "final_g": final_g}


def reference(x, c, ctx, c_ctx, norm_g, w_ada, b_ada, w_in, shift_mu, rw_w_up, rw_w0, rw_a_up,
              rw_a0, rw_k_k, rw_k_a, rw_r_k, rw_ln_w, rw_ln_b, at_q_g, at_k_g, ml_gate_b, ml_norm_g,
              w_branch, w_out, final_g):
    f32 = jnp.float32
    n_ctx = ctx.shape[1]
    n_lat = x.shape[1]
    rows = n_lat // GRID_W
    row = jnp.repeat(jnp.arange(rows), GRID_W).astype(f32)
    col = jnp.tile(jnp.arange(GRID_W), rows).astype(f32)
    inv_freq = ROPE_THETA ** (-jnp.arange(0, AT_HD // 2, 2, dtype=f32) / (AT_HD // 2))
    ang_lat = jnp.stack([row[:, None] * inv_freq, col[:, None] * inv_freq], axis=1)
    ang = jnp.concatenate([jnp.zeros((n_ctx, 2, AT_HD // 4), f32), ang_lat], axis=0)
    cos, sin = jnp.cos(ang), jnp.sin(ang)
    bwd = jnp.concatenate([jnp.arange(n_ctx)[::-1], n_ctx + jnp.arange(n_lat)[::-1]])

    z = jnp.concatenate([ctx, x], axis=1)
    for l in range(DEPTH):
        mod_l = jax.nn.silu(c) @ w_ada[l] + b_ada[l]
        mod_c = jax.nn.silu(c_ctx) @ w_ada[l] + b_ada[l]
        z = hybrid_layer(z, mod_c, mod_l, norm_g[l], w_in[l], shift_mu[l], rw_w_up[l], rw_w0[l],
                         rw_a_up[l], rw_a0[l], rw_k_k[l], rw_k_a[l], rw_r_k[l], rw_ln_w[l], rw_ln_b[l],
                         at_q_g[l], at_k_g[l], ml_gate_b[l], ml_norm_g[l], w_branch[l], w_out[l],
                         cos, sin, bwd, n_ctx)
    return rms_norm(z[:, n_ctx:], final_g)
```

```python
import numpy as np
from contextlib import ExitStack
import concourse.bass as bass
import concourse.mybir as mybir

F32 = mybir.dt.float32
BF16 = mybir.dt.bfloat16
AF = mybir.ActivationFunctionType
ALU = mybir.AluOpType
AX = mybir.AxisListType

NS_DMA = 8


class Tile:
    def __init__(self, t, name):
        self.t = t
        self.name = name

    def __getitem__(self, idx):
        return self.t[idx]

    def k(self, *sub):
        return (self.name,) + tuple(sub)


class Prog:
    def __init__(self, nc):
        self.nc = nc
        self.es = ExitStack()
        self.stacks = [self.es]
        self.eng = {"pe": nc.tensor, "act": nc.scalar, "dve": nc.vector, "pool": nc.gpsimd, "sp": nc.sync}
        self.sem = {}
        self.cnt = {}
        for e in ("pe", "act", "dve", "pool"):
            self.sem[e] = self.es.enter_context(nc.semaphore("c_" + e))
            self.cnt[e] = 0
        self.unit = {e: 1 for e in self.sem}
        self.dq_n = {}
        for q in ("sp", "pool", "act"):
            self.dq_n[q] = 0
            for s in range(NS_DMA):
                ch = ("dma", q, s)
                self.sem[ch] = self.es.enter_context(nc.semaphore("d_%s%d" % (q, s)))
                self.cnt[ch] = 0
                self.unit[ch] = 16
        self.seen = {e: {} for e in self.eng}
        self.lastw = {}
        self.readers = {}
        self.nuniq = 0
        self.n_inst = 0

    def sbuf(self, name, shape, dtype=F32):
        name = "s_" + name
        t = self.stacks[-1].enter_context(self.nc.sbuf_tensor(name, list(shape), dtype))
        return Tile(t, name)

    def psum(self, name, shape, dtype=F32):
        name = "p_" + name
        t = self.stacks[-1].enter_context(self.nc.psum_tensor(name, list(shape), dtype))
        return Tile(t, name)

    def dram(self, name, shape, dtype=F32, kind=None):
        if kind is None:
            t = self.nc.dram_tensor(name, list(shape), dtype)
        else:
            t = self.nc.dram_tensor(name, list(shape), dtype, kind=kind)
        return Tile(t.ap(), "D:" + name)

    @staticmethod
    def _overlap(a, b):
        n = min(len(a), len(b))
        return a[:n] == b[:n]

    def _deps(self, rkeys, wkeys):
        deps = set()
        for k in list(rkeys) + list(wkeys):
            d = self.lastw.get(k[0])
            if d:
                for sk, v in d.items():
                    if self._overlap(sk, k):
                        deps.add(v)
        for k in wkeys:
            d = self.readers.get(k[0])
            if d:
                for sk, lst in d.items():
                    if self._overlap(sk, k):
                        deps.update(lst)
        return deps

    def _record(self, rkeys, wkeys, me):
        for k in wkeys:
            d = self.lastw.setdefault(k[0], {})
            for sk in [sk for sk in d if len(sk) >= len(k) and sk[:len(k)] == k]:
                del d[sk]
            d[k] = me
            r = self.readers.get(k[0])
            if r:
                for sk in [sk for sk in r if self._overlap(sk, k)]:
                    if len(sk) >= len(k):
                        del r[sk]
        for k in rkeys:
            r = self.readers.setdefault(k[0], {})
            lst = r.setdefault(k, [])
            lst[:] = [x for x in lst if x[0] != me[0]]
            lst.append(me)

    def _wait(self, e, deps, skip_same=None):
        eng = self.eng[e]
        seen = self.seen[e]
        best = {}
        for ch, n in deps:
            if ch == skip_same:
                continue
            if seen.get(ch, 0) >= n:
                continue
            if best.get(ch, 0) < n:
                best[ch] = n
        for ch, n in best.items():
            eng.wait_ge(self.sem[ch], n * self.unit[ch])
            seen[ch] = n
            self.n_inst += 1

    def op(self, e, fn, r=(), w=()):
        if e != "pe":
            w = list(w) + [(k[0],) for k in r if k[0].startswith("p_")]
        deps = self._deps(r, w)
        self._wait(e, deps, skip_same=("pe" if e == "pe" else None))
        ins = fn(self.eng[e])
        self.cnt[e] += 1
        ins.then_inc(self.sem[e], 1)
        self.n_inst += 1
        me = (e, self.cnt[e])
        self._record(r, w, me)
        return ins

    def dma(self, q, out, in_, r=(), w=(), **kw):
        i = self.dq_n[q]
        self.dq_n[q] += 1
        ch = ("dma", q, i % NS_DMA)
        deps = self._deps(r, w)
        if self.cnt[ch] > 0:
            deps.add((ch, self.cnt[ch]))
        self._wait(q, deps)
        ins = self.eng[q].dma_start(out=out, in_=in_, **kw)
        self.cnt[ch] += 1
        ins.then_inc(self.sem[ch], 16)
        self.n_inst += 1
        self._record(r, w, (ch, self.cnt[ch]))
        return ins

    def barrier(self):
        for e in self.eng:
            deps = set()
            for ch, n in self.cnt.items():
                if n > 0:
                    deps.add((ch, n))
            self._wait(e, deps)
        self.lastw.clear()
        self.readers.clear()

    def push(self):
        self.stacks.append(ExitStack())

    def pop(self):
        self.barrier()
        self.stacks.pop().close()

    def finish(self):
        self.barrier()

    def close(self):
        self.es.close()


KC = 16
HD = 128
EPS = 1e-6


def chunks_of(T, n_ctx):
    out = []
    t = 0
    while t < n_ctx:
        n = min(512, n_ctx - t)
        out.append((t, n))
        t += n
    while t < T:
        n = min(512, T - t)
        out.append((t, n))
        t += n
    return out


def load_w_bf16(P, w_dram, dst_bf, ncols, stage, tag, blk=128):
    i = 0
    for c0 in range(0, ncols, blk):
        nb = min(blk, ncols - c0)
        st = stage[i % 2]
        P.dma("sp", st[:, :, 0:nb], w_dram[:, :, c0:c0 + nb], r=[], w=[st.k()])
        eng = "dve" if i % 2 == 0 else "act"
        if eng == "dve":
            P.op("dve", lambda e: e.tensor_copy(out=dst_bf[:, :, c0:c0 + nb], in_=st[:, :, 0:nb]), r=[st.k()], w=[dst_bf.k(c0)])
        else:
            P.op("act", lambda e: e.activation(out=dst_bf[:, :, c0:c0 + nb], in_=st[:, :, 0:nb], func=AF.Copy), r=[st.k()], w=[dst_bf.k(c0)])
        i += 1


def gqa_program(P, T, n_ctx, io):
    nc = P.nc
    NT = T // 128
    hT, w_fm_d, w_tm_d = io["hT"], io["w_fm"], io["w_tm"]
    qT = [P.sbuf("qT%d" % h, [128, T], BF16) for h in range(2)]
    kT = P.sbuf("kT", [128, T], BF16)
    Vaug = P.sbuf("Vaug", [128, NT, 132], BF16)
    gs = P.sbuf("gs", [128, NT, 256], F32)
    w_fm = P.sbuf("w_fm", [128, KC, 384], BF16)
    w_tm = P.sbuf("w_tm", [128, KC, 384], BF16)
    stage = [P.sbuf("wst%d" % i, [128, KC, 128], F32) for i in range(2)]
    hTc = [P.sbuf("hTc%d" % i, [128, KC, 512], BF16) for i in range(2)]
    cosc = [P.sbuf("cosc%d" % i, [128, 512], F32) for i in range(2)]
    sinc = [P.sbuf("sinc%d" % i, [128, 512], F32) for i in range(2)]
    pm = P.sbuf("pm", [128, 128], F32)
    ones = P.sbuf("ones", [128, 128], F32)
    gvec = P.sbuf("gvec", [128, 4], F32)
    raw = [P.sbuf("raw%d" % i, [128, 512], F32) for i in range(2)]
    sq = P.sbuf("sq", [128, 512], F32)
    rstd = P.sbuf("rstd", [128, 512], F32)
    t1 = P.sbuf("t1", [128, 512], F32)
    t2 = P.sbuf("t2", [128, 512], F32)
    ps_a = [P.psum("ps_a%d" % i, [128, 512], F32) for i in range(2)]
    ps_b = P.psum("ps_b", [128, 512], F32)
    ps_c = P.psum("ps_c", [128, 512], F32)

    P.dma("sp", pm[:], io["pm"][:, :], w=[pm.k()])
    P.dma("sp", ones[:], io["ones"][:, :], w=[ones.k()])
    P.dma("sp", gvec[:], io["gvec"][:, :], w=[gvec.k()])
    load_w_bf16(P, w_fm_d, w_fm, 384, stage, "fm")
    load_w_bf16(P, w_tm_d, w_tm, 384, stage, "tm")
    P.op("pool", lambda e: e.memset(Vaug[:, :, 128:132], 1.0), w=[Vaug.k("ones")])

    scale = HD ** -0.5
    for ci, (t0, n) in enumerate(chunks_of(T, n_ctx)):
        hc = hTc[ci % 2]
        P.dma("sp", hc[:, :, 0:n], hT[:, :, t0:t0 + n].rearrange("k p t -> p k t"), w=[hc.k()])
        cc, sc = cosc[ci % 2], sinc[ci % 2]
        P.dma("sp", cc[:, 0:n], io["cosT"][:, t0:t0 + n], w=[cc.k()])
        P.dma("sp", sc[:, 0:n], io["sinT"][:, t0:t0 + n], w=[sc.k()])
        for bi in range(3):
            ps = ps_a[bi % 2]
            for kc in range(KC):
                P.op("pe", lambda e: e.matmul(ps[:, 0:n], lhsT=w_fm[:, kc, bi * 128:(bi + 1) * 128], rhs=hc[:, kc, 0:n],
                                               start=(kc == 0), stop=(kc == KC - 1)),
                     r=[w_fm.k(bi * 128), hc.k()], w=[ps.k()])
            rw = raw[bi % 2]
            P.op("act", lambda e: e.activation(out=rw[:, 0:n], in_=ps[:, 0:n], func=AF.Copy), r=[ps.k()], w=[rw.k()])
            P.op("dve", lambda e: e.tensor_tensor(out=sq[:, 0:n], in0=rw[:, 0:n], in1=rw[:, 0:n], op=ALU.mult), r=[rw.k()], w=[sq.k()])
            P.op("pe", lambda e: e.matmul(ps_b[:, 0:n], lhsT=ones[:, :], rhs=sq[:, 0:n], start=True, stop=True),
                 r=[ones.k(), sq.k()], w=[ps_b.k()])
            P.op("pe", lambda e: e.matmul(ps_c[:, 0:n], lhsT=pm[:, :], rhs=rw[:, 0:n], start=True, stop=True),
                 r=[pm.k(), rw.k()], w=[ps_c.k()])
            P.op("act", lambda e: e.activation(out=rstd[:, 0:n], in_=ps_b[:, 0:n], func=AF.Sqrt, scale=1.0 / HD, bias=EPS),
                 r=[ps_b.k()], w=[rstd.k()])
            P.op("dve", lambda e: e.reciprocal(out=rstd[:, 0:n], in_=rstd[:, 0:n]), r=[rstd.k()], w=[rstd.k()])
            gi = 0 if bi < 2 else 2
            P.op("dve", lambda e: e.scalar_tensor_tensor(out=t1[:, 0:n], in0=rw[:, 0:n], scalar=gvec[:, gi:gi + 1], in1=cc[:, 0:n],
                                                          op0=ALU.mult, op1=ALU.mult), r=[rw.k(), gvec.k(), cc.k()], w=[t1.k()])
            P.op("dve", lambda e: e.scalar_tensor_tensor(out=t2[:, 0:n], in0=ps_c[:, 0:n], scalar=gvec[:, gi + 1:gi + 2], in1=sc[:, 0:n],
                                                          op0=ALU.mult, op1=ALU.mult), r=[ps_c.k(), gvec.k(), sc.k()], w=[t2.k()])
            P.op("dve", lambda e: e.tensor_tensor(out=t1[:, 0:n], in0=t1[:, 0:n], in1=t2[:, 0:n], op=ALU.add), r=[t1.k(), t2.k()], w=[t1.k()])
            dst = qT[bi] if bi < 2 else kT
            sc_f = scale if bi < 2 else 1.0
            P.op("dve", lambda e: e.scalar_tensor_tensor(out=dst[:, t0:t0 + n], in0=t1[:, 0:n], scalar=sc_f, in1=rstd[:, 0:n],
                                                          op0=ALU.mult, op1=ALU.mult), r=[t1.k(), rstd.k()], w=[dst.k(ci)])
        for ti in range(n // 128):
            tt = (t0 // 128) + ti
            ps = ps_a[ti % 2]
            for kc in range(KC):
                P.op("pe", lambda e: e.matmul(ps[:, 0:384], lhsT=hc[:, kc, ti * 128:(ti + 1) * 128], rhs=w_tm[:, kc, 0:384],
                                               start=(kc == 0), stop=(kc == KC - 1)),
                     r=[w_tm.k(), hc.k()], w=[ps.k()])
            P.op("dve", lambda e: e.tensor_copy(out=Vaug[:, tt, 0:128], in_=ps[:, 0:128]), r=[ps.k()], w=[Vaug.k("v", tt)])
            P.op("act", lambda e: e.activation(out=gs[:, tt, :], in_=ps[:, 128:384], func=AF.Silu), r=[ps.k()], w=[gs.k(tt)])

    pex = [P.sbuf("pex%d" % i, [128, 512], BF16) for i in range(3)]
    acc = [ps_b, ps_c] + [P.psum("acc%d" % i, [128, 512], F32) for i in range(2)]
    rec = P.sbuf("rec", [128, 4], F32)
    yt = [P.sbuf("yt%d" % i, [128, 128], F32) for i in range(2)]
    yo = [P.sbuf("yo%d" % i, [128, 128], BF16) for i in range(2)]
    nctx_t = n_ctx // 128
    it = 0
    for h in range(2):
        for ci, (t0, n) in enumerate(chunks_of(T, n_ctx)):
            nk = nctx_t if t0 < n_ctx else NT
            nsub = n // 128
            for kt in range(nk):
                ps = ps_a[it % 2]
                px = pex[it % 3]
                it += 1
                P.op("pe", lambda e: e.matmul(ps[:, 0:n], lhsT=kT[:, kt * 128:(kt + 1) * 128], rhs=qT[h][:, t0:t0 + n], start=True, stop=True),
                     r=[kT.k(), qT[h].k(ci)], w=[ps.k()])
                P.op("act", lambda e: e.activation(out=px[:, 0:n], in_=ps[:, 0:n], func=AF.Exp), r=[ps.k()], w=[px.k()])
                for s in range(nsub):
                    a = acc[s]
                    P.op("pe", lambda e: e.matmul(a[:, 0:129], lhsT=px[:, s * 128:(s + 1) * 128], rhs=Vaug[:, kt, 0:129],
                                                   start=(kt == 0), stop=(kt == nk - 1)),
                         r=[px.k(), Vaug.k()], w=[a.k()])
            for s in range(nsub):
                a = acc[s]
                tt = t0 // 128 + s
                y1 = yt[s % 2]
                y2 = yo[s % 2]
                P.op("dve", lambda e: e.reciprocal(out=rec[:, s:s + 1], in_=a[:, 128:129]), r=[a.k()], w=[rec.k(s)])
                P.op("dve", lambda e: e.scalar_tensor_tensor(out=y2[:, :], in0=a[:, 0:128], scalar=rec[:, s:s + 1],
                                                              in1=gs[:, tt, h * 128:(h + 1) * 128], op0=ALU.mult, op1=ALU.mult),
                     r=[a.k(), rec.k(s), gs.k(tt)], w=[y2.k()])
                P.dma("pool", io["yb"][tt * 128:(tt + 1) * 128, h * 128:(h + 1) * 128], y2[:, :], r=[y2.k()], w=[io["yb"].k(tt, h)])


GATE_CAP = 15.0
NEG = -30000.0


def mlstm_program(P, T, n_ctx, io):
    NCH = T // 128
    hT = io["hT"]
    qT = P.sbuf("m_qT", [128, T], F32)
    qTb = P.sbuf("m_qTb", [128, T], BF16)
    kTb = P.sbuf("m_kTb", [128, T], BF16)
    ktm = P.sbuf("m_ktm", [128, NCH, 128], F32)
    vaug = P.sbuf("m_vaug", [128, NCH, 260], BF16)
    og = P.sbuf("m_og", [128, NCH, 256], F32)
    gpre = P.sbuf("m_gpre", [128, NCH, 4], F32)
    lfn = P.sbuf("m_lfn", [128, NCH, 2], F32)
    gbias = P.sbuf("m_gbias", [128, 4], F32)
    normg = P.sbuf("m_normg", [128, 256], F32)
    tri = [P.sbuf("m_tri%d" % d, [128, 128], F32) for d in range(2)]
    mneg = [P.sbuf("m_mneg%d" % d, [128, 128], F32) for d in range(2)]
    onesf = P.sbuf("m_onesf", [128, 128], F32)
    pb = [P.psum("m_pb%d" % i, [128, 512], F32) for i in range(8)]
    P.push()
    w_fm = P.sbuf("m_wfm", [128, KC, 256], BF16)
    w_tm = P.sbuf("m_wtm", [128, KC, 900], BF16)
    stage = [P.sbuf("m_wst%d" % i, [128, KC, 128], F32) for i in range(2)]
    hTc = [P.sbuf("m_hTc%d" % i, [128, KC, 512], BF16) for i in range(2)]
    sig = P.sbuf("m_sig", [128, 256], F32)
    sil = P.sbuf("m_sil", [128, 256], F32)

    P.dma("sp", gbias[:], io["gbias"][:, :], w=[gbias.k()])
    P.dma("sp", normg[:], io["normg"][:, :], w=[normg.k()])
    for d, nm in enumerate(("tri_f", "tri_b")):
        P.dma("sp", tri[d][:], io[nm][:, :], w=[tri[d].k()])
    for d, nm in enumerate(("mneg_f", "mneg_b")):
        P.dma("sp", mneg[d][:], io[nm][:, :], w=[mneg[d].k()])
    P.op("pool", lambda e: e.memset(onesf[:, :], 1.0), w=[onesf.k()])
    P.op("pool", lambda e: e.memset(vaug[:, :, 256:260], 1.0), w=[vaug.k("ones")])
    load_w_bf16(P, io["w_fm"], w_fm, 256, stage, "fm")
    load_w_bf16(P, io["w_tm"], w_tm, 900, stage, "tm")

    qscale = 128 ** -0.5
    for ci, (t0, n) in enumerate(chunks_of(T, n_ctx)):
        hc = hTc[ci % 2]
        P.dma("sp", hc[:, :, 0:n], hT[:, :, t0:t0 + n].rearrange("k p t -> p k t"), w=[hc.k()])
        for bi in range(2):
            ps = pb[bi]
            for kc in range(KC):
                P.op("pe", lambda e: e.matmul(ps[:, 0:n], lhsT=w_fm[:, kc, bi * 128:(bi + 1) * 128], rhs=hc[:, kc, 0:n],
                                               start=(kc == 0), stop=(kc == KC - 1)), r=[w_fm.k(bi * 128), hc.k()], w=[ps.k()])
            if bi == 0:
                P.op("act", lambda e: e.activation(out=qT[:, t0:t0 + n], in_=ps[:, 0:n], func=AF.Copy, scale=qscale), r=[ps.k()], w=[qT.k(ci)])
                P.op("dve", lambda e: e.tensor_copy(out=qTb[:, t0:t0 + n], in_=qT[:, t0:t0 + n]), r=[qT.k(ci)], w=[qTb.k(ci)])
            else:
                P.op("act", lambda e: e.activation(out=kTb[:, t0:t0 + n], in_=ps[:, 0:n], func=AF.Copy), r=[ps.k()], w=[kTb.k(ci)])
        for ti in range(n // 128):
            c = t0 // 128 + ti
            p1, p2 = pb[2 + (ti % 2) * 2], pb[3 + (ti % 2) * 2]
            for kc in range(KC):
                P.op("pe", lambda e: e.matmul(p1[:, 0:388], lhsT=hc[:, kc, ti * 128:(ti + 1) * 128], rhs=w_tm[:, kc, 0:388],
                                               start=(kc == 0), stop=(kc == KC - 1)), r=[w_tm.k(), hc.k()], w=[p1.k()])
            for kc in range(KC):
                P.op("pe", lambda e: e.matmul(p2[:, 0:512], lhsT=hc[:, kc, ti * 128:(ti + 1) * 128], rhs=w_tm[:, kc, 388:900],
                                               start=(kc == 0), stop=(kc == KC - 1)), r=[w_tm.k(), hc.k()], w=[p2.k()])
            P.op("dve", lambda e: e.tensor_copy(out=ktm[:, c, :], in_=p1[:, 0:128]), r=[p1.k()], w=[ktm.k(c)])
            P.op("act", lambda e: e.activation(out=vaug[:, c, 0:256], in_=p1[:, 128:384], func=AF.Copy), r=[p1.k()], w=[vaug.k("v", c)])
            P.op("dve", lambda e: e.tensor_tensor(out=gpre[:, c, :], in0=p1[:, 384:388], in1=gbias[:, :], op=ALU.add), r=[p1.k(), gbias.k()], w=[gpre.k(c)])
            P.op("act", lambda e: e.activation(out=sig[:, :], in_=p2[:, 0:256], func=AF.Sigmoid), r=[p2.k()], w=[sig.k()])
            P.op("act", lambda e: e.activation(out=sil[:, :], in_=p2[:, 256:512], func=AF.Silu), r=[p2.k()], w=[sil.k()])
            P.op("dve", lambda e: e.tensor_tensor(out=og[:, c, :], in0=sig[:, :], in1=sil[:, :], op=ALU.mult), r=[sig.k(), sil.k()], w=[og.k(c)])

    P.pop()
    P.push()
    Hacc = P.sbuf("m_Hacc", [128, NCH, 256], F32)
    P.op("act", lambda e: e.activation(out=gpre[:, :, :], in_=gpre[:, :, :], func=AF.Tanh, scale=1.0 / GATE_CAP), r=[gpre.k()], w=[gpre.k()])
    P.op("dve", lambda e: e.tensor_scalar(out=gpre[:, :, :], in0=gpre[:, :, :], scalar1=GATE_CAP, scalar2=None, op0=ALU.mult), r=[gpre.k()], w=[gpre.k()])
    P.op("act", lambda e: e.activation(out=lfn[:, :, :], in_=gpre[:, :, 2:4], func=AF.Exp, scale=-1.0), r=[gpre.k()], w=[lfn.k()])
    P.op("act", lambda e: e.activation(out=lfn[:, :, :], in_=lfn[:, :, :], func=AF.Ln, bias=1.0), r=[lfn.k()], w=[lfn.k()])
    P.op("dve", lambda e: e.tensor_scalar(out=lfn[:, :, :], in0=lfn[:, :, :], scalar1=-1.0, scalar2=None, op0=ALU.mult), r=[lfn.k()], w=[lfn.k()])

    nctx_c = n_ctx // 128
    order_f = list(range(NCH))
    order_b = list(range(nctx_c - 1, -1, -1)) + list(range(NCH - 1, nctx_c - 1, -1))
    st = []
    for d in range(2):
        s = {}
        s["Cn"] = P.sbuf("m_Cn%d" % d, [128, 260], F32)
        s["Cnb"] = P.sbuf("m_Cnb%d" % d, [128, 260], BF16)
        for nm, shp, dt in (("LFbc", [128, 128], F32), ("tmp", [128, 128], F32), ("ET", [128, 128], F32), ("expB", [128, 128], F32),
                            ("qt", [128, 128], BF16), ("smT", [128, 128], BF16), ("kw", [128, 128], BF16), ("hd", [128, 256], F32)):
            s[nm] = P.sbuf("m_%s%d" % (nm, d), shp, dt)
        s["col"] = P.sbuf("m_col%d" % d, [128, 8], F32)
        P.op("pool", lambda e: e.memset(s["Cn"][:, :], 0.0), w=[s["Cn"].k()])
        P.op("pool", lambda e: e.memset(s["Cnb"][:, :], 0.0), w=[s["Cnb"].k()])
        st.append(s)

    def step(d, c, first):
        s = st[d]
        p1, p2, p3, p4 = pb[4 * d], pb[4 * d + 1], pb[4 * d + 2], pb[4 * d + 3]
        col = s["col"]
        gcol = 127 if d == 0 else 0
        tsl = slice(c * 128, (c + 1) * 128)
        P.op("dve", lambda e: e.tensor_scalar(out=s["LFbc"][:, :], in0=onesf[:, :], scalar1=lfn[:, c, d:d + 1], scalar2=None, op0=ALU.mult),
             r=[onesf.k(), lfn.k()], w=[s["LFbc"].k()])
        P.op("pe", lambda e: e.matmul(p1[:, 0:128], lhsT=s["LFbc"][:, :], rhs=tri[d][:, :], start=True, stop=True),
             r=[s["LFbc"].k(), tri[d].k()], w=[p1.k()])
        P.op("pe", lambda e: e.matmul(p1[:, 128:129], lhsT=tri[d][:, :], rhs=lfn[:, c, d:d + 1], start=True, stop=True),
             r=[tri[d].k(), lfn.k()], w=[p1.k()])
        P.op("dve", lambda e: e.tensor_tensor(out=col[:, 0:1], in0=gpre[:, c, d:d + 1], in1=p1[:, 128:129], op=ALU.subtract),
             r=[gpre.k(), p1.k()], w=[col.k(0)])
        P.op("dve", lambda e: e.tensor_tensor(out=s["tmp"][:, :], in0=p1[:, 0:128], in1=mneg[d][:, :], op=ALU.add),
             r=[p1.k(), mneg[d].k()], w=[s["tmp"].k()])
        P.op("act", lambda e: e.activation(out=s["ET"][:, :], in_=s["tmp"][:, :], func=AF.Exp, bias=col[:, 0:1]),
             r=[s["tmp"].k(), col.k(0)], w=[s["ET"].k()])
        P.op("act", lambda e: e.activation(out=s["expB"][:, :], in_=p1[:, 0:128], func=AF.Exp), r=[p1.k()], w=[s["expB"].k()])
        P.op("act", lambda e: e.activation(out=col[:, 1:2], in_=p1[:, gcol:gcol + 1], func=AF.Exp, bias=col[:, 0:1]),
             r=[p1.k(), col.k(0)], w=[col.k(1)])
        P.op("dve", lambda e: e.tensor_tensor(out=s["qt"][:, :], in0=qT[:, tsl], in1=s["expB"][:, :], op=ALU.mult),
             r=[qT.k(), s["expB"].k()], w=[s["qt"].k()])
        P.op("pe", lambda e: e.matmul(p2[:, 0:128], lhsT=kTb[:, tsl], rhs=qTb[:, tsl], start=True, stop=True),
             r=[kTb.k(), qTb.k()], w=[p2.k()])
        P.op("dve", lambda e: e.tensor_tensor(out=s["smT"][:, :], in0=p2[:, 0:128], in1=s["ET"][:, :], op=ALU.mult),
             r=[p2.k(), s["ET"].k()], w=[s["smT"].k()])
        P.op("pe", lambda e: e.matmul(p3[:, 0:257], lhsT=s["smT"][:, :], rhs=vaug[:, c, 0:257], start=True, stop=False),
             r=[s["smT"].k(), vaug.k()], w=[p3.k()])
        P.op("pe", lambda e: e.matmul(p3[:, 0:257], lhsT=s["qt"][:, :], rhs=s["Cnb"][:, 0:257], start=False, stop=True),
             r=[s["qt"].k(), s["Cnb"].k()], w=[p3.k()])
        P.op("act", lambda e: e.activation(out=col[:, 2:3], in_=p3[:, 256:257], func=AF.Abs), r=[p3.k()], w=[col.k(2)])
        P.op("dve", lambda e: e.tensor_scalar(out=col[:, 2:3], in0=col[:, 2:3], scalar1=1.0, scalar2=None, op0=ALU.max),
             r=[col.k(2)], w=[col.k(2)])
        P.op("dve", lambda e: e.reciprocal(out=col[:, 3:4], in_=col[:, 2:3]), r=[col.k(2)], w=[col.k(3)])
        if first:
            P.op("dve", lambda e: e.tensor_scalar(out=Hacc[:, c, :], in0=p3[:, 0:256], scalar1=col[:, 3:4], scalar2=None, op0=ALU.mult),
                 r=[p3.k(), col.k(3)], w=[Hacc.k(c)])
        else:
            P.op("dve", lambda e: e.scalar_tensor_tensor(out=Hacc[:, c, :], in0=p3[:, 0:256], scalar=col[:, 3:4], in1=Hacc[:, c, :],
                                                          op0=ALU.mult, op1=ALU.add), r=[p3.k(), col.k(3), Hacc.k(c)], w=[Hacc.k(c)])
        P.op("dve", lambda e: e.tensor_scalar(out=s["kw"][:, :], in0=ktm[:, c, :], scalar1=col[:, 1:2], scalar2=None, op0=ALU.mult),
             r=[ktm.k(c), col.k(1)], w=[s["kw"].k()])
        P.op("pe", lambda e: e.matmul(p4[:, 0:257], lhsT=s["kw"][:, :], rhs=vaug[:, c, 0:257], start=True, stop=True),
             r=[s["kw"].k(), vaug.k()], w=[p4.k()])
        P.op("dve", lambda e: e.scalar_tensor_tensor(out=s["Cn"][:, 0:257], in0=s["Cn"][:, 0:257], scalar=s["expB"][:, gcol:gcol + 1], in1=p4[:, 0:257],
                                                      op0=ALU.mult, op1=ALU.add), r=[s["Cn"].k(), s["expB"].k(), p4.k()], w=[s["Cn"].k()])
        P.op("act", lambda e: e.activation(out=s["Cnb"][:, 0:257], in_=s["Cn"][:, 0:257], func=AF.Copy), r=[s["Cn"].k()], w=[s["Cnb"].k()])

    done = set()
    for i in range(NCH):
        for d, order in ((0, order_f), (1, order_b)):
            c = order[i]
            step(d, c, c not in done)
            done.add(c)

    ssq = P.sbuf("m_ssq", [128, NCH], F32)
    junk = P.sbuf("m_junk", [128, 256], F32)
    yo = [P.sbuf("m_yo%d" % i, [128, 256], BF16) for i in range(2)]
    ytmp = [P.sbuf("m_ytmp%d" % i, [128, 256], F32) for i in range(2)]
    for c in range(NCH):
        P.op("dve", lambda e: e.tensor_tensor(out=junk[:, :], in0=Hacc[:, c, :], in1=Hacc[:, c, :], op=ALU.mult), r=[Hacc.k(c)], w=[junk.k()])
        P.op("dve", lambda e: e.reduce_sum(out=ssq[:, c:c + 1], in_=junk[:, :], axis=AX.X), r=[junk.k()], w=[ssq.k(c)])
    P.op("act", lambda e: e.activation(out=ssq[:, :], in_=ssq[:, :], func=AF.Sqrt, scale=1.0 / 256, bias=1e-6), r=[ssq.k()], w=[ssq.k()])
    P.op("dve", lambda e: e.reciprocal(out=ssq[:, :], in_=ssq[:, :]), r=[ssq.k()], w=[ssq.k()])
    for c in range(NCH):
        yt, y2 = ytmp[c % 2], yo[c % 2]
        P.op("dve", lambda e: e.scalar_tensor_tensor(out=yt[:, :], in0=Hacc[:, c, :], scalar=ssq[:, c:c + 1], in1=normg[:, :],
                                                      op0=ALU.mult, op1=ALU.mult), r=[Hacc.k(c), ssq.k(), normg.k()], w=[yt.k()])
        P.op("dve", lambda e: e.tensor_tensor(out=y2[:, :], in0=yt[:, :], in1=og[:, c, :], op=ALU.mult), r=[yt.k(), og.k(c)], w=[y2.k()])
        P.dma("pool", io["yc"][c * 128:(c + 1) * 128, :], y2[:, :], r=[y2.k()], w=[io["yc"].k(c)])
    P.pop()


RW_GN_EPS = 64e-5
C64 = 64


def seg_chunks(T, n_ctx, n=256):
    out = []
    for (a, b) in ((0, n_ctx), (n_ctx, T)):
        t = a
        while t < b:
            m = min(n, b - t)
            out.append((t, m, a, b))
            t += m
    return out


def rwkv_program(P, T, n_ctx, io, tag=""):
    NC64 = T // C64
    nctx_c = n_ctx // C64
    hT = io["hT"]
    R = P.sbuf(tag + "R", [128, T], F32)
    A = P.sbuf(tag + "A", [128, T], F32)
    KD = [P.sbuf(tag + "KD%d" % d, [128, T], F32) for d in range(2)]
    BB = [P.sbuf(tag + "BB%d" % d, [128, T], F32) for d in range(2)]
    LW = [P.sbuf(tag + "LW%d" % d, [128, T], F32) for d in range(2)]
    vtm = P.sbuf(tag + "vtm", [128, NC64, 64], F32)
    bon = P.sbuf(tag + "bon", [128, NC64], F32)
    ident = P.sbuf(tag + "ident", [128, 128], F32)
    id64 = P.sbuf(tag + "id64", [128, 64], F32)
    onesc = P.sbuf(tag + "onesc", [128, 64], F32)
    pcol = P.sbuf(tag + "pcol", [128, 8], F32)
    P.dma("sp", ident[:], io["ident"][:, :], w=[ident.k()])
    P.dma("sp", id64[:], io["id64"][:, :], w=[id64.k()])
    P.dma("sp", onesc[:], io["onesc"][:, :], w=[onesc.k()])
    P.dma("sp", pcol[:], io["pcol"][:, :], w=[pcol.k()])

    P.push()
    w_fm = P.sbuf(tag + "wfm", [128, KC, 640], BF16)
    stage = [P.sbuf(tag + "wst%d" % i, [128, KC, 64], F32) for i in range(2)]
    hTc = [P.sbuf(tag + "hTc0", [128, KC, 258], BF16)] * 2
    mu = P.sbuf(tag + "mu", [128, 16], F32)
    wup = P.sbuf(tag + "wup", [128, 128], F32)
    aup = P.sbuf(tag + "aup", [128, 128], F32)
    bones = P.sbuf(tag + "bones", [128, 128], F32)
    tmpn = {}
    for nm in ("K", "Vf", "WD", "AD", "tw", "as0", "as1", "kk0", "sq", "rn", "kka", "t1", "rkr"):
        tmpn[nm] = P.sbuf(tag + "t_" + nm, [128, 256], F32)
    ps = [P.psum(tag + "pp%d" % i, [128, 512], F32) for i in range(8)]
    P.dma("sp", mu[:, 0:10], io["mu"][:, :], w=[mu.k()])
    P.dma("sp", wup[:], io["wup"][:, :], w=[wup.k()])
    P.dma("sp", aup[:], io["aup"][:, :], w=[aup.k()])
    P.dma("sp", bones[:], io["bones"][:, :], w=[bones.k()])
    load_w_bf16(P, io["w_fm"], w_fm, 640, stage, "fm", blk=64)
    P.op("dve", lambda e: e.tensor_tensor(out=mu[:, 10:15], in0=mu[:, 0:5], in1=mu[:, 5:10], op=ALU.add), r=[mu.k()], w=[mu.k()])
    P.op("dve", lambda e: e.tensor_scalar(out=mu[:, 10:15], in0=mu[:, 10:15], scalar1=-1.0, scalar2=1.0, op0=ALU.mult, op1=ALU.add), r=[mu.k()], w=[mu.k()])

    for ci, (t0, n, sa, sb) in enumerate(seg_chunks(T, n_ctx)):
        lo, hi = max(sa, t0 - 1), min(sb, t0 + n + 1)
        nn = hi - lo
        o = t0 - lo
        hc = hTc[ci % 2]
        P.dma("sp", hc[:, :, 0:nn], hT[:, :, lo:hi].rearrange("k p t -> p k t"), w=[hc.k()])
        dsts = [R[:, t0:t0 + n], tmpn["K"][:, 0:n], tmpn["Vf"][:, 0:n], tmpn["WD"][:, 0:n], tmpn["AD"][:, 0:n]]
        dkeys = [R.k(ci), tmpn["K"].k(), tmpn["Vf"].k(), tmpn["WD"].k(), tmpn["AD"].k()]
        for bi in range(5):
            pp = ps[bi % 2]
            for kc in range(KC):
                P.op("pe", lambda e: e.matmul(pp[:, 0:nn], lhsT=w_fm[:, kc, bi * 128:(bi + 1) * 128], rhs=hc[:, kc, 0:nn],
                                               start=(kc == 0), stop=(kc == KC - 1)), r=[w_fm.k(), hc.k()], w=[pp.k()])
            dst, dk = dsts[bi], dkeys[bi]
            P.op("act", lambda e: e.activation(out=dst, in_=pp[:, o:o + n], func=AF.Copy, scale=mu[:, 10 + bi:11 + bi]), r=[pp.k(), mu.k()], w=[dk])
            j0 = 0 if o == 1 else 1
            P.op("dve", lambda e: e.scalar_tensor_tensor(out=dst[:, j0:n], in0=pp[:, o + j0 - 1:o + n - 1], scalar=mu[:, bi:bi + 1], in1=dst[:, j0:n],
                                                          op0=ALU.mult, op1=ALU.add), r=[pp.k(), mu.k(), dk], w=[dk])
            j1 = n if (o + n + 1 <= nn) else n - 1
            P.op("dve", lambda e: e.scalar_tensor_tensor(out=dst[:, 0:j1], in0=pp[:, o + 1:o + 1 + j1], scalar=mu[:, 5 + bi:6 + bi], in1=dst[:, 0:j1],
                                                          op0=ALU.mult, op1=ALU.add), r=[pp.k(), mu.k(), dk], w=[dk])
        K, Vf, WD, AD = tmpn["K"], tmpn["Vf"], tmpn["WD"], tmpn["AD"]
        tw, kk0, sq, rn, kka, t1, rkr = (tmpn[x] for x in ("tw", "kk0", "sq", "rn", "kka", "t1", "rkr"))
        asg = [tmpn["as0"], tmpn["as1"]]
        P.op("act", lambda e: e.activation(out=tw[:, 0:n], in_=WD[:, 0:n], func=AF.Tanh), r=[WD.k()], w=[tw.k()])
        for d in range(2):
            rows = slice(64 * d, 64 * d + 64)
            px = ps[2 + d]
            P.op("pe", lambda e: e.matmul(px[:, 0:n], lhsT=wup[rows, :], rhs=tw[rows, 0:n], start=True, stop=True), r=[wup.k(), tw.k()], w=[px.k()])
            P.op("act", lambda e: e.activation(out=LW[d][:, t0:t0 + n], in_=px[:, 0:n], func=AF.Sigmoid, bias=pcol[:, d:d + 1]), r=[px.k(), pcol.k()], w=[LW[d].k(ci)])
            P.op("dve", lambda e: e.tensor_scalar(out=LW[d][:, t0:t0 + n], in0=LW[d][:, t0:t0 + n], scalar1=-float(np.exp(-0.5)), scalar2=None, op0=ALU.mult),
                 r=[LW[d].k(ci)], w=[LW[d].k(ci)])
            pa = ps[4 + d]
            P.op("pe", lambda e: e.matmul(pa[:, 0:n], lhsT=aup[rows, :], rhs=AD[rows, 0:n], start=True, stop=True), r=[aup.k(), AD.k()], w=[pa.k()])
            P.op("act", lambda e: e.activation(out=asg[d][:, 0:n], in_=pa[:, 0:n], func=AF.Sigmoid, bias=pcol[:, 2 + d:3 + d]), r=[pa.k(), pcol.k()], w=[asg[d].k()])
        P.op("dve", lambda e: e.tensor_scalar(out=kk0[:, 0:n], in0=K[:, 0:n], scalar1=pcol[:, 4:5], scalar2=None, op0=ALU.mult), r=[K.k(), pcol.k()], w=[kk0.k()])
        P.op("dve", lambda e: e.tensor_tensor(out=sq[:, 0:n], in0=kk0[:, 0:n], in1=kk0[:, 0:n], op=ALU.mult), r=[kk0.k()], w=[sq.k()])
        pn = ps[6]
        P.op("pe", lambda e: e.matmul(pn[:, 0:n], lhsT=bones[:, :], rhs=sq[:, 0:n], start=True, stop=True), r=[bones.k(), sq.k()], w=[pn.k()])
        P.op("act", lambda e: e.activation(out=rn[:, 0:n], in_=pn[:, 0:n], func=AF.Sqrt), r=[pn.k()], w=[rn.k()])
        P.op("dve", lambda e: e.tensor_scalar(out=rn[:, 0:n], in0=rn[:, 0:n], scalar1=1e-12, scalar2=None, op0=ALU.max), r=[rn.k()], w=[rn.k()])
        P.op("dve", lambda e: e.reciprocal(out=rn[:, 0:n], in_=rn[:, 0:n]), r=[rn.k()], w=[rn.k()])
        P.op("dve", lambda e: e.scalar_tensor_tensor(out=A[:, t0:t0 + n], in0=kk0[:, 0:n], scalar=-1.0, in1=rn[:, 0:n], op0=ALU.mult, op1=ALU.mult),
             r=[kk0.k(), rn.k()], w=[A.k(ci)])
        P.op("dve", lambda e: e.tensor_scalar(out=kka[:, 0:n], in0=K[:, 0:n], scalar1=pcol[:, 5:6], scalar2=None, op0=ALU.mult), r=[K.k(), pcol.k()], w=[kka.k()])
        for d in range(2):
            P.op("dve", lambda e: e.scalar_tensor_tensor(out=BB[d][:, t0:t0 + n], in0=A[:, t0:t0 + n], scalar=-1.0, in1=asg[d][:, 0:n], op0=ALU.mult, op1=ALU.mult),
                 r=[A.k(ci), asg[d].k()], w=[BB[d].k(ci)])
            P.op("dve", lambda e: e.scalar_tensor_tensor(out=t1[:, 0:n], in0=asg[d][:, 0:n], scalar=-1.0, in1=kka[:, 0:n], op0=ALU.add, op1=ALU.mult),
                 r=[asg[d].k(), kka.k()], w=[t1.k()])
            P.op("dve", lambda e: e.tensor_tensor(out=KD[d][:, t0:t0 + n], in0=t1[:, 0:n], in1=K[:, 0:n], op=ALU.add), r=[t1.k(), K.k()], w=[KD[d].k(ci)])
        P.op("dve", lambda e: e.scalar_tensor_tensor(out=rkr[:, 0:n], in0=R[:, t0:t0 + n], scalar=pcol[:, 6:7], in1=K[:, 0:n], op0=ALU.mult, op1=ALU.mult),
             r=[R.k(ci), pcol.k(), K.k()], w=[rkr.k()])
        pbn = ps[7]
        nq = n // 64
        c0 = t0 // 64
        for hh in range(2):
            rows = slice(64 * hh, 64 * hh + 64)
            for q in range(nq):
                P.op("pe", lambda e: e.matmul(pbn[rows, q:q + 1], lhsT=rkr[rows, q * 64:(q + 1) * 64], rhs=onesc[rows, 0:1], start=True, stop=True),
                     r=[rkr.k(), onesc.k()], w=[pbn.k()])
                P.op("pe", lambda e: e.matmul(pbn[rows, 64 + q * 64:64 + (q + 1) * 64], lhsT=Vf[rows, q * 64:(q + 1) * 64], rhs=ident[rows, 64 * hh:64 * hh + 64],
                                               start=True, stop=True), r=[Vf.k(), ident.k()], w=[pbn.k()])
        P.op("dve", lambda e: e.tensor_copy(out=bon[:, c0:c0 + nq], in_=pbn[:, 0:nq]), r=[pbn.k()], w=[bon.k(ci)])
        P.op("act", lambda e: e.activation(out=vtm[:, c0:c0 + nq, :], in_=pbn[:, 64:64 + nq * 64].rearrange("p (q v) -> p q v", v=64), func=AF.Copy), r=[pbn.k()], w=[vtm.k(ci)])
    P.pop()

    P.push()
    yacc = P.sbuf(tag + "yacc", [128, NC64, 64], F32)
    P.push()
    masks = [P.sbuf(tag + "mask%d" % d, [128, 320], F32) for d in range(2)]
    for d, nm in enumerate(("mask_f", "mask_b")):
        P.dma("sp", masks[d][:], io[nm][:, :], w=[masks[d].k()])
    sts = []
    for d in range(2):
        s = {}
        for nm, w_ in (("G", 64), ("GE", 64), ("pre", 64), ("E1", 64), ("E2", 64), ("E3", 64), ("E4", 64), ("AR", 128), ("BK", 128), ("BKh", 128),
                       ("AM", 320), ("Z", 64), ("PP0", 128), ("PP1", 128), ("Wsb", 64), ("Usb", 64), ("BKT", 128), ("ST", 64), ("col", 4)):
            s[nm] = P.sbuf(tag + "s%d_%s" % (d, nm), [128, w_], F32)
        s["ps"] = [P.psum(tag + "sp%d_%d" % (d, i), [128, 512], F32) for i in range(4)]
        P.op("pool", lambda e: e.memset(s["ST"][:, :], 0.0), w=[s["ST"].k()])
        sts.append(s)

    def step(d, c, first):
        s = sts[d]
        pA, pL, pW, pT = s["ps"]
        tsl = slice(c * 64, (c + 1) * 64)
        G, GE, pre, E1, E2, E3, E4, AR, BK, BKh, AM, Z, Wsb, Usb, BKT, ST, col = (s[x] for x in
            ("G", "GE", "pre", "E1", "E2", "E3", "E4", "AR", "BK", "BKh", "AM", "Z", "Wsb", "Usb", "BKT", "ST", "col"))
        lw = LW[d][:, tsl]
        if d == 0:
            P.op("dve", lambda e: e.tensor_tensor_scan(out=G[:, :], data0=onesc[:, :], data1=lw, initial=0.0, op0=ALU.mult, op1=ALU.add),
                 r=[onesc.k(), LW[d].k()], w=[G.k()])
            gcol = G[:, 63:64]
        else:
            P.op("dve", lambda e: e.tensor_tensor_scan(out=pre[:, :], data0=onesc[:, :], data1=lw, initial=0.0, op0=ALU.mult, op1=ALU.add),
                 r=[onesc.k(), LW[d].k()], w=[pre.k()])
            P.op("dve", lambda e: e.tensor_tensor(out=G[:, :], in0=lw, in1=pre[:, :], op=ALU.subtract), r=[LW[d].k(), pre.k()], w=[G.k()])
            P.op("dve", lambda e: e.tensor_scalar(out=G[:, :], in0=G[:, :], scalar1=pre[:, 63:64], scalar2=None, op0=ALU.add), r=[G.k(), pre.k()], w=[G.k()])
            gcol = G[:, 0:1]
        P.op("dve", lambda e: e.tensor_tensor(out=GE[:, :], in0=G[:, :], in1=lw, op=ALU.subtract), r=[G.k(), LW[d].k()], w=[GE.k()])
        P.op("act", lambda e: e.activation(out=E1[:, :], in_=G[:, :], func=AF.Exp), r=[G.k()], w=[E1.k()])
        P.op("act", lambda e: e.activation(out=E2[:, :], in_=G[:, :], func=AF.Exp, scale=-1.0), r=[G.k()], w=[E2.k()])
        P.op("act", lambda e: e.activation(out=E3[:, :], in_=GE[:, :], func=AF.Exp), r=[GE.k()], w=[E3.k()])
        P.op("act", lambda e: e.activation(out=E4[:, :], in_=G[:, :], func=AF.Exp, scale=-1.0, bias=gcol), r=[G.k()], w=[E4.k()])
        P.op("act", lambda e: e.activation(out=col[:, 0:1], in_=gcol, func=AF.Exp), r=[G.k()], w=[col.k()])
        P.op("dve", lambda e: e.tensor_tensor(out=AR[:, 0:64], in0=A[:, tsl], in1=E3[:, :], op=ALU.mult), r=[A.k(), E3.k()], w=[AR.k(0)])
        P.op("dve", lambda e: e.tensor_tensor(out=AR[:, 64:128], in0=R[:, tsl], in1=E1[:, :], op=ALU.mult), r=[R.k(), E1.k()], w=[AR.k(1)])
        P.op("dve", lambda e: e.tensor_tensor(out=BK[:, 0:64], in0=BB[d][:, tsl], in1=E2[:, :], op=ALU.mult), r=[BB[d].k(), E2.k()], w=[BK.k(0)])
        P.op("dve", lambda e: e.tensor_tensor(out=BK[:, 64:128], in0=KD[d][:, tsl], in1=E2[:, :], op=ALU.mult), r=[KD[d].k(), E2.k()], w=[BK.k(1)])
        P.op("dve", lambda e: e.tensor_tensor(out=BKh[:, 0:64], in0=BB[d][:, tsl], in1=E4[:, :], op=ALU.mult), r=[BB[d].k(), E4.k()], w=[BKh.k(0)])
        P.op("dve", lambda e: e.tensor_tensor(out=BKh[:, 64:128], in0=KD[d][:, tsl], in1=E4[:, :], op=ALU.mult), r=[KD[d].k(), E4.k()], w=[BKh.k(1)])
        H = [slice(0, 64), slice(64, 128)]
        for hh in range(2):
            rw = H[hh]
            P.op("pe", lambda e: e.matmul(pA[rw, 0:128], lhsT=BK[rw, 0:64], rhs=AR[rw, 0:128], start=True, stop=True), r=[BK.k(), AR.k()], w=[pA.k()])
            P.op("pe", lambda e: e.matmul(pA[rw, 128:256], lhsT=BK[rw, 64:128], rhs=AR[rw, 0:128], start=True, stop=True), r=[BK.k(), AR.k()], w=[pA.k()])
            P.op("pe", lambda e: e.matmul(pA[rw, 256:320], lhsT=AR[rw, 0:64], rhs=BK[rw, 0:64], start=True, stop=True), r=[BK.k(), AR.k()], w=[pA.k()])
            P.op("pe", lambda e: e.matmul(pT[rw, 0:64], lhsT=BKh[rw, 0:64], rhs=ident[rw, 64 * hh:64 * hh + 64], start=True, stop=True), r=[BKh.k(), ident.k()], w=[pT.k()])
            P.op("pe", lambda e: e.matmul(pT[rw, 64:128], lhsT=BKh[rw, 64:128], rhs=ident[rw, 64 * hh:64 * hh + 64], start=True, stop=True), r=[BKh.k(), ident.k()], w=[pT.k()])
        P.op("dve", lambda e: e.tensor_tensor(out=AM[:, :], in0=pA[:, 0:320], in1=masks[d][:, :], op=ALU.mult), r=[pA.k(), masks[d].k()], w=[AM.k()])
        P.op("act", lambda e: e.activation(out=BKT[:, :], in_=pT[:, 0:128], func=AF.Copy), r=[pT.k()], w=[BKT.k()])
        P.op("dve", lambda e: e.tensor_tensor(out=Z[:, :], in0=AM[:, 0:64], in1=id64[:, :], op=ALU.add), r=[AM.k(), id64.k()], w=[Z.k()])
        Pc, PTc = AM[:, 0:64], AM[:, 256:320]
        Pk = AM.k()
        for lvl in range(1, 6):
            PPn = s["PP%d" % (lvl % 2)]
            for hh in range(2):
                rw = H[hh]
                P.op("pe", lambda e: e.matmul(pL[rw, 0:64], lhsT=Pc[rw, :], rhs=PTc[rw, :], start=True, stop=True), r=[Pk], w=[pL.k()])
                if lvl < 5:
                    P.op("pe", lambda e: e.matmul(pL[rw, 64:128], lhsT=PTc[rw, :], rhs=Pc[rw, :], start=True, stop=True), r=[Pk], w=[pL.k()])
            wcols = 128 if lvl < 5 else 64
            P.op("act", lambda e: e.activation(out=PPn[:, 0:wcols], in_=pL[:, 0:wcols], func=AF.Copy), r=[pL.k()], w=[PPn.k()])
            PTc, Pc, Pk = PPn[:, 0:64], PPn[:, 64:128], PPn.k()
            for hh in range(2):
                rw = H[hh]
                P.op("pe", lambda e: e.matmul(pL[rw, 128:192], lhsT=PTc[rw, :], rhs=Z[rw, :], start=True, stop=True), r=[Pk, Z.k()], w=[pL.k()])
            P.op("dve", lambda e: e.tensor_tensor(out=Z[:, :], in0=Z[:, :], in1=pL[:, 128:192], op=ALU.add), r=[Z.k(), pL.k()], w=[Z.k()])
        for hh in range(2):
            rw = H[hh]
            P.op("pe", lambda e: e.matmul(pW[rw, 0:64], lhsT=AR[rw, 0:64], rhs=ST[rw, :], start=True, stop=False), r=[AR.k(), ST.k()], w=[pW.k()])
            P.op("pe", lambda e: e.matmul(pW[rw, 0:64], lhsT=AM[rw, 128:192], rhs=vtm[rw, c, :], start=False, stop=True), r=[AM.k(), vtm.k()], w=[pW.k()])
        P.op("act", lambda e: e.activation(out=Wsb[:, :], in_=pW[:, 0:64], func=AF.Copy), r=[pW.k()], w=[Wsb.k()])
        for hh in range(2):
            rw = H[hh]
            P.op("pe", lambda e: e.matmul(pW[rw, 64:128], lhsT=Z[rw, :], rhs=Wsb[rw, :], start=True, stop=True), r=[Z.k(), Wsb.k()], w=[pW.k()])
        P.op("act", lambda e: e.activation(out=Usb[:, :], in_=pW[:, 64:128], func=AF.Copy), r=[pW.k()], w=[Usb.k()])
        for hh in range(2):
            rw = H[hh]
            P.op("pe", lambda e: e.matmul(pW[rw, 128:192], lhsT=AR[rw, 64:128], rhs=ST[rw, :], start=True, stop=False), r=[AR.k(), ST.k()], w=[pW.k()])
            P.op("pe", lambda e: e.matmul(pW[rw, 128:192], lhsT=AM[rw, 64:128], rhs=Usb[rw, :], start=False, stop=False), r=[AM.k(), Usb.k()], w=[pW.k()])
            P.op("pe", lambda e: e.matmul(pW[rw, 128:192], lhsT=AM[rw, 192:256], rhs=vtm[rw, c, :], start=False, stop=True), r=[AM.k(), vtm.k()], w=[pW.k()])
        for hh in range(2):
            rw = H[hh]
            P.op("pe", lambda e: e.matmul(pW[rw, 192:256], lhsT=BKT[rw, 0:64], rhs=Usb[rw, :], start=True, stop=False), r=[BKT.k(), Usb.k()], w=[pW.k()])
            P.op("pe", lambda e: e.matmul(pW[rw, 192:256], lhsT=BKT[rw, 64:128], rhs=vtm[rw, c, :], start=False, stop=True), r=[BKT.k(), vtm.k()], w=[pW.k()])
        if first:
            P.op("dve", lambda e: e.tensor_copy(out=yacc[:, c, :], in_=pW[:, 128:192]), r=[pW.k()], w=[yacc.k(c)])
        else:
            P.op("dve", lambda e: e.tensor_tensor(out=yacc[:, c, :], in0=yacc[:, c, :], in1=pW[:, 128:192], op=ALU.add), r=[pW.k(), yacc.k(c)], w=[yacc.k(c)])
        P.op("dve", lambda e: e.scalar_tensor_tensor(out=ST[:, :], in0=ST[:, :], scalar=col[:, 0:1], in1=pW[:, 192:256], op0=ALU.mult, op1=ALU.add),
             r=[ST.k(), col.k(), pW.k()], w=[ST.k()])

    order_f = list(range(NC64))
    order_b = list(range(nctx_c - 1, -1, -1)) + list(range(NC64 - 1, nctx_c - 1, -1))
    done = set()
    for i in range(NC64):
        for d, order in ((0, order_f), (1, order_b)):
            c = order[i]
            step(d, c, c not in done)
            done.add(c)
    P.pop()

    P.push()
    w_g = P.sbuf(tag + "wg", [128, KC, 128], BF16)
    stage = [P.sbuf(tag + "owst%d" % i, [128, KC, 64], F32) for i in range(2)]
    hTc = [P.sbuf(tag + "ohTc0", [128, KC, 256], BF16)] * 2
    lnw = P.sbuf(tag + "lnw", [128, 64], F32)
    lnb = P.sbuf(tag + "lnb", [128, 64], F32)
    gt = P.sbuf(tag + "gt", [128, 4, 64], F32)
    sc = P.sbuf(tag + "sc", [128, 8], F32)
    cen = P.sbuf(tag + "cen", [128, 64], F32)
    sq2 = P.sbuf(tag + "sq2", [128, 64], F32)
    yn = P.sbuf(tag + "yn", [128, 64], F32)
    yo = [P.sbuf(tag + "yo%d" % i, [128, 64], BF16) for i in range(2)]
    pg = [P.psum(tag + "pg%d" % i, [128, 512], F32) for i in range(2)]
    P.dma("sp", lnw[:], io["lnw"][:, :], w=[lnw.k()])
    P.dma("sp", lnb[:], io["lnb"][:, :], w=[lnb.k()])
    load_w_bf16(P, io["w_g"], w_g, 128, stage, "g", blk=64)
    for ci, (t0, n, sa, sb) in enumerate(seg_chunks(T, n_ctx)):
        hc = hTc[ci % 2]
        P.dma("sp", hc[:, :, 0:n], hT[:, :, t0:t0 + n].rearrange("k p t -> p k t"), w=[hc.k()])
        pgc = pg[ci % 2]
        nq = n // 64
        for hh in range(2):
            rows = slice(64 * hh, 64 * hh + 64)
            for q in range(nq):
                for kc in range(KC):
                    P.op("pe", lambda e: e.matmul(pgc[rows, q * 64:(q + 1) * 64], lhsT=hc[:, kc, q * 64:(q + 1) * 64], rhs=w_g[:, kc, hh * 64:(hh + 1) * 64],
                                                   start=(kc == 0), stop=(kc == KC - 1)), r=[hc.k(), w_g.k()], w=[pgc.k()])
        P.op("act", lambda e: e.activation(out=gt[:, 0:nq, :], in_=pgc[:, 0:nq * 64].rearrange("p (q v) -> p q v", v=64), func=AF.Silu), r=[pgc.k()], w=[gt.k()])
        for q in range(nq):
            c = t0 // 64 + q
            y = yacc[:, c, :]
            P.op("dve", lambda e: e.reduce_sum(out=sc[:, 0:1], in_=y, axis=AX.X), r=[yacc.k(c)], w=[sc.k(0)])
            P.op("dve", lambda e: e.tensor_scalar(out=sc[:, 1:2], in0=sc[:, 0:1], scalar1=-1.0 / 64, scalar2=None, op0=ALU.mult), r=[sc.k(0)], w=[sc.k(1)])
            P.op("dve", lambda e: e.tensor_scalar(out=cen[:, :], in0=y, scalar1=sc[:, 1:2], scalar2=None, op0=ALU.add), r=[yacc.k(c), sc.k(1)], w=[cen.k()])
            P.op("dve", lambda e: e.tensor_tensor(out=sq2[:, :], in0=cen[:, :], in1=cen[:, :], op=ALU.mult), r=[cen.k()], w=[sq2.k()])
            P.op("dve", lambda e: e.reduce_sum(out=sc[:, 2:3], in_=sq2[:, :], axis=AX.X), r=[sq2.k()], w=[sc.k(2)])
            P.op("act", lambda e: e.activation(out=sc[:, 3:4], in_=sc[:, 2:3], func=AF.Sqrt, scale=1.0 / 64, bias=RW_GN_EPS), r=[sc.k(2)], w=[sc.k(3)])
            P.op("dve", lambda e: e.reciprocal(out=sc[:, 4:5], in_=sc[:, 3:4]), r=[sc.k(3)], w=[sc.k(4)])
            P.op("dve", lambda e: e.scalar_tensor_tensor(out=yn[:, :], in0=cen[:, :], scalar=sc[:, 4:5], in1=lnw[:, :], op0=ALU.mult, op1=ALU.mult),
                 r=[cen.k(), sc.k(4), lnw.k()], w=[yn.k()])
            P.op("dve", lambda e: e.tensor_tensor(out=yn[:, :], in0=yn[:, :], in1=lnb[:, :], op=ALU.add), r=[yn.k(), lnb.k()], w=[yn.k()])
            P.op("dve", lambda e: e.scalar_tensor_tensor(out=yn[:, :], in0=vtm[:, c, :], scalar=bon[:, c:c + 1], in1=yn[:, :], op0=ALU.mult, op1=ALU.add),
                 r=[vtm.k(), bon.k(), yn.k()], w=[yn.k()])
            y2 = yo[q % 2]
            P.op("dve", lambda e: e.tensor_tensor(out=y2[:, :], in0=yn[:, :], in1=gt[:, q, :], op=ALU.mult), r=[yn.k(), gt.k()], w=[y2.k()])
            for hh in range(2):
                P.dma("pool", io["ya"][c * 64:(c + 1) * 64, hh * 64:(hh + 1) * 64], y2[64 * hh:64 * hh + 64, :], r=[y2.k()], w=[io["ya"].k(c, hh)])
    P.pop()
    P.pop()


KC = 16
EPS = 1e-6


def ca_program(P, NTOK, chunks, io, do_merge, mode):
    modc = P.sbuf("c_modc", [128, KC, 8], F32)
    gsc = P.sbuf("c_gsc", [128, KC, 2], F32)
    ones = P.sbuf("c_ones", [128, 128], F32)
    P.dma("sp", modc[:], io["modc"][:, :, :], w=[modc.k()])
    P.dma("sp", ones[:], io["ones"][:, :], w=[ones.k()])
    for j in range(2):
        P.op("dve", lambda e: e.scalar_tensor_tensor(out=gsc[:, :, j:j + 1], in0=modc[:, :, 2 + j:3 + j], scalar=1.0, in1=modc[:, :, 6:7],
                                                      op0=ALU.add, op1=ALU.mult), r=[modc.k()], w=[gsc.k(j)])
    if do_merge:
        mergedT = P.sbuf("c_mergedT", [128, KC, NTOK], BF16)
        P.push()
        hT = P.sbuf("c_hT", [128, KC, NTOK], BF16)
        yT = P.sbuf("c_yT", [128, 24, NTOK], BF16)
        P.dma("sp", hT[:], io["hT"][:, :, :].rearrange("k p t -> p k t"), w=[hT.k()])
        for q in range(3):
            P.dma("sp", yT[:, q * 8:(q + 1) * 8, :], io["yT"][q * 8:(q + 1) * 8, :, :].rearrange("k p t -> p k t"), w=[yT.k(q)])
        stg = [P.sbuf("c_stg%d" % i, [128, KC, 128], F32) for i in range(2)]
        stb = [P.sbuf("c_stb0", [128, 24, 128], F32)] * 2
        wm = [P.sbuf("c_wm%d" % i, [128, KC, 384], BF16) for i in range(2)]
        wb = [P.sbuf("c_wb%d" % i, [128, 24, 128], BF16) for i in range(2)]
        sig = [P.sbuf("c_sig%d" % i, [128, 512], F32) for i in range(2)]
        macc = P.sbuf("c_macc", [128, 512], F32)
        prod = P.sbuf("c_prod", [128, 512], F32)
        pg = [P.psum("c_pg%d" % i, [128, 512], F32) for i in range(2)]
        pbr = [P.psum("c_pbr%d" % i, [128, 512], F32) for i in range(2)]
        si = 0
        it = 0
        for db in range(KC):
            wmb, wbb = wm[db % 2], wb[db % 2]
            for nb in range(3):
                st = stg[si % 2]
                si += 1
                c0 = nb * 2048 + db * 128
                P.dma("sp", st[:], io["w_merge"][:, :, c0:c0 + 128], w=[st.k()])
                if nb % 2 == 0:
                    P.op("dve", lambda e: e.tensor_copy(out=wmb[:, :, nb * 128:(nb + 1) * 128], in_=st[:]), r=[st.k()], w=[wmb.k(nb)])
                else:
                    P.op("act", lambda e: e.activation(out=wmb[:, :, nb * 128:(nb + 1) * 128], in_=st[:], func=AF.Copy), r=[st.k()], w=[wmb.k(nb)])
            sb_ = stb[db % 2]
            P.dma("sp", sb_[:], io["w_branch"][:, :, db * 128:(db + 1) * 128], w=[sb_.k()])
            P.op("act", lambda e: e.activation(out=wbb[:], in_=sb_[:], func=AF.Copy), r=[sb_.k()], w=[wbb.k()])
            for (t0, n, isc) in chunks:
                for nb in range(3):
                    g_, b_ = pg[it % 2], pbr[it % 2]
                    sg = sig[it % 2]
                    it += 1
                    for kc in range(KC):
                        P.op("pe", lambda e: e.matmul(g_[:, 0:n], lhsT=wmb[:, kc, nb * 128:(nb + 1) * 128], rhs=hT[:, kc, t0:t0 + n],
                                                       start=(kc == 0), stop=(kc == KC - 1)), r=[wmb.k(nb), hT.k()], w=[g_.k()])
                    for cc in range(8):
                        P.op("pe", lambda e: e.matmul(b_[:, 0:n], lhsT=wbb[:, nb * 8 + cc, :], rhs=yT[:, nb * 8 + cc, t0:t0 + n],
                                                       start=(cc == 0), stop=(cc == 7)), r=[wbb.k(), yT.k(nb)], w=[b_.k()])
                    P.op("act", lambda e: e.activation(out=sg[:, 0:n], in_=g_[:, 0:n], func=AF.Sigmoid), r=[g_.k()], w=[sg.k()])
                    if nb == 0:
                        P.op("dve", lambda e: e.tensor_tensor(out=macc[:, 0:n], in0=sg[:, 0:n], in1=b_[:, 0:n], op=ALU.mult), r=[sg.k(), b_.k()], w=[macc.k()])
                    else:
                        P.op("dve", lambda e: e.tensor_tensor(out=prod[:, 0:n], in0=sg[:, 0:n], in1=b_[:, 0:n], op=ALU.mult), r=[sg.k(), b_.k()], w=[prod.k()])
                        if nb == 1:
                            P.op("dve", lambda e: e.tensor_tensor(out=macc[:, 0:n], in0=macc[:, 0:n], in1=prod[:, 0:n], op=ALU.add), r=[macc.k(), prod.k()], w=[macc.k()])
                        else:
                            P.op("dve", lambda e: e.tensor_tensor(out=mergedT[:, db, t0:t0 + n], in0=macc[:, 0:n], in1=prod[:, 0:n], op=ALU.add),
                                 r=[macc.k(), prod.k()], w=[mergedT.k(db, t0)])
        P.pop()

    P.push()
    znew = P.sbuf("c_znew", [128, KC, NTOK], F32)
    sq = [P.sbuf("c_sq%d" % i, [128, 512], F32) for i in range(2)]
    pss = [P.psum("c_pss%d" % i, [128, 512], F32) for i in range(len(chunks))]
    if do_merge:
        ze = [P.sbuf("c_ze%d" % i, [128, NTOK], F32) for i in range(2)]
        sto = [P.sbuf("c_sto%d" % i, [128, KC, 128], F32) for i in range(2)]
        wo = [P.sbuf("c_wo%d" % i, [128, KC, 128], BF16) for i in range(2)]
        po = [P.psum("c_po%d" % i, [128, 512], F32) for i in range(2)]
    it = 0
    for eb in range(KC):
        if do_merge:
            st, wob, z_e = sto[eb % 2], wo[eb % 2], ze[eb % 2]
            P.dma("sp", st[:], io["w_out"][:, :, eb * 128:(eb + 1) * 128], w=[st.k()])
            if eb % 2 == 0:
                P.op("act", lambda e: e.activation(out=wob[:], in_=st[:], func=AF.Copy), r=[st.k()], w=[wob.k()])
            else:
                P.op("dve", lambda e: e.tensor_copy(out=wob[:], in_=st[:]), r=[st.k()], w=[wob.k()])
            P.dma("sp", z_e[:], io["zT"][eb, :, :], w=[z_e.k()])
        else:
            P.dma("sp", znew[:, eb, :], io["zT"][eb, :, :], w=[znew.k(eb)])
        for ci, (t0, n, isc) in enumerate(chunks):
            if do_merge:
                p_ = po[it % 2]
                for db in range(KC):
                    P.op("pe", lambda e: e.matmul(p_[:, 0:n], lhsT=wob[:, db, :], rhs=mergedT[:, db, t0:t0 + n], start=(db == 0), stop=(db == KC - 1)),
                         r=[wob.k(), mergedT.k()], w=[p_.k()])
                gcol = modc[:, eb, 0:1] if isc else modc[:, eb, 1:2]
                P.op("dve", lambda e: e.scalar_tensor_tensor(out=znew[:, eb, t0:t0 + n], in0=p_[:, 0:n], scalar=gcol, in1=z_e[:, t0:t0 + n],
                                                              op0=ALU.mult, op1=ALU.add), r=[p_.k(), modc.k(), z_e.k()], w=[znew.k(eb, t0)])
            s_ = sq[it % 2]
            it += 1
            P.op("act", lambda e: e.activation(out=s_[:, 0:n], in_=znew[:, eb, t0:t0 + n], func=AF.Square), r=[znew.k(eb)], w=[s_.k()])
            P.op("pe", lambda e: e.matmul(pss[ci][:, 0:n], lhsT=ones[:, :], rhs=s_[:, 0:n], start=(eb == 0), stop=(eb == KC - 1)),
                 r=[ones.k(), s_.k()], w=[pss[ci].k()])
        if do_merge and mode == "mod":
            P.dma("pool", io["zTn"][eb, :, :], znew[:, eb, :], r=[znew.k(eb)], w=[io["zTn"].k(eb)])
    rstd = P.sbuf("c_rstd", [128, NTOK], F32)
    for ci, (t0, n, isc) in enumerate(chunks):
        P.op("act", lambda e: e.activation(out=rstd[:, t0:t0 + n], in_=pss[ci][:, 0:n], func=AF.Sqrt, scale=1.0 / 2048, bias=EPS), r=[pss[ci].k()], w=[rstd.k(ci)])
        P.op("dve", lambda e: e.reciprocal(out=rstd[:, t0:t0 + n], in_=rstd[:, t0:t0 + n]), r=[rstd.k(ci)], w=[rstd.k(ci)])
    odt = BF16 if mode == "mod" else F32
    hn = [P.sbuf("c_hn%d" % i, [128, NTOK], F32) for i in range(2)]
    ho = [P.sbuf("c_ho%d" % i, [128, NTOK], odt) for i in range(2)]
    oname = "hTn" if mode == "mod" else "oT"
    for eb in range(KC):
        h1, h2 = hn[eb % 2], ho[eb % 2]
        for ci, (t0, n, isc) in enumerate(chunks):
            j = 0 if isc else 1
            P.op("dve", lambda e: e.scalar_tensor_tensor(out=h1[:, t0:t0 + n], in0=znew[:, eb, t0:t0 + n], scalar=gsc[:, eb, j:j + 1], in1=rstd[:, t0:t0 + n],
                                                          op0=ALU.mult, op1=ALU.mult), r=[znew.k(eb), gsc.k(), rstd.k(ci)], w=[h1.k(ci)])
            P.op("dve", lambda e: e.tensor_scalar(out=h2[:, t0:t0 + n], in0=h1[:, t0:t0 + n], scalar1=modc[:, eb, 4 + j:5 + j], scalar2=None, op0=ALU.add),
                 r=[h1.k(ci), modc.k()], w=[h2.k(ci)])
        P.dma("pool", io[oname][eb, :, :], h2[:, :], r=[h2.k()], w=[io[oname].k(eb)])
    P.pop()


KC = 16


def mod_program(P, io):
    cT = P.sbuf("mm_cT", [128, KC, 3], F32)
    sT = P.sbuf("mm_sT", [128, KC, 3], F32)
    wa = [P.sbuf("mm_wa%d" % l, [128, KC, 768], F32) for l in range(2)]
    ba = P.sbuf("mm_ba", [3, 2, 768], F32)
    mo = P.sbuf("mm_mo", [3, 2, 768], F32)
    pm = [P.psum("mm_p%d" % i, [128, 512], F32) for i in range(2)]
    P.dma("sp", cT[:], io["cT"][:, :, :], w=[cT.k()])
    for l in range(2):
        P.dma("sp", wa[l][:], io["wa"][l, :, :, :], w=[wa[l].k()])
        P.dma("sp", ba[:, l, :], io["ba"][l, :, :], w=[ba.k(l)])
    P.op("act", lambda e: e.activation(out=sT[:], in_=cT[:], func=AF.Silu), r=[cT.k()], w=[sT.k()])
    it = 0
    for l in range(2):
        for hf in range(2):
            p_ = pm[it % 2]
            it += 1
            for kc in range(KC):
                P.op("pe", lambda e: e.matmul(p_[0:3, 0:384], lhsT=sT[:, kc, 0:3], rhs=wa[l][:, kc, hf * 384:(hf + 1) * 384],
                                               start=(kc == 0), stop=(kc == KC - 1)), r=[sT.k(), wa[l].k()], w=[p_.k()])
            P.op("dve", lambda e: e.tensor_tensor(out=mo[:, l, hf * 384:(hf + 1) * 384], in0=p_[0:3, 0:384], in1=ba[:, l, hf * 384:(hf + 1) * 384], op=ALU.add),
                 r=[p_.k(), ba.k(l)], w=[mo.k(l, hf)])
        P.dma("pool", io["mod"][l, :, :], mo[:, l, :], r=[mo.k(l)], w=[io["mod"].k(l)])

import ml_dtypes
from concourse.bass_utils import run_bass_kernel_spmd

NCORE = 8
DM = 2048
N_CTX = 256
N_LAT = 4096
TT = N_CTX + N_LAT
NTOK = TT // 4
CA_CHUNKS = [(0, 256, True), (256, 512, False), (768, 320, False)]
NPBF = ml_dtypes.bfloat16


def _wl(wc):
    return np.ascontiguousarray(wc.reshape(-1, 128, wc.shape[1]).transpose(1, 0, 2))


def _fm(a):
    return np.ascontiguousarray(a.T.reshape(-1, 128, a.shape[0]))


def _partner(d):
    return d + 32 if (d % 64) < 32 else d - 32


def _rope_tables(T, n_ctx):
    n_lat = T - n_ctx
    rows = n_lat // 64
    row = np.repeat(np.arange(rows), 64).astype(np.float32)
    col = np.tile(np.arange(64), rows).astype(np.float32)
    inv_freq = (np.float32(10000.0) ** (-np.arange(0, 64, 2, dtype=np.float32) / np.float32(64))).astype(np.float32)
    ang_lat = np.stack([row[:, None] * inv_freq, col[:, None] * inv_freq], axis=1)
    ang = np.concatenate([np.zeros((n_ctx, 2, 32), np.float32), ang_lat], axis=0)
    cos, sin = np.cos(ang).astype(np.float32), np.sin(ang).astype(np.float32)
    cosT = np.zeros((128, T), np.float32)
    sinT = np.zeros((128, T), np.float32)
    for d in range(128):
        a, half, i = d // 64, (d % 64) // 32, d % 32
        cosT[d] = cos[:, a, i]
        sinT[d] = sin[:, a, i] * (-1.0 if half == 0 else 1.0)
    return cosT, sinT


def _consts():
    c = {}
    pm = np.zeros((128, 128), np.float32)
    for d in range(128):
        pm[_partner(d), d] = 1.0
    c["pm"] = pm
    c["ones"] = np.ones((128, 128), np.float32)
    u = np.arange(128)
    c["tri_f"] = (u[:, None] <= u[None, :]).astype(np.float32)
    c["tri_b"] = (u[:, None] >= u[None, :]).astype(np.float32)
    c["mneg_f"] = np.where(u[:, None] <= u[None, :], 0.0, -30000.0).astype(np.float32)
    c["mneg_b"] = np.where(u[:, None] >= u[None, :], 0.0, -30000.0).astype(np.float32)
    i = np.arange(64)
    sT_f = (i[:, None] < i[None, :]).astype(np.float32)
    iT_f = (i[:, None] <= i[None, :]).astype(np.float32)
    st_f = (i[None, :] < i[:, None]).astype(np.float32)
    sT_b = (i[:, None] > i[None, :]).astype(np.float32)
    iT_b = (i[:, None] >= i[None, :]).astype(np.float32)
    st_b = (i[None, :] > i[:, None]).astype(np.float32)
    c["mask_f"] = np.tile(np.concatenate([sT_f, iT_f, sT_f, iT_f, st_f], axis=1), (2, 1))
    c["mask_b"] = np.tile(np.concatenate([sT_b, iT_b, sT_b, iT_b, st_b], axis=1), (2, 1))
    bones = np.zeros((128, 128), np.float32)
    bones[:64, :64] = 1
    bones[64:, 64:] = 1
    c["bones"] = bones
    c["ident"] = np.eye(128, dtype=np.float32)
    c["id64"] = np.tile(np.eye(64, dtype=np.float32), (2, 1))
    c["onesc"] = np.ones((128, 64), np.float32)
    c["cosT"], c["sinT"] = _rope_tables(TT, N_CTX)
    return c


_PROGS = {}


def _declare(P, specs):
    io = {}
    for name, (shape, dt, kind) in specs.items():
        io[name] = P.dram(name, list(shape), dt, kind=kind)
    return io


B_IN = {
    "hT": ([16, 128, TT], BF16),
    "g_w_fm": ([128, 16, 384], F32), "g_w_tm": ([128, 16, 384], F32), "cosT": ([128, TT], F32), "sinT": ([128, TT], F32),
    "pm": ([128, 128], F32), "ones": ([128, 128], F32), "gvec": ([128, 4], F32),
    "m_w_fm": ([128, 16, 256], F32), "m_w_tm": ([128, 16, 900], F32), "gbias": ([128, 4], F32), "normg": ([128, 256], F32),
    "tri_f": ([128, 128], F32), "tri_b": ([128, 128], F32), "mneg_f": ([128, 128], F32), "mneg_b": ([128, 128], F32),
    "ident": ([128, 128], F32), "id64": ([128, 64], F32), "bones": ([128, 128], F32), "onesc": ([128, 64], F32),
    "mask_f": ([128, 320], F32), "mask_b": ([128, 320], F32),
}
for _p in range(2):
    B_IN.update({"r%d_w_fm" % _p: ([128, 16, 640], F32), "r%d_w_g" % _p: ([128, 16, 128], F32), "r%d_mu" % _p: ([128, 10], F32),
                 "r%d_wup" % _p: ([128, 128], F32), "r%d_aup" % _p: ([128, 128], F32), "r%d_pcol" % _p: ([128, 8], F32),
                 "r%d_lnw" % _p: ([128, 64], F32), "r%d_lnb" % _p: ([128, 64], F32)})
B_OUT = {"yb": ([TT, 256], BF16), "yc": ([TT, 256], BF16), "ya0": ([TT, 128], BF16), "ya1": ([TT, 128], BF16)}


def _prog_B():
    if "B" in _PROGS:
        return _PROGS["B"]
    nc = bass.Bass("TRN2", target_bir_lowering=False)
    P = Prog(nc)
    specs = {k: (v[0], v[1], "ExternalInput") for k, v in B_IN.items()}
    specs.update({k: (v[0], v[1], "ExternalOutput") for k, v in B_OUT.items()})
    io = _declare(P, specs)
    P.push()
    gqa_program(P, TT, N_CTX, {"hT": io["hT"], "w_fm": io["g_w_fm"], "w_tm": io["g_w_tm"], "cosT": io["cosT"], "sinT": io["sinT"],
                               "pm": io["pm"], "ones": io["ones"], "gvec": io["gvec"], "yb": io["yb"]})
    P.pop()
    P.push()
    mlstm_program(P, TT, N_CTX, {"hT": io["hT"], "w_fm": io["m_w_fm"], "w_tm": io["m_w_tm"], "gbias": io["gbias"], "normg": io["normg"],
                                 "tri_f": io["tri_f"], "tri_b": io["tri_b"], "mneg_f": io["mneg_f"], "mneg_b": io["mneg_b"], "yc": io["yc"]})
    P.pop()
    for p in range(2):
        P.push()
        d = {"hT": io["hT"], "ya": io["ya%d" % p]}
        for k in ("ident", "id64", "bones", "onesc", "mask_f", "mask_b"):
            d[k] = io[k]
        for k in ("w_fm", "w_g", "mu", "wup", "aup", "pcol", "lnw", "lnb"):
            d[k] = io["r%d_%s" % (p, k)]
        rwkv_program(P, TT, N_CTX, d, tag="r%d_" % p)
        P.pop()
    P.finish()
    P.close()
    _PROGS["B"] = nc
    return nc


def _prog_CA(do_merge, mode):
    key = ("CA", do_merge, mode)
    if key in _PROGS:
        return _PROGS[key]
    nc = bass.Bass("TRN2", target_bir_lowering=False)
    P = Prog(nc)
    specs = {"zT": ([16, 128, NTOK], F32, "ExternalInput"), "modc": ([128, 16, 8], F32, "ExternalInput"), "ones": ([128, 128], F32, "ExternalInput")}
    if do_merge:
        specs.update({"hT": ([16, 128, NTOK], BF16, "ExternalInput"), "yT": ([24, 128, NTOK], BF16, "ExternalInput"),
                      "w_merge": ([128, 16, 6144], F32, "ExternalInput"), "w_branch": ([128, 24, 2048], F32, "ExternalInput"),
                      "w_out": ([128, 16, 2048], F32, "ExternalInput")})
    if mode == "mod":
        specs["hTn"] = ([16, 128, NTOK], BF16, "ExternalOutput")
        if do_merge:
            specs["zTn"] = ([16, 128, NTOK], F32, "ExternalOutput")
    else:
        specs["oT"] = ([16, 128, NTOK], F32, "ExternalOutput")
    io = _declare(P, specs)
    ca_program(P, NTOK, CA_CHUNKS, io, do_merge, mode)
    P.finish()
    P.close()
    _PROGS[key] = nc
    return nc


def _prog_M():
    if "M" in _PROGS:
        return _PROGS["M"]
    nc = bass.Bass("TRN2", target_bir_lowering=False)
    P = Prog(nc)
    io = _declare(P, {"cT": ([128, 16, 3], F32, "ExternalInput"), "wa": ([2, 128, 16, 768], F32, "ExternalInput"),
                      "ba": ([2, 3, 768], F32, "ExternalInput"), "mod": ([2, 3, 768], F32, "ExternalOutput")})
    mod_program(P, io)
    P.finish()
    P.close()
    _PROGS["M"] = nc
    return nc


def _run(nc, in_maps):
    res = run_bass_kernel_spmd(nc, in_maps, core_ids=list(range(NCORE)))
    return res.results


def _b_inputs(l, b, j, hT_full, consts, w_in, I):
    m = {"hT": hT_full[b]}
    for k in ("cosT", "sinT", "pm", "ones", "tri_f", "tri_b", "mneg_f", "mneg_b", "ident", "id64", "bones", "onesc", "mask_f", "mask_b"):
        m[k] = consts[k]
    W = w_in[l]
    kv = j // 2
    m["g_w_fm"] = _wl(np.concatenate([W[:, 4352 + 256 * j:4352 + 256 * j + 256], W[:, 5376 + 128 * kv:5376 + 128 * kv + 128]], axis=1))
    m["g_w_tm"] = _wl(np.concatenate([W[:, 5632 + 128 * kv:5632 + 128 * kv + 128], W[:, 5888 + 256 * j:5888 + 256 * j + 256]], axis=1))
    pidx = np.array([_partner(d) for d in range(128)])
    gq, gk = I["at_q_g"][l], I["at_k_g"][l]
    m["gvec"] = np.ascontiguousarray(np.stack([gq, gq[pidx], gk, gk[pidx]], axis=1).astype(np.float32))
    m["m_w_fm"] = _wl(np.concatenate([W[:, 6912 + 128 * j:6912 + 128 * j + 128], W[:, 7424 + 128 * j:7424 + 128 * j + 128]], axis=1))
    gcols = [9984 + 4 * t + j for t in range(4)]
    m["m_w_tm"] = _wl(np.concatenate([W[:, 7424 + 128 * j:7424 + 128 * j + 128], W[:, 7936 + 256 * j:7936 + 256 * j + 256], W[:, gcols],
                                      W[:, 8960 + 256 * j:8960 + 256 * j + 256], W[:, 10000 + 256 * j:10000 + 256 * j + 256]], axis=1))
    m["gbias"] = np.ascontiguousarray(np.tile(I["ml_gate_b"][l][:, j][None, :], (128, 1)).astype(np.float32))
    m["normg"] = np.ascontiguousarray(np.tile(I["ml_norm_g"][l][256 * j:256 * j + 256][None, :], (128, 1)).astype(np.float32))
    for p in range(2):
        ch0 = 256 * j + 128 * p
        cols = np.concatenate([np.arange(ch0, ch0 + 128), 1024 + np.arange(ch0, ch0 + 128), 2048 + np.arange(ch0, ch0 + 128),
                               np.arange(3072, 3200), np.arange(3200, 3328)])
        m["r%d_w_fm" % p] = _wl(W[:, cols])
        m["r%d_w_g" % p] = _wl(W[:, 3328 + ch0:3328 + ch0 + 128])
        mu = I["shift_mu"][l][:, cols]
        m["r%d_mu" % p] = np.ascontiguousarray(np.concatenate([mu[0].reshape(5, 128).T, mu[1].reshape(5, 128).T], axis=1).astype(np.float32))
        m["r%d_wup" % p] = np.ascontiguousarray(I["rw_w_up"][l][:, :, ch0:ch0 + 128].reshape(128, 128))
        m["r%d_aup" % p] = np.ascontiguousarray(I["rw_a_up"][l][:, :, ch0:ch0 + 128].reshape(128, 128))
        sl = slice(ch0, ch0 + 128)
        m["r%d_pcol" % p] = np.ascontiguousarray(np.stack([I["rw_w0"][l][0][sl], I["rw_w0"][l][1][sl], I["rw_a0"][l][0][sl], I["rw_a0"][l][1][sl],
                                                            I["rw_k_k"][l][sl], I["rw_k_a"][l][sl], I["rw_r_k"][l].reshape(1024)[sl],
                                                            np.zeros(128, np.float32)], axis=1).astype(np.float32))
        m["r%d_lnw" % p] = np.ascontiguousarray(np.repeat(I["rw_ln_w"][l][sl].reshape(2, 1, 64), 64, axis=1).reshape(128, 64))
        m["r%d_lnb" % p] = np.ascontiguousarray(np.repeat(I["rw_ln_b"][l][sl].reshape(2, 1, 64), 64, axis=1).reshape(128, 64))
    return m


def _modc(gt, sc, sh, g, b, j):
    z = np.zeros((3, DM), np.float32)
    gt = z if gt is None else gt
    sc = z if sc is None else sc
    sh = z if sh is None else sh
    rc = 2 if j == 0 else b
    cols = [gt[rc], gt[b], sc[rc], sc[b], sh[rc], sh[b], g, np.zeros(DM, np.float32)]
    out = np.zeros((128, 16, 8), np.float32)
    for i, c in enumerate(cols):
        out[:, :, i] = np.asarray(c, np.float32).reshape(16, 128).T
    return out


def kernel(**I):
    I = {k: np.asarray(v) for k, v in I.items()}
    consts = _consts()
    cores = [(c // 4, c % 4) for c in range(NCORE)]
    cstack = np.stack([I["c"][0], I["c"][1], I["c_ctx"]], axis=0).astype(np.float32)
    cT = np.ascontiguousarray(cstack.T.reshape(16, 128, 3).transpose(1, 0, 2))
    in_maps = []
    for c in range(NCORE):
        cs = slice(768 * c, 768 * (c + 1))
        wa = np.stack([_wl(I["w_ada"][l][:, cs]) for l in range(2)], axis=0)
        ba = np.stack([np.tile(I["b_ada"][l][cs][None, :], (3, 1)) for l in range(2)], axis=0).astype(np.float32)
        in_maps.append({"cT": cT, "wa": np.ascontiguousarray(wa), "ba": np.ascontiguousarray(ba)})
    r = _run(_prog_M(), in_maps)
    mod = np.concatenate([np.asarray(r[c]["mod"]) for c in range(NCORE)], axis=2)
    sh = [mod[l][:, 0:DM] for l in range(2)]
    sc = [mod[l][:, DM:2 * DM] for l in range(2)]
    gt = [mod[l][:, 2 * DM:3 * DM] for l in range(2)]
    zfull = [np.concatenate([I["ctx"][b], I["x"][b]], axis=0) for b in range(2)]
    zT = [_fm(zfull[b][NTOK * j:NTOK * (j + 1)]) for (b, j) in cores]
    in_maps = [{"zT": zT[c], "modc": _modc(None, sc[0], sh[0], I["norm_g"][0], b, j), "ones": consts["ones"]} for c, (b, j) in enumerate(cores)]
    r = _run(_prog_CA(False, "mod"), in_maps)
    hT = [np.asarray(r[c]["hTn"]) for c in range(NCORE)]
    out = None
    for l in range(2):
        hT_full = [np.ascontiguousarray(np.concatenate(hT[4 * b:4 * b + 4], axis=2)) for b in range(2)]
        in_maps = [_b_inputs(l, b, j, hT_full, consts, I["w_in"], I) for (b, j) in cores]
        r = _run(_prog_B(), in_maps)
        yT = []
        for b in range(2):
            y = np.zeros((TT, 3, 1024), NPBF)
            for j in range(4):
                rr = r[4 * b + j]
                for p in range(2):
                    y[:, 0, 256 * j + 128 * p:256 * j + 128 * p + 128] = np.asarray(rr["ya%d" % p])
                y[:, 1, 256 * j:256 * j + 256] = np.asarray(rr["yb"])
                y[:, 2, 256 * j:256 * j + 256] = np.asarray(rr["yc"])
            yT.append(_fm(y.reshape(TT, 3072)))
        last = (l == 1)
        w_merge = _wl(I["w_in"][l][:, 11024:17168])
        w_branch = _wl(I["w_branch"][l].reshape(3072, DM))
        w_out = _wl(I["w_out"][l])
        in_maps = []
        for c, (b, j) in enumerate(cores):
            ts = slice(NTOK * j, NTOK * (j + 1))
            if last:
                mc = _modc(gt[l], None, None, I["final_g"], b, j)
            else:
                mc = _modc(gt[l], sc[l + 1], sh[l + 1], I["norm_g"][l + 1], b, j)
            in_maps.append({"zT": zT[c], "modc": mc, "ones": consts["ones"], "hT": hT[c], "yT": np.ascontiguousarray(yT[b][:, :, ts]),
                            "w_merge": w_merge, "w_branch": w_branch, "w_out": w_out})
        r = _run(_prog_CA(True, "final" if last else "mod"), in_maps)
        if last:
            out = np.zeros((2, N_LAT, DM), np.float32)
            for c, (b, j) in enumerate(cores):
                o = np.asarray(r[c]["oT"]).reshape(DM, NTOK).T
                t0 = NTOK * j
                lo = max(t0, N_CTX)
                out[b, lo - N_CTX:t0 + NTOK - N_CTX] = o[lo - t0:]
        else:
            zT = [np.asarray(r[c]["zTn"]) for c in range(NCORE)]
            hT = [np.asarray(r[c]["hTn"]) for c in range(NCORE)]
    return out
```

```python
import numpy as np
from contextlib import ExitStack
import concourse.bass as bass
import concourse.mybir as mybir

F32 = mybir.dt.float32
BF16 = mybir.dt.bfloat16
AF = mybir.ActivationFunctionType
ALU = mybir.AluOpType
AX = mybir.AxisListType

NS_DMA = 8


class Tile:
    def __init__(self, t, name):
        self.t = t
        self.name = name

    def __getitem__(self, idx):
        return self.t[idx]

    def k(self, *sub):
        return (self.name,) + tuple(sub)


class Prog:
    def __init__(self, nc):
        self.nc = nc
        self.es = ExitStack()
        self.stacks = [self.es]
        self.eng = {"pe": nc.tensor, "act": nc.scalar, "dve": nc.vector, "pool": nc.gpsimd, "sp": nc.sync}
        self.sem = {}
        self.cnt = {}
        for e in ("pe", "act", "dve", "pool"):
            self.sem[e] = self.es.enter_context(nc.semaphore("c_" + e))
            self.cnt[e] = 0
        self.unit = {e: 1 for e in self.sem}
        self.dq_n = {}
        for q in ("sp", "pool", "act"):
            self.dq_n[q] = 0
            for s in range(NS_DMA):
                ch = ("dma", q, s)
                self.sem[ch] = self.es.enter_context(nc.semaphore("d_%s%d" % (q, s)))
                self.cnt[ch] = 0
                self.unit[ch] = 16
        self.seen = {e: {} for e in self.eng}
        self.lastw = {}
        self.readers = {}
        self.nuniq = 0
        self.n_inst = 0

    def sbuf(self, name, shape, dtype=F32):
        name = "s_" + name
        t = self.stacks[-1].enter_context(self.nc.sbuf_tensor(name, list(shape), dtype))
        return Tile(t, name)

    def psum(self, name, shape, dtype=F32):
        name = "p_" + name
        t = self.stacks[-1].enter_context(self.nc.psum_tensor(name, list(shape), dtype))
        return Tile(t, name)

    def dram(self, name, shape, dtype=F32, kind=None):
        if kind is None:
            t = self.nc.dram_tensor(name, list(shape), dtype)
        else:
            t = self.nc.dram_tensor(name, list(shape), dtype, kind=kind)
        return Tile(t.ap(), "D:" + name)

    @staticmethod
    def _overlap(a, b):
        n = min(len(a), len(b))
        return a[:n] == b[:n]

    def _deps(self, rkeys, wkeys):
        deps = set()
        for k in list(rkeys) + list(wkeys):
            d = self.lastw.get(k[0])
            if d:
                for sk, v in d.items():
                    if self._overlap(sk, k):
                        deps.add(v)
        for k in wkeys:
            d = self.readers.get(k[0])
            if d:
                for sk, lst in d.items():
                    if self._overlap(sk, k):
                        deps.update(lst)
        return deps

    def _record(self, rkeys, wkeys, me):
        for k in wkeys:
            d = self.lastw.setdefault(k[0], {})
            for sk in [sk for sk in d if len(sk) >= len(k) and sk[:len(k)] == k]:
                del d[sk]
            d[k] = me
            r = self.readers.get(k[0])
            if r:
                for sk in [sk for sk in r if self._overlap(sk, k)]:
                    if len(sk) >= len(k):
                        del r[sk]
        for k in rkeys:
            r = self.readers.setdefault(k[0], {})
            lst = r.setdefault(k, [])
            lst[:] = [x for x in lst if x[0] != me[0]]
            lst.append(me)

    def _wait(self, e, deps, skip_same=None):
        eng = self.eng[e]
        seen = self.seen[e]
        best = {}
        for ch, n in deps:
            if ch == skip_same:
                continue
            if seen.get(ch, 0) >= n:
                continue
            if best.get(ch, 0) < n:
                best[ch] = n
        for ch, n in best.items():
            eng.wait_ge(self.sem[ch], n * self.unit[ch])
            seen[ch] = n
            self.n_inst += 1

    def op(self, e, fn, r=(), w=()):
        if e != "pe":
            w = list(w) + [(k[0],) for k in r if k[0].startswith("p_")]
        deps = self._deps(r, w)
        self._wait(e, deps, skip_same=("pe" if e == "pe" else None))
        ins = fn(self.eng[e])
        self.cnt[e] += 1
        ins.then_inc(self.sem[e], 1)
        self.n_inst += 1
        me = (e, self.cnt[e])
        self._record(r, w, me)
        return ins

    def dma(self, q, out, in_, r=(), w=(), **kw):
        i = self.dq_n[q]
        self.dq_n[q] += 1
        ch = ("dma", q, i % NS_DMA)
        deps = self._deps(r, w)
        if self.cnt[ch] > 0:
            deps.add((ch, self.cnt[ch]))
        self._wait(q, deps)
        ins = self.eng[q].dma_start(out=out, in_=in_, **kw)
        self.cnt[ch] += 1
        ins.then_inc(self.sem[ch], 16)
        self.n_inst += 1
        self._record(r, w, (ch, self.cnt[ch]))
        return ins

    def barrier(self):
        for e in self.eng:
            deps = set()
            for ch, n in self.cnt.items():
                if n > 0:
                    deps.add((ch, n))
            self._wait(e, deps)
        self.lastw.clear()
        self.readers.clear()

    def push(self):
        self.stacks.append(ExitStack())

    def pop(self):
        self.barrier()
        self.stacks.pop().close()

    def finish(self):
        self.barrier()

    def close(self):
        self.es.close()


KC = 16
HD = 128
EPS = 1e-6


def chunks_of(T, n_ctx):
    out = []
    t = 0
    while t < n_ctx:
        n = min(512, n_ctx - t)
        out.append((t, n))
        t += n
    while t < T:
        n = min(512, T - t)
        out.append((t, n))
        t += n
    return out


def load_w_bf16(P, w_dram, dst_bf, ncols, stage, tag, blk=128):
    i = 0
    for c0 in range(0, ncols, blk):
        nb = min(blk, ncols - c0)
        st = stage[i % 2]
        P.dma("sp", st[:, :, 0:nb], w_dram[:, :, c0:c0 + nb], r=[], w=[st.k()])
        eng = "dve" if i % 2 == 0 else "act"
        if eng == "dve":
            P.op("dve", lambda e: e.tensor_copy(out=dst_bf[:, :, c0:c0 + nb], in_=st[:, :, 0:nb]), r=[st.k()], w=[dst_bf.k(c0)])
        else:
            P.op("act", lambda e: e.activation(out=dst_bf[:, :, c0:c0 + nb], in_=st[:, :, 0:nb], func=AF.Copy), r=[st.k()], w=[dst_bf.k(c0)])
        i += 1


def gqa_program(P, T, n_ctx, io):
    nc = P.nc
    NT = T // 128
    hT, w_fm_d, w_tm_d = io["hT"], io["w_fm"], io["w_tm"]
    qT = [P.sbuf("qT%d" % h, [128, T], BF16) for h in range(2)]
    kT = P.sbuf("kT", [128, T], BF16)
    Vaug = P.sbuf("Vaug", [128, NT, 132], BF16)
    gs = P.sbuf("gs", [128, NT, 256], F32)
    w_fm = P.sbuf("w_fm", [128, KC, 384], BF16)
    w_tm = P.sbuf("w_tm", [128, KC, 384], BF16)
    stage = [P.sbuf("wst%d" % i, [128, KC, 128], F32) for i in range(2)]
    hTc = [P.sbuf("hTc%d" % i, [128, KC, 512], BF16) for i in range(2)]
    cosc = [P.sbuf("cosc%d" % i, [128, 512], F32) for i in range(2)]
    sinc = [P.sbuf("sinc%d" % i, [128, 512], F32) for i in range(2)]
    pm = P.sbuf("pm", [128, 128], F32)
    ones = P.sbuf("ones", [128, 128], F32)
    gvec = P.sbuf("gvec", [128, 4], F32)
    raw = [P.sbuf("raw%d" % i, [128, 512], F32) for i in range(2)]
    sq = P.sbuf("sq", [128, 512], F32)
    rstd = P.sbuf("rstd", [128, 512], F32)
    t1 = P.sbuf("t1", [128, 512], F32)
    t2 = P.sbuf("t2", [128, 512], F32)
    ps_a = [P.psum("ps_a%d" % i, [128, 512], F32) for i in range(2)]
    ps_b = P.psum("ps_b", [128, 512], F32)
    ps_c = P.psum("ps_c", [128, 512], F32)

    P.dma("sp", pm[:], io["pm"][:, :], w=[pm.k()])
    P.dma("sp", ones[:], io["ones"][:, :], w=[ones.k()])
    P.dma("sp", gvec[:], io["gvec"][:, :], w=[gvec.k()])
    load_w_bf16(P, w_fm_d, w_fm, 384, stage, "fm")
    load_w_bf16(P, w_tm_d, w_tm, 384, stage, "tm")
    P.op("pool", lambda e: e.memset(Vaug[:, :, 128:132], 1.0), w=[Vaug.k("ones")])

    scale = HD ** -0.5
    for ci, (t0, n) in enumerate(chunks_of(T, n_ctx)):
        hc = hTc[ci % 2]
        P.dma("sp", hc[:, :, 0:n], hT[:, :, t0:t0 + n].rearrange("k p t -> p k t"), w=[hc.k()])
        cc, sc = cosc[ci % 2], sinc[ci % 2]
        P.dma("sp", cc[:, 0:n], io["cosT"][:, t0:t0 + n], w=[cc.k()])
        P.dma("sp", sc[:, 0:n], io["sinT"][:, t0:t0 + n], w=[sc.k()])
        for bi in range(3):
            ps = ps_a[bi % 2]
            for kc in range(KC):
                P.op("pe", lambda e: e.matmul(ps[:, 0:n], lhsT=w_fm[:, kc, bi * 128:(bi + 1) * 128], rhs=hc[:, kc, 0:n],
                                               start=(kc == 0), stop=(kc == KC - 1)),
                     r=[w_fm.k(bi * 128), hc.k()], w=[ps.k()])
            rw = raw[bi % 2]
            P.op("act", lambda e: e.activation(out=rw[:, 0:n], in_=ps[:, 0:n], func=AF.Copy), r=[ps.k()], w=[rw.k()])
            P.op("dve", lambda e: e.tensor_tensor(out=sq[:, 0:n], in0=rw[:, 0:n], in1=rw[:, 0:n], op=ALU.mult), r=[rw.k()], w=[sq.k()])
            P.op("pe", lambda e: e.matmul(ps_b[:, 0:n], lhsT=ones[:, :], rhs=sq[:, 0:n], start=True, stop=True),
                 r=[ones.k(), sq.k()], w=[ps_b.k()])
            P.op("pe", lambda e: e.matmul(ps_c[:, 0:n], lhsT=pm[:, :], rhs=rw[:, 0:n], start=True, stop=True),
                 r=[pm.k(), rw.k()], w=[ps_c.k()])
            P.op("act", lambda e: e.activation(out=rstd[:, 0:n], in_=ps_b[:, 0:n], func=AF.Sqrt, scale=1.0 / HD, bias=EPS),
                 r=[ps_b.k()], w=[rstd.k()])
            P.op("dve", lambda e: e.reciprocal(out=rstd[:, 0:n], in_=rstd[:, 0:n]), r=[rstd.k()], w=[rstd.k()])
            gi = 0 if bi < 2 else 2
            P.op("dve", lambda e: e.scalar_tensor_tensor(out=t1[:, 0:n], in0=rw[:, 0:n], scalar=gvec[:, gi:gi + 1], in1=cc[:, 0:n],
                                                          op0=ALU.mult, op1=ALU.mult), r=[rw.k(), gvec.k(), cc.k()], w=[t1.k()])
            P.op("dve", lambda e: e.scalar_tensor_tensor(out=t2[:, 0:n], in0=ps_c[:, 0:n], scalar=gvec[:, gi + 1:gi + 2], in1=sc[:, 0:n],
                                                          op0=ALU.mult, op1=ALU.mult), r=[ps_c.k(), gvec.k(), sc.k()], w=[t2.k()])
            P.op("dve", lambda e: e.tensor_tensor(out=t1[:, 0:n], in0=t1[:, 0:n], in1=t2[:, 0:n], op=ALU.add), r=[t1.k(), t2.k()], w=[t1.k()])
            dst = qT[bi] if bi < 2 else kT
            sc_f = scale if bi < 2 else 1.0
            P.op("dve", lambda e: e.scalar_tensor_tensor(out=dst[:, t0:t0 + n], in0=t1[:, 0:n], scalar=sc_f, in1=rstd[:, 0:n],
                                                          op0=ALU.mult, op1=ALU.mult), r=[t1.k(), rstd.k()], w=[dst.k(ci)])
        for ti in range(n // 128):
            tt = (t0 // 128) + ti
            ps = ps_a[ti % 2]
            for kc in range(KC):
                P.op("pe", lambda e: e.matmul(ps[:, 0:384], lhsT=hc[:, kc, ti * 128:(ti + 1) * 128], rhs=w_tm[:, kc, 0:384],
                                               start=(kc == 0), stop=(kc == KC - 1)),
                     r=[w_tm.k(), hc.k()], w=[ps.k()])
            P.op("dve", lambda e: e.tensor_copy(out=Vaug[:, tt, 0:128], in_=ps[:, 0:128]), r=[ps.k()], w=[Vaug.k("v", tt)])
            P.op("act", lambda e: e.activation(out=gs[:, tt, :], in_=ps[:, 128:384], func=AF.Silu), r=[ps.k()], w=[gs.k(tt)])

    pex = [P.sbuf("pex%d" % i, [128, 512], BF16) for i in range(3)]
    acc = [ps_b, ps_c] + [P.psum("acc%d" % i, [128, 512], F32) for i in range(2)]
    rec = P.sbuf("rec", [128, 4], F32)
    yt = [P.sbuf("yt%d" % i, [128, 128], F32) for i in range(2)]
    yo = [P.sbuf("yo%d" % i, [128, 128], BF16) for i in range(2)]
    nctx_t = n_ctx // 128
    it = 0
    for h in range(2):
        for ci, (t0, n) in enumerate(chunks_of(T, n_ctx)):
            nk = nctx_t if t0 < n_ctx else NT
            nsub = n // 128
            for kt in range(nk):
                ps = ps_a[it % 2]
                px = pex[it % 3]
                it += 1
                P.op("pe", lambda e: e.matmul(ps[:, 0:n], lhsT=kT[:, kt * 128:(kt + 1) * 128], rhs=qT[h][:, t0:t0 + n], start=True, stop=True),
                     r=[kT.k(), qT[h].k(ci)], w=[ps.k()])
                P.op("act", lambda e: e.activation(out=px[:, 0:n], in_=ps[:, 0:n], func=AF.Exp), r=[ps.k()], w=[px.k()])
                for s in range(nsub):
                    a = acc[s]
                    P.op("pe", lambda e: e.matmul(a[:, 0:129], lhsT=px[:, s * 128:(s + 1) * 128], rhs=Vaug[:, kt, 0:129],
                                                   start=(kt == 0), stop=(kt == nk - 1)),
                         r=[px.k(), Vaug.k()], w=[a.k()])
            for s in range(nsub):
                a = acc[s]
                tt = t0 // 128 + s
                y1 = yt[s % 2]
                y2 = yo[s % 2]
                P.op("dve", lambda e: e.reciprocal(out=rec[:, s:s + 1], in_=a[:, 128:129]), r=[a.k()], w=[rec.k(s)])
                P.op("dve", lambda e: e.scalar_tensor_tensor(out=y2[:, :], in0=a[:, 0:128], scalar=rec[:, s:s + 1],
                                                              in1=gs[:, tt, h * 128:(h + 1) * 128], op0=ALU.mult, op1=ALU.mult),
                     r=[a.k(), rec.k(s), gs.k(tt)], w=[y2.k()])
                P.dma("pool", io["yb"][tt * 128:(tt + 1) * 128, h * 128:(h + 1) * 128], y2[:, :], r=[y2.k()], w=[io["yb"].k(tt, h)])


GATE_CAP = 15.0
NEG = -30000.0


def mlstm_program(P, T, n_ctx, io):
    NCH = T // 128
    hT = io["hT"]
    qT = P.sbuf("m_qT", [128, T], F32)
    qTb = P.sbuf("m_qTb", [128, T], BF16)
    kTb = P.sbuf("m_kTb", [128, T], BF16)
    ktm = P.sbuf("m_ktm", [128, NCH, 128], F32)
    vaug = P.sbuf("m_vaug", [128, NCH, 260], BF16)
    og = P.sbuf("m_og", [128, NCH, 256], F32)
    gpre = P.sbuf("m_gpre", [128, NCH, 4], F32)
    lfn = P.sbuf("m_lfn", [128, NCH, 2], F32)
    gbias = P.sbuf("m_gbias", [128, 4], F32)
    normg = P.sbuf("m_normg", [128, 256], F32)
    tri = [P.sbuf("m_tri%d" % d, [128, 128], F32) for d in range(2)]
    mneg = [P.sbuf("m_mneg%d" % d, [128, 128], F32) for d in range(2)]
    onesf = P.sbuf("m_onesf", [128, 128], F32)
    pb = [P.psum("m_pb%d" % i, [128, 512], F32) for i in range(8)]
    P.push()
    w_fm = P.sbuf("m_wfm", [128, KC, 256], BF16)
    w_tm = P.sbuf("m_wtm", [128, KC, 900], BF16)
    stage = [P.sbuf("m_wst%d" % i, [128, KC, 128], F32) for i in range(2)]
    hTc = [P.sbuf("m_hTc%d" % i, [128, KC, 512], BF16) for i in range(2)]
    sig = P.sbuf("m_sig", [128, 256], F32)
    sil = P.sbuf("m_sil", [128, 256], F32)

    P.dma("sp", gbias[:], io["gbias"][:, :], w=[gbias.k()])
    P.dma("sp", normg[:], io["normg"][:, :], w=[normg.k()])
    for d, nm in enumerate(("tri_f", "tri_b")):
        P.dma("sp", tri[d][:], io[nm][:, :], w=[tri[d].k()])
    for d, nm in enumerate(("mneg_f", "mneg_b")):
        P.dma("sp", mneg[d][:], io[nm][:, :], w=[mneg[d].k()])
    P.op("pool", lambda e: e.memset(onesf[:, :], 1.0), w=[onesf.k()])
    P.op("pool", lambda e: e.memset(vaug[:, :, 256:260], 1.0), w=[vaug.k("ones")])
    load_w_bf16(P, io["w_fm"], w_fm, 256, stage, "fm")
    load_w_bf16(P, io["w_tm"], w_tm, 900, stage, "tm")

    qscale = 128 ** -0.5
    for ci, (t0, n) in enumerate(chunks_of(T, n_ctx)):
        hc = hTc[ci % 2]
        P.dma("sp", hc[:, :, 0:n], hT[:, :, t0:t0 + n].rearrange("k p t -> p k t"), w=[hc.k()])
        for bi in range(2):
            ps = pb[bi]
            for kc in range(KC):
                P.op("pe", lambda e: e.matmul(ps[:, 0:n], lhsT=w_fm[:, kc, bi * 128:(bi + 1) * 128], rhs=hc[:, kc, 0:n],
                                               start=(kc == 0), stop=(kc == KC - 1)), r=[w_fm.k(bi * 128), hc.k()], w=[ps.k()])
            if bi == 0:
                P.op("act", lambda e: e.activation(out=qT[:, t0:t0 + n], in_=ps[:, 0:n], func=AF.Copy, scale=qscale), r=[ps.k()], w=[qT.k(ci)])
                P.op("dve", lambda e: e.tensor_copy(out=qTb[:, t0:t0 + n], in_=qT[:, t0:t0 + n]), r=[qT.k(ci)], w=[qTb.k(ci)])
            else:
                P.op("act", lambda e: e.activation(out=kTb[:, t0:t0 + n], in_=ps[:, 0:n], func=AF.Copy), r=[ps.k()], w=[kTb.k(ci)])
        for ti in range(n // 128):
            c = t0 // 128 + ti
            p1, p2 = pb[2 + (ti % 2) * 2], pb[3 + (ti % 2) * 2]
            for kc in range(KC):
                P.op("pe", lambda e: e.matmul(p1[:, 0:388], lhsT=hc[:, kc, ti * 128:(ti + 1) * 128], rhs=w_tm[:, kc, 0:388],
                                               start=(kc == 0), stop=(kc == KC - 1)), r=[w_tm.k(), hc.k()], w=[p1.k()])
            for kc in range(KC):
                P.op("pe", lambda e: e.matmul(p2[:, 0:512], lhsT=hc[:, kc, ti * 128:(ti + 1) * 128], rhs=w_tm[:, kc, 388:900],
                                               start=(kc == 0), stop=(kc == KC - 1)), r=[w_tm.k(), hc.k()], w=[p2.k()])
            P.op("dve", lambda e: e.tensor_copy(out=ktm[:, c, :], in_=p1[:, 0:128]), r=[p1.k()], w=[ktm.k(c)])
            P.op("act", lambda e: e.activation(out=vaug[:, c, 0:256], in_=p1[:, 128:384], func=AF.Copy), r=[p1.k()], w=[vaug.k("v", c)])
            P.op("dve", lambda e: e.tensor_tensor(out=gpre[:, c, :], in0=p1[:, 384:388], in1=gbias[:, :], op=ALU.add), r=[p1.k(), gbias.k()], w=[gpre.k(c)])
            P.op("act", lambda e: e.activation(out=sig[:, :], in_=p2[:, 0:256], func=AF.Sigmoid), r=[p2.k()], w=[sig.k()])
            P.op("act", lambda e: e.activation(out=sil[:, :], in_=p2[:, 256:512], func=AF.Silu), r=[p2.k()], w=[sil.k()])
            P.op("dve", lambda e: e.tensor_tensor(out=og[:, c, :], in0=sig[:, :], in1=sil[:, :], op=ALU.mult), r=[sig.k(), sil.k()], w=[og.k(c)])

    P.pop()
    P.push()
    Hacc = P.sbuf("m_Hacc", [128, NCH, 256], F32)
    P.op("act", lambda e: e.activation(out=gpre[:, :, :], in_=gpre[:, :, :], func=AF.Tanh, scale=1.0 / GATE_CAP), r=[gpre.k()], w=[gpre.k()])
    P.op("dve", lambda e: e.tensor_scalar(out=gpre[:, :, :], in0=gpre[:, :, :], scalar1=GATE_CAP, scalar2=None, op0=ALU.mult), r=[gpre.k()], w=[gpre.k()])
    P.op("act", lambda e: e.activation(out=lfn[:, :, :], in_=gpre[:, :, 2:4], func=AF.Exp, scale=-1.0), r=[gpre.k()], w=[lfn.k()])
    P.op("act", lambda e: e.activation(out=lfn[:, :, :], in_=lfn[:, :, :], func=AF.Ln, bias=1.0), r=[lfn.k()], w=[lfn.k()])
    P.op("dve", lambda e: e.tensor_scalar(out=lfn[:, :, :], in0=lfn[:, :, :], scalar1=-1.0, scalar2=None, op0=ALU.mult), r=[lfn.k()], w=[lfn.k()])

    nctx_c = n_ctx // 128
    order_f = list(range(NCH))
    order_b = list(range(nctx_c - 1, -1, -1)) + list(range(NCH - 1, nctx_c - 1, -1))
    st = []
    for d in range(2):
        s = {}
        s["Cn"] = P.sbuf("m_Cn%d" % d, [128, 260], F32)
        s["Cnb"] = P.sbuf("m_Cnb%d" % d, [128, 260], BF16)
        for nm, shp, dt in (("LFbc", [128, 128], F32), ("tmp", [128, 128], F32), ("ET", [128, 128], F32), ("expB", [128, 128], F32),
                            ("qt", [128, 128], BF16), ("smT", [128, 128], BF16), ("kw", [128, 128], BF16), ("hd", [128, 256], F32)):
            s[nm] = P.sbuf("m_%s%d" % (nm, d), shp, dt)
        s["col"] = P.sbuf("m_col%d" % d, [128, 8], F32)
        P.op("pool", lambda e: e.memset(s["Cn"][:, :], 0.0), w=[s["Cn"].k()])
        P.op("pool", lambda e: e.memset(s["Cnb"][:, :], 0.0), w=[s["Cnb"].k()])
        st.append(s)

    def step(d, c, first):
        s = st[d]
        p1, p2, p3, p4 = pb[4 * d], pb[4 * d + 1], pb[4 * d + 2], pb[4 * d + 3]
        col = s["col"]
        gcol = 127 if d == 0 else 0
        tsl = slice(c * 128, (c + 1) * 128)
        P.op("dve", lambda e: e.tensor_scalar(out=s["LFbc"][:, :], in0=onesf[:, :], scalar1=lfn[:, c, d:d + 1], scalar2=None, op0=ALU.mult),
             r=[onesf.k(), lfn.k()], w=[s["LFbc"].k()])
        P.op("pe", lambda e: e.matmul(p1[:, 0:128], lhsT=s["LFbc"][:, :], rhs=tri[d][:, :], start=True, stop=True),
             r=[s["LFbc"].k(), tri[d].k()], w=[p1.k()])
        P.op("pe", lambda e: e.matmul(p1[:, 128:129], lhsT=tri[d][:, :], rhs=lfn[:, c, d:d + 1], start=True, stop=True),
             r=[tri[d].k(), lfn.k()], w=[p1.k()])
        P.op("dve", lambda e: e.tensor_tensor(out=col[:, 0:1], in0=gpre[:, c, d:d + 1], in1=p1[:, 128:129], op=ALU.subtract),
             r=[gpre.k(), p1.k()], w=[col.k(0)])
        P.op("dve", lambda e: e.tensor_tensor(out=s["tmp"][:, :], in0=p1[:, 0:128], in1=mneg[d][:, :], op=ALU.add),
             r=[p1.k(), mneg[d].k()], w=[s["tmp"].k()])
        P.op("act", lambda e: e.activation(out=s["ET"][:, :], in_=s["tmp"][:, :], func=AF.Exp, bias=col[:, 0:1]),
             r=[s["tmp"].k(), col.k(0)], w=[s["ET"].k()])
        P.op("act", lambda e: e.activation(out=s["expB"][:, :], in_=p1[:, 0:128], func=AF.Exp), r=[p1.k()], w=[s["expB"].k()])
        P.op("act", lambda e: e.activation(out=col[:, 1:2], in_=p1[:, gcol:gcol + 1], func=AF.Exp, bias=col[:, 0:1]),
             r=[p1.k(), col.k(0)], w=[col.k(1)])
        P.op("dve", lambda e: e.tensor_tensor(out=s["qt"][:, :], in0=qT[:, tsl], in1=s["expB"][:, :], op=ALU.mult),
             r=[qT.k(), s["expB"].k()], w=[s["qt"].k()])
        P.op("pe", lambda e: e.matmul(p2[:, 0:128], lhsT=kTb[:, tsl], rhs=qTb[:, tsl], start=True, stop=True),
             r=[kTb.k(), qTb.k()], w=[p2.k()])
        P.op("dve", lambda e: e.tensor_tensor(out=s["smT"][:, :], in0=p2[:, 0:128], in1=s["ET"][:, :], op=ALU.mult),
             r=[p2.k(), s["ET"].k()], w=[s["smT"].k()])
        P.op("pe", lambda e: e.matmul(p3[:, 0:257], lhsT=s["smT"][:, :], rhs=vaug[:, c, 0:257], start=True, stop=False),
             r=[s["smT"].k(), vaug.k()], w=[p3.k()])
        P.op("pe", lambda e: e.matmul(p3[:, 0:257], lhsT=s["qt"][:, :], rhs=s["Cnb"][:, 0:257], start=False, stop=True),
             r=[s["qt"].k(), s["Cnb"].k()], w=[p3.k()])
        P.op("act", lambda e: e.activation(out=col[:, 2:3], in_=p3[:, 256:257], func=AF.Abs), r=[p3.k()], w=[col.k(2)])
        P.op("dve", lambda e: e.tensor_scalar(out=col[:, 2:3], in0=col[:, 2:3], scalar1=1.0, scalar2=None, op0=ALU.max),
             r=[col.k(2)], w=[col.k(2)])
        P.op("dve", lambda e: e.reciprocal(out=col[:, 3:4], in_=col[:, 2:3]), r=[col.k(2)], w=[col.k(3)])
        if first:
            P.op("dve", lambda e: e.tensor_scalar(out=Hacc[:, c, :], in0=p3[:, 0:256], scalar1=col[:, 3:4], scalar2=None, op0=ALU.mult),
                 r=[p3.k(), col.k(3)], w=[Hacc.k(c)])
        else:
            P.op("dve", lambda e: e.scalar_tensor_tensor(out=Hacc[:, c, :], in0=p3[:, 0:256], scalar=col[:, 3:4], in1=Hacc[:, c, :],
                                                          op0=ALU.mult, op1=ALU.add), r=[p3.k(), col.k(3), Hacc.k(c)], w=[Hacc.k(c)])
        P.op("dve", lambda e: e.tensor_scalar(out=s["kw"][:, :], in0=ktm[:, c, :], scalar1=col[:, 1:2], scalar2=None, op0=ALU.mult),
             r=[ktm.k(c), col.k(1)], w=[s["kw"].k()])
        P.op("pe", lambda e: e.matmul(p4[:, 0:257], lhsT=s["kw"][:, :], rhs=vaug[:, c, 0:257], start=True, stop=True),
             r=[s["kw"].k(), vaug.k()], w=[p4.k()])
        P.op("dve", lambda e: e.scalar_tensor_tensor(out=s["Cn"][:, 0:257], in0=s["Cn"][:, 0:257], scalar=s["expB"][:, gcol:gcol + 1], in1=p4[:, 0:257],
                                                      op0=ALU.mult, op1=ALU.add), r=[s["Cn"].k(), s["expB"].k(), p4.k()], w=[s["Cn"].k()])
        P.op("act", lambda e: e.activation(out=s["Cnb"][:, 0:257], in_=s["Cn"][:, 0:257], func=AF.Copy), r=[s["Cn"].k()], w=[s["Cnb"].k()])

    done = set()
    for i in range(NCH):
        for d, order in ((0, order_f), (1, order_b)):
            c = order[i]
            step(d, c, c not in done)
            done.add(c)

    ssq = P.sbuf("m_ssq", [128, NCH], F32)
    junk = P.sbuf("m_junk", [128, 256], F32)
    yo = [P.sbuf("m_yo%d" % i, [128, 256], BF16) for i in range(2)]
    ytmp = [P.sbuf("m_ytmp%d" % i, [128, 256], F32) for i in range(2)]
    for c in range(NCH):
        P.op("dve", lambda e: e.tensor_tensor(out=junk[:, :], in0=Hacc[:, c, :], in1=Hacc[:, c, :], op=ALU.mult), r=[Hacc.k(c)], w=[junk.k()])
        P.op("dve", lambda e: e.reduce_sum(out=ssq[:, c:c + 1], in_=junk[:, :], axis=AX.X), r=[junk.k()], w=[ssq.k(c)])
    P.op("act", lambda e: e.activation(out=ssq[:, :], in_=ssq[:, :], func=AF.Sqrt, scale=1.0 / 256, bias=1e-6), r=[ssq.k()], w=[ssq.k()])
    P.op("dve", lambda e: e.reciprocal(out=ssq[:, :], in_=ssq[:, :]), r=[ssq.k()], w=[ssq.k()])
    for c in range(NCH):
        yt, y2 = ytmp[c % 2], yo[c % 2]
        P.op("dve", lambda e: e.scalar_tensor_tensor(out=yt[:, :], in0=Hacc[:, c, :], scalar=ssq[:, c:c + 1], in1=normg[:, :],
                                                      op0=ALU.mult, op1=ALU.mult), r=[Hacc.k(c), ssq.k(), normg.k()], w=[yt.k()])
        P.op("dve", lambda e: e.tensor_tensor(out=y2[:, :], in0=yt[:, :], in1=og[:, c, :], op=ALU.mult), r=[yt.k(), og.k(c)], w=[y2.k()])
        P.dma("pool", io["yc"][c * 128:(c + 1) * 128, :], y2[:, :], r=[y2.k()], w=[io["yc"].k(c)])
    P.pop()


RW_GN_EPS = 64e-5
C64 = 64


def seg_chunks(T, n_ctx, n=256):
    out = []
    for (a, b) in ((0, n_ctx), (n_ctx, T)):
        t = a
        while t < b:
            m = min(n, b - t)
            out.append((t, m, a, b))
            t += m
    return out


def rwkv_program(P, T, n_ctx, io, tag=""):
    NC64 = T // C64
    nctx_c = n_ctx // C64
    hT = io["hT"]
    R = P.sbuf(tag + "R", [128, T], F32)
    A = P.sbuf(tag + "A", [128, T], F32)
    KD = [P.sbuf(tag + "KD%d" % d, [128, T], F32) for d in range(2)]
    BB = [P.sbuf(tag + "BB%d" % d, [128, T], F32) for d in range(2)]
    LW = [P.sbuf(tag + "LW%d" % d, [128, T], F32) for d in range(2)]
    vtm = P.sbuf(tag + "vtm", [128, NC64, 64], F32)
    bon = P.sbuf(tag + "bon", [128, NC64], F32)
    ident = P.sbuf(tag + "ident", [128, 128], F32)
    id64 = P.sbuf(tag + "id64", [128, 64], F32)
    onesc = P.sbuf(tag + "onesc", [128, 64], F32)
    pcol = P.sbuf(tag + "pcol", [128, 8], F32)
    P.dma("sp", ident[:], io["ident"][:, :], w=[ident.k()])
    P.dma("sp", id64[:], io["id64"][:, :], w=[id64.k()])
    P.dma("sp", onesc[:], io["onesc"][:, :], w=[onesc.k()])
    P.dma("sp", pcol[:], io["pcol"][:, :], w=[pcol.k()])

    P.push()
    w_fm = P.sbuf(tag + "wfm", [128, KC, 640], BF16)
    stage = [P.sbuf(tag + "wst%d" % i, [128, KC, 64], F32) for i in range(2)]
    hTc = [P.sbuf(tag + "hTc0", [128, KC, 258], BF16)] * 2
    mu = P.sbuf(tag + "mu", [128, 16], F32)
    wup = P.sbuf(tag + "wup", [128, 128], F32)
    aup = P.sbuf(tag + "aup", [128, 128], F32)
    bones = P.sbuf(tag + "bones", [128, 128], F32)
    tmpn = {}
    for nm in ("K", "Vf", "WD", "AD", "tw", "as0", "as1", "kk0", "sq", "rn", "kka", "t1", "rkr"):
        tmpn[nm] = P.sbuf(tag + "t_" + nm, [128, 256], F32)
    ps = [P.psum(tag + "pp%d" % i, [128, 512], F32) for i in range(8)]
    P.dma("sp", mu[:, 0:10], io["mu"][:, :], w=[mu.k()])
    P.dma("sp", wup[:], io["wup"][:, :], w=[wup.k()])
    P.dma("sp", aup[:], io["aup"][:, :], w=[aup.k()])
    P.dma("sp", bones[:], io["bones"][:, :], w=[bones.k()])
    load_w_bf16(P, io["w_fm"], w_fm, 640, stage, "fm", blk=64)
    P.op("dve", lambda e: e.tensor_tensor(out=mu[:, 10:15], in0=mu[:, 0:5], in1=mu[:, 5:10], op=ALU.add), r=[mu.k()], w=[mu.k()])
    P.op("dve", lambda e: e.tensor_scalar(out=mu[:, 10:15], in0=mu[:, 10:15], scalar1=-1.0, scalar2=1.0, op0=ALU.mult, op1=ALU.add), r=[mu.k()], w=[mu.k()])

    for ci, (t0, n, sa, sb) in enumerate(seg_chunks(T, n_ctx)):
        lo, hi = max(sa, t0 - 1), min(sb, t0 + n + 1)
        nn = hi - lo
        o = t0 - lo
        hc = hTc[ci % 2]
        P.dma("sp", hc[:, :, 0:nn], hT[:, :, lo:hi].rearrange("k p t -> p k t"), w=[hc.k()])
        dsts = [R[:, t0:t0 + n], tmpn["K"][:, 0:n], tmpn["Vf"][:, 0:n], tmpn["WD"][:, 0:n], tmpn["AD"][:, 0:n]]
        dkeys = [R.k(ci), tmpn["K"].k(), tmpn["Vf"].k(), tmpn["WD"].k(), tmpn["AD"].k()]
        for bi in range(5):
            pp = ps[bi % 2]
            for kc in range(KC):
                P.op("pe", lambda e: e.matmul(pp[:, 0:nn], lhsT=w_fm[:, kc, bi * 128:(bi + 1) * 128], rhs=hc[:, kc, 0:nn],
                                               start=(kc == 0), stop=(kc == KC - 1)), r=[w_fm.k(), hc.k()], w=[pp.k()])
            dst, dk = dsts[bi], dkeys[bi]
            P.op("act", lambda e: e.activation(out=dst, in_=pp[:, o:o + n], func=AF.Copy, scale=mu[:, 10 + bi:11 + bi]), r=[pp.k(), mu.k()], w=[dk])
            j0 = 0 if o == 1 else 1
            P.op("dve", lambda e: e.scalar_tensor_tensor(out=dst[:, j0:n], in0=pp[:, o + j0 - 1:o + n - 1], scalar=mu[:, bi:bi + 1], in1=dst[:, j0:n],
                                                          op0=ALU.mult, op1=ALU.add), r=[pp.k(), mu.k(), dk], w=[dk])
            j1 = n if (o + n + 1 <= nn) else n - 1
            P.op("dve", lambda e: e.scalar_tensor_tensor(out=dst[:, 0:j1], in0=pp[:, o + 1:o + 1 + j1], scalar=mu[:, 5 + bi:6 + bi], in1=dst[:, 0:j1],
                                                          op0=ALU.mult, op1=ALU.add), r=[pp.k(), mu.k(), dk], w=[dk])
        K, Vf, WD, AD = tmpn["K"], tmpn["Vf"], tmpn["WD"], tmpn["AD"]
        tw, kk0, sq, rn, kka, t1, rkr = (tmpn[x] for x in ("tw", "kk0", "sq", "rn", "kka", "t1", "rkr"))
        asg = [tmpn["as0"], tmpn["as1"]]
        P.op("act", lambda e: e.activation(out=tw[:, 0:n], in_=WD[:, 0:n], func=AF.Tanh), r=[WD.k()], w=[tw.k()])
        for d in range(2):
            rows = slice(64 * d, 64 * d + 64)
            px = ps[2 + d]
            P.op("pe", lambda e: e.matmul(px[:, 0:n], lhsT=wup[rows, :], rhs=tw[rows, 0:n], start=True, stop=True), r=[wup.k(), tw.k()], w=[px.k()])
            P.op("act", lambda e: e.activation(out=LW[d][:, t0:t0 + n], in_=px[:, 0:n], func=AF.Sigmoid, bias=pcol[:, d:d + 1]), r=[px.k(), pcol.k()], w=[LW[d].k(ci)])
            P.op("dve", lambda e: e.tensor_scalar(out=LW[d][:, t0:t0 + n], in0=LW[d][:, t0:t0 + n], scalar1=-float(np.exp(-0.5)), scalar2=None, op0=ALU.mult),
                 r=[LW[d].k(ci)], w=[LW[d].k(ci)])
            pa = ps[4 + d]
            P.op("pe", lambda e: e.matmul(pa[:, 0:n], lhsT=aup[rows, :], rhs=AD[rows, 0:n], start=True, stop=True), r=[aup.k(), AD.k()], w=[pa.k()])
            P.op("act", lambda e: e.activation(out=asg[d][:, 0:n], in_=pa[:, 0:n], func=AF.Sigmoid, bias=pcol[:, 2 + d:3 + d]), r=[pa.k(), pcol.k()], w=[asg[d].k()])
        P.op("dve", lambda e: e.tensor_scalar(out=kk0[:, 0:n], in0=K[:, 0:n], scalar1=pcol[:, 4:5], scalar2=None, op0=ALU.mult), r=[K.k(), pcol.k()], w=[kk0.k()])
        P.op("dve", lambda e: e.tensor_tensor(out=sq[:, 0:n], in0=kk0[:, 0:n], in1=kk0[:, 0:n], op=ALU.mult), r=[kk0.k()], w=[sq.k()])
        pn = ps[6]
        P.op("pe", lambda e: e.matmul(pn[:, 0:n], lhsT=bones[:, :], rhs=sq[:, 0:n], start=True, stop=True), r=[bones.k(), sq.k()], w=[pn.k()])
        P.op("act", lambda e: e.activation(out=rn[:, 0:n], in_=pn[:, 0:n], func=AF.Sqrt), r=[pn.k()], w=[rn.k()])
        P.op("dve", lambda e: e.tensor_scalar(out=rn[:, 0:n], in0=rn[:, 0:n], scalar1=1e-12, scalar2=None, op0=ALU.max), r=[rn.k()], w=[rn.k()])
        P.op("dve", lambda e: e.reciprocal(out=rn[:, 0:n], in_=rn[:, 0:n]), r=[rn.k()], w=[rn.k()])
        P.op("dve", lambda e: e.scalar_tensor_tensor(out=A[:, t0:t0 + n], in0=kk0[:, 0:n], scalar=-1.0, in1=rn[:, 0:n], op0=ALU.mult, op1=ALU.mult),
             r=[kk0.k(), rn.k()], w=[A.k(ci)])
        P.op("dve", lambda e: e.tensor_scalar(out=kka[:, 0:n], in0=K[:, 0:n], scalar1=pcol[:, 5:6], scalar2=None, op0=ALU.mult), r=[K.k(), pcol.k()], w=[kka.k()])
        for d in range(2):
            P.op("dve", lambda e: e.scalar_tensor_tensor(out=BB[d][:, t0:t0 + n], in0=A[:, t0:t0 + n], scalar=-1.0, in1=asg[d][:, 0:n], op0=ALU.mult, op1=ALU.mult),
                 r=[A.k(ci), asg[d].k()], w=[BB[d].k(ci)])
            P.op("dve", lambda e: e.scalar_tensor_tensor(out=t1[:, 0:n], in0=asg[d][:, 0:n], scalar=-1.0, in1=kka[:, 0:n], op0=ALU.add, op1=ALU.mult),
                 r=[asg[d].k(), kka.k()], w=[t1.k()])
            P.op("dve", lambda e: e.tensor_tensor(out=KD[d][:, t0:t0 + n], in0=t1[:, 0:n], in1=K[:, 0:n], op=ALU.add), r=[t1.k(), K.k()], w=[KD[d].k(ci)])
        P.op("dve", lambda e: e.scalar_tensor_tensor(out=rkr[:, 0:n], in0=R[:, t0:t0 + n], scalar=pcol[:, 6:7], in1=K[:, 0:n], op0=ALU.mult, op1=ALU.mult),
             r=[R.k(ci), pcol.k(), K.k()], w=[rkr.k()])
        pbn = ps[7]
        nq = n // 64
        c0 = t0 // 64
        for hh in range(2):
            rows = slice(64 * hh, 64 * hh + 64)
            for q in range(nq):
                P.op("pe", lambda e: e.matmul(pbn[rows, q:q + 1], lhsT=rkr[rows, q * 64:(q + 1) * 64], rhs=onesc[rows, 0:1], start=True, stop=True),
                     r=[rkr.k(), onesc.k()], w=[pbn.k()])
                P.op("pe", lambda e: e.matmul(pbn[rows, 64 + q * 64:64 + (q + 1) * 64], lhsT=Vf[rows, q * 64:(q + 1) * 64], rhs=ident[rows, 64 * hh:64 * hh + 64],
                                               start=True, stop=True), r=[Vf.k(), ident.k()], w=[pbn.k()])
        P.op("dve", lambda e: e.tensor_copy(out=bon[:, c0:c0 + nq], in_=pbn[:, 0:nq]), r=[pbn.k()], w=[bon.k(ci)])
        P.op("act", lambda e: e.activation(out=vtm[:, c0:c0 + nq, :], in_=pbn[:, 64:64 + nq * 64].rearrange("p (q v) -> p q v", v=64), func=AF.Copy), r=[pbn.k()], w=[vtm.k(ci)])
    P.pop()

    P.push()
    yacc = P.sbuf(tag + "yacc", [128, NC64, 64], F32)
    P.push()
    masks = [P.sbuf(tag + "mask%d" % d, [128, 320], F32) for d in range(2)]
    for d, nm in enumerate(("mask_f", "mask_b")):
        P.dma("sp", masks[d][:], io[nm][:, :], w=[masks[d].k()])
    sts = []
    for d in range(2):
        s = {}
        for nm, w_ in (("G", 64), ("GE", 64), ("pre", 64), ("E1", 64), ("E2", 64), ("E3", 64), ("E4", 64), ("BK", 128), ("BKh", 128),
                       ("PP0", 128), ("PP1", 128), ("Wsb", 64), ("Usb", 64), ("ST", 64)):
            s[nm] = P.sbuf(tag + "s%d_%s" % (d, nm), [128, w_], F32)
        for q in range(2):
            for nm, w_ in (("AR", 128), ("AM", 320), ("Z", 64), ("BKT", 128), ("col", 4)):
                s[nm + str(q)] = P.sbuf(tag + "s%d_%s%d" % (d, nm, q), [128, w_], F32)
        s["ps"] = [P.psum(tag + "sp%d_%d" % (d, i), [128, 512], F32) for i in range(4)]
        P.op("pool", lambda e: e.memset(s["ST"][:, :], 0.0), w=[s["ST"].k()])
        sts.append(s)
    H = [slice(0, 64), slice(64, 128)]

    def indep(d, c, q):
        s = sts[d]
        pA, pL, pW, pT = s["ps"]
        tsl = slice(c * 64, (c + 1) * 64)
        G, GE, pre, E1, E2, E3, E4, BK, BKh = (s[x] for x in ("G", "GE", "pre", "E1", "E2", "E3", "E4", "BK", "BKh"))
        AR, AM, Z, BKT, col = (s[x + str(q)] for x in ("AR", "AM", "Z", "BKT", "col"))
        lw = LW[d][:, tsl]
        if d == 0:
            P.op("dve", lambda e: e.tensor_tensor_scan(out=G[:, :], data0=onesc[:, :], data1=lw, initial=0.0, op0=ALU.mult, op1=ALU.add),
                 r=[onesc.k(), LW[d].k()], w=[G.k()])
            gcol = G[:, 63:64]
        else:
            P.op("dve", lambda e: e.tensor_tensor_scan(out=pre[:, :], data0=onesc[:, :], data1=lw, initial=0.0, op0=ALU.mult, op1=ALU.add),
                 r=[onesc.k(), LW[d].k()], w=[pre.k()])
            P.op("dve", lambda e: e.tensor_tensor(out=G[:, :], in0=lw, in1=pre[:, :], op=ALU.subtract), r=[LW[d].k(), pre.k()], w=[G.k()])
            P.op("dve", lambda e: e.tensor_scalar(out=G[:, :], in0=G[:, :], scalar1=pre[:, 63:64], scalar2=None, op0=ALU.add), r=[G.k(), pre.k()], w=[G.k()])
            gcol = G[:, 0:1]
        P.op("dve", lambda e: e.tensor_tensor(out=GE[:, :], in0=G[:, :], in1=lw, op=ALU.subtract), r=[G.k(), LW[d].k()], w=[GE.k()])
        yield
        P.op("act", lambda e: e.activation(out=E1[:, :], in_=G[:, :], func=AF.Exp), r=[G.k()], w=[E1.k()])
        P.op("act", lambda e: e.activation(out=E2[:, :], in_=G[:, :], func=AF.Exp, scale=-1.0), r=[G.k()], w=[E2.k()])
        P.op("act", lambda e: e.activation(out=E3[:, :], in_=GE[:, :], func=AF.Exp), r=[GE.k()], w=[E3.k()])
        P.op("act", lambda e: e.activation(out=E4[:, :], in_=G[:, :], func=AF.Exp, scale=-1.0, bias=gcol), r=[G.k()], w=[E4.k()])
        P.op("act", lambda e: e.activation(out=col[:, 0:1], in_=gcol, func=AF.Exp), r=[G.k()], w=[col.k()])
        yield
        P.op("dve", lambda e: e.tensor_tensor(out=AR[:, 0:64], in0=A[:, tsl], in1=E3[:, :], op=ALU.mult), r=[A.k(), E3.k()], w=[AR.k(0)])
        P.op("dve", lambda e: e.tensor_tensor(out=AR[:, 64:128], in0=R[:, tsl], in1=E1[:, :], op=ALU.mult), r=[R.k(), E1.k()], w=[AR.k(1)])
        P.op("dve", lambda e: e.tensor_tensor(out=BK[:, 0:64], in0=BB[d][:, tsl], in1=E2[:, :], op=ALU.mult), r=[BB[d].k(), E2.k()], w=[BK.k(0)])
        P.op("dve", lambda e: e.tensor_tensor(out=BK[:, 64:128], in0=KD[d][:, tsl], in1=E2[:, :], op=ALU.mult), r=[KD[d].k(), E2.k()], w=[BK.k(1)])
        P.op("dve", lambda e: e.tensor_tensor(out=BKh[:, 0:64], in0=BB[d][:, tsl], in1=E4[:, :], op=ALU.mult), r=[BB[d].k(), E4.k()], w=[BKh.k(0)])
        P.op("dve", lambda e: e.tensor_tensor(out=BKh[:, 64:128], in0=KD[d][:, tsl], in1=E4[:, :], op=ALU.mult), r=[KD[d].k(), E4.k()], w=[BKh.k(1)])
        yield
        for hh in range(2):
            rw = H[hh]
            P.op("pe", lambda e: e.matmul(pA[rw, 0:128], lhsT=BK[rw, 0:64], rhs=AR[rw, 0:128], start=True, stop=True), r=[BK.k(), AR.k()], w=[pA.k()])
            P.op("pe", lambda e: e.matmul(pA[rw, 128:256], lhsT=BK[rw, 64:128], rhs=AR[rw, 0:128], start=True, stop=True), r=[BK.k(), AR.k()], w=[pA.k()])
            P.op("pe", lambda e: e.matmul(pA[rw, 256:320], lhsT=AR[rw, 0:64], rhs=BK[rw, 0:64], start=True, stop=True), r=[BK.k(), AR.k()], w=[pA.k()])
            P.op("pe", lambda e: e.matmul(pT[rw, 0:64], lhsT=BKh[rw, 0:64], rhs=ident[rw, 64 * hh:64 * hh + 64], start=True, stop=True), r=[BKh.k(), ident.k()], w=[pT.k()])
            P.op("pe", lambda e: e.matmul(pT[rw, 64:128], lhsT=BKh[rw, 64:128], rhs=ident[rw, 64 * hh:64 * hh + 64], start=True, stop=True), r=[BKh.k(), ident.k()], w=[pT.k()])
        yield
        P.op("dve", lambda e: e.tensor_tensor(out=AM[:, :], in0=pA[:, 0:320], in1=masks[d][:, :], op=ALU.mult), r=[pA.k(), masks[d].k()], w=[AM.k()])
        P.op("act", lambda e: e.activation(out=BKT[:, :], in_=pT[:, 0:128], func=AF.Copy), r=[pT.k()], w=[BKT.k()])
        P.op("dve", lambda e: e.tensor_tensor(out=Z[:, :], in0=AM[:, 0:64], in1=id64[:, :], op=ALU.add), r=[AM.k(), id64.k()], w=[Z.k()])
        yield
        Pc, PTc = AM[:, 0:64], AM[:, 256:320]
        Pk = AM.k()
        for lvl in range(1, 6):
            PPn = s["PP%d" % (lvl % 2)]
            for hh in range(2):
                rw = H[hh]
                P.op("pe", lambda e: e.matmul(pL[rw, 0:64], lhsT=Pc[rw, :], rhs=PTc[rw, :], start=True, stop=True), r=[Pk], w=[pL.k()])
                if lvl < 5:
                    P.op("pe", lambda e: e.matmul(pL[rw, 64:128], lhsT=PTc[rw, :], rhs=Pc[rw, :], start=True, stop=True), r=[Pk], w=[pL.k()])
            yield
            wcols = 128 if lvl < 5 else 64
            P.op("act", lambda e: e.activation(out=PPn[:, 0:wcols], in_=pL[:, 0:wcols], func=AF.Copy), r=[pL.k()], w=[PPn.k()])
            yield
            PTc, Pc, Pk = PPn[:, 0:64], PPn[:, 64:128], PPn.k()
            for hh in range(2):
                rw = H[hh]
                P.op("pe", lambda e: e.matmul(pL[rw, 128:192], lhsT=PTc[rw, :], rhs=Z[rw, :], start=True, stop=True), r=[Pk, Z.k()], w=[pL.k()])
            yield
            P.op("dve", lambda e: e.tensor_tensor(out=Z[:, :], in0=Z[:, :], in1=pL[:, 128:192], op=ALU.add), r=[Z.k(), pL.k()], w=[Z.k()])
            yield

    def dep(d, c, q, first):
        s = sts[d]
        pA, pL, pW, pT = s["ps"]
        Wsb, Usb, ST = s["Wsb"], s["Usb"], s["ST"]
        AR, AM, Z, BKT, col = (s[x + str(q)] for x in ("AR", "AM", "Z", "BKT", "col"))
        for hh in range(2):
            rw = H[hh]
            P.op("pe", lambda e: e.matmul(pW[rw, 0:64], lhsT=AR[rw, 0:64], rhs=ST[rw, :], start=True, stop=False), r=[AR.k(), ST.k()], w=[pW.k()])
            P.op("pe", lambda e: e.matmul(pW[rw, 0:64], lhsT=AM[rw, 128:192], rhs=vtm[rw, c, :], start=False, stop=True), r=[AM.k(), vtm.k()], w=[pW.k()])
        yield
        P.op("act", lambda e: e.activation(out=Wsb[:, :], in_=pW[:, 0:64], func=AF.Copy), r=[pW.k()], w=[Wsb.k()])
        yield
        for hh in range(2):
            rw = H[hh]
            P.op("pe", lambda e: e.matmul(pW[rw, 64:128], lhsT=Z[rw, :], rhs=Wsb[rw, :], start=True, stop=True), r=[Z.k(), Wsb.k()], w=[pW.k()])
        yield
        P.op("act", lambda e: e.activation(out=Usb[:, :], in_=pW[:, 64:128], func=AF.Copy), r=[pW.k()], w=[Usb.k()])
        yield
        for hh in range(2):
            rw = H[hh]
            P.op("pe", lambda e: e.matmul(pW[rw, 192:256], lhsT=BKT[rw, 0:64], rhs=Usb[rw, :], start=True, stop=False), r=[BKT.k(), Usb.k()], w=[pW.k()])
            P.op("pe", lambda e: e.matmul(pW[rw, 192:256], lhsT=BKT[rw, 64:128], rhs=vtm[rw, c, :], start=False, stop=True), r=[BKT.k(), vtm.k()], w=[pW.k()])
        for hh in range(2):
            rw = H[hh]
            P.op("pe", lambda e: e.matmul(pW[rw, 128:192], lhsT=AR[rw, 64:128], rhs=ST[rw, :], start=True, stop=False), r=[AR.k(), ST.k()], w=[pW.k()])
            P.op("pe", lambda e: e.matmul(pW[rw, 128:192], lhsT=AM[rw, 64:128], rhs=Usb[rw, :], start=False, stop=False), r=[AM.k(), Usb.k()], w=[pW.k()])
            P.op("pe", lambda e: e.matmul(pW[rw, 128:192], lhsT=AM[rw, 192:256], rhs=vtm[rw, c, :], start=False, stop=True), r=[AM.k(), vtm.k()], w=[pW.k()])
        yield
        P.op("dve", lambda e: e.scalar_tensor_tensor(out=ST[:, :], in0=ST[:, :], scalar=col[:, 0:1], in1=pW[:, 192:256], op0=ALU.mult, op1=ALU.add),
             r=[ST.k(), col.k(), pW.k()], w=[ST.k()])
        if first:
            P.op("dve", lambda e: e.tensor_copy(out=yacc[:, c, :], in_=pW[:, 128:192]), r=[pW.k()], w=[yacc.k(c)])
        else:
            P.op("dve", lambda e: e.tensor_tensor(out=yacc[:, c, :], in0=yacc[:, c, :], in1=pW[:, 128:192], op=ALU.add), r=[pW.k(), yacc.k(c)], w=[yacc.k(c)])
        yield

    def run_interleaved(gens):
        gens = list(gens)
        while gens:
            for g in list(gens):
                try:
                    next(g)
                except StopIteration:
                    gens.remove(g)

    order_f = list(range(NC64))
    order_b = list(range(nctx_c - 1, -1, -1)) + list(range(NC64 - 1, nctx_c - 1, -1))
    orders = (order_f, order_b)
    done = set()
    run_interleaved([indep(d, orders[d][0], 0) for d in range(2)])
    for i in range(NC64):
        gens = []
        for d in range(2):
            c = orders[d][i]
            gens.append(dep(d, c, i % 2, c not in done))
            done.add(c)
        if i + 1 < NC64:
            for d in range(2):
                gens.append(indep(d, orders[d][i + 1], (i + 1) % 2))
        run_interleaved(gens)
    P.pop()

    P.push()
    w_g = P.sbuf(tag + "wg", [128, KC, 128], BF16)
    stage = [P.sbuf(tag + "owst%d" % i, [128, KC, 64], F32) for i in range(2)]
    hTc = [P.sbuf(tag + "ohTc0", [128, KC, 256], BF16)] * 2
    lnw = P.sbuf(tag + "lnw", [128, 64], F32)
    lnb = P.sbuf(tag + "lnb", [128, 64], F32)
    gt = P.sbuf(tag + "gt", [128, 4, 64], F32)
    sc = P.sbuf(tag + "sc", [128, 8], F32)
    cen = P.sbuf(tag + "cen", [128, 64], F32)
    sq2 = P.sbuf(tag + "sq2", [128, 64], F32)
    yn = P.sbuf(tag + "yn", [128, 64], F32)
    yo = [P.sbuf(tag + "yo%d" % i, [128, 64], BF16) for i in range(2)]
    pg = [P.psum(tag + "pg%d" % i, [128, 512], F32) for i in range(2)]
    P.dma("sp", lnw[:], io["lnw"][:, :], w=[lnw.k()])
    P.dma("sp", lnb[:], io["lnb"][:, :], w=[lnb.k()])
    load_w_bf16(P, io["w_g"], w_g, 128, stage, "g", blk=64)
    for ci, (t0, n, sa, sb) in enumerate(seg_chunks(T, n_ctx)):
        hc = hTc[ci % 2]
        P.dma("sp", hc[:, :, 0:n], hT[:, :, t0:t0 + n].rearrange("k p t -> p k t"), w=[hc.k()])
        pgc = pg[ci % 2]
        nq = n // 64
        for hh in range(2):
            rows = slice(64 * hh, 64 * hh + 64)
            for q in range(nq):
                for kc in range(KC):
                    P.op("pe", lambda e: e.matmul(pgc[rows, q * 64:(q + 1) * 64], lhsT=hc[:, kc, q * 64:(q + 1) * 64], rhs=w_g[:, kc, hh * 64:(hh + 1) * 64],
                                                   start=(kc == 0), stop=(kc == KC - 1)), r=[hc.k(), w_g.k()], w=[pgc.k()])
        P.op("act", lambda e: e.activation(out=gt[:, 0:nq, :], in_=pgc[:, 0:nq * 64].rearrange("p (q v) -> p q v", v=64), func=AF.Silu), r=[pgc.k()], w=[gt.k()])
        for q in range(nq):
            c = t0 // 64 + q
            y = yacc[:, c, :]
            P.op("dve", lambda e: e.reduce_sum(out=sc[:, 0:1], in_=y, axis=AX.X), r=[yacc.k(c)], w=[sc.k(0)])
            P.op("dve", lambda e: e.tensor_scalar(out=sc[:, 1:2], in0=sc[:, 0:1], scalar1=-1.0 / 64, scalar2=None, op0=ALU.mult), r=[sc.k(0)], w=[sc.k(1)])
            P.op("dve", lambda e: e.tensor_scalar(out=cen[:, :], in0=y, scalar1=sc[:, 1:2], scalar2=None, op0=ALU.add), r=[yacc.k(c), sc.k(1)], w=[cen.k()])
            P.op("dve", lambda e: e.tensor_tensor(out=sq2[:, :], in0=cen[:, :], in1=cen[:, :], op=ALU.mult), r=[cen.k()], w=[sq2.k()])
            P.op("dve", lambda e: e.reduce_sum(out=sc[:, 2:3], in_=sq2[:, :], axis=AX.X), r=[sq2.k()], w=[sc.k(2)])
            P.op("act", lambda e: e.activation(out=sc[:, 3:4], in_=sc[:, 2:3], func=AF.Sqrt, scale=1.0 / 64, bias=RW_GN_EPS), r=[sc.k(2)], w=[sc.k(3)])
            P.op("dve", lambda e: e.reciprocal(out=sc[:, 4:5], in_=sc[:, 3:4]), r=[sc.k(3)], w=[sc.k(4)])
            P.op("dve", lambda e: e.scalar_tensor_tensor(out=yn[:, :], in0=cen[:, :], scalar=sc[:, 4:5], in1=lnw[:, :], op0=ALU.mult, op1=ALU.mult),
                 r=[cen.k(), sc.k(4), lnw.k()], w=[yn.k()])
            P.op("dve", lambda e: e.tensor_tensor(out=yn[:, :], in0=yn[:, :], in1=lnb[:, :], op=ALU.add), r=[yn.k(), lnb.k()], w=[yn.k()])
            P.op("dve", lambda e: e.scalar_tensor_tensor(out=yn[:, :], in0=vtm[:, c, :], scalar=bon[:, c:c + 1], in1=yn[:, :], op0=ALU.mult, op1=ALU.add),
                 r=[vtm.k(), bon.k(), yn.k()], w=[yn.k()])
            y2 = yo[q % 2]
            P.op("dve", lambda e: e.tensor_tensor(out=y2[:, :], in0=yn[:, :], in1=gt[:, q, :], op=ALU.mult), r=[yn.k(), gt.k()], w=[y2.k()])
            for hh in range(2):
                P.dma("pool", io["ya"][c * 64:(c + 1) * 64, hh * 64:(hh + 1) * 64], y2[64 * hh:64 * hh + 64, :], r=[y2.k()], w=[io["ya"].k(c, hh)])
    P.pop()
    P.pop()


KC = 16
EPS = 1e-6


def ca_program(P, NTOK, chunks, io, do_merge, mode):
    modc = P.sbuf("c_modc", [128, KC, 8], F32)
    gsc = P.sbuf("c_gsc", [128, KC, 2], F32)
    ones = P.sbuf("c_ones", [128, 128], F32)
    P.dma("sp", modc[:], io["modc"][:, :, :], w=[modc.k()])
    P.dma("sp", ones[:], io["ones"][:, :], w=[ones.k()])
    for j in range(2):
        P.op("dve", lambda e: e.scalar_tensor_tensor(out=gsc[:, :, j:j + 1], in0=modc[:, :, 2 + j:3 + j], scalar=1.0, in1=modc[:, :, 6:7],
                                                      op0=ALU.add, op1=ALU.mult), r=[modc.k()], w=[gsc.k(j)])
    if do_merge:
        mergedT = P.sbuf("c_mergedT", [128, KC, NTOK], BF16)
        P.push()
        hT = P.sbuf("c_hT", [128, KC, NTOK], BF16)
        yT = P.sbuf("c_yT", [128, 24, NTOK], BF16)
        P.dma("sp", hT[:], io["hT"][:, :, :].rearrange("k p t -> p k t"), w=[hT.k()])
        for q in range(3):
            P.dma("sp", yT[:, q * 8:(q + 1) * 8, :], io["yT"][q * 8:(q + 1) * 8, :, :].rearrange("k p t -> p k t"), w=[yT.k(q)])
        stg = [P.sbuf("c_stg%d" % i, [128, KC, 128], F32) for i in range(2)]
        stb = [P.sbuf("c_stb0", [128, 24, 128], F32)] * 2
        wm = [P.sbuf("c_wm%d" % i, [128, KC, 384], BF16) for i in range(2)]
        wb = [P.sbuf("c_wb%d" % i, [128, 24, 128], BF16) for i in range(2)]
        sig = [P.sbuf("c_sig%d" % i, [128, 512], F32) for i in range(2)]
        macc = P.sbuf("c_macc", [128, 512], F32)
        prod = P.sbuf("c_prod", [128, 512], F32)
        pg = [P.psum("c_pg%d" % i, [128, 512], F32) for i in range(2)]
        pbr = [P.psum("c_pbr%d" % i, [128, 512], F32) for i in range(2)]
        si = 0
        it = 0
        for db in range(KC):
            wmb, wbb = wm[db % 2], wb[db % 2]
            for nb in range(3):
                st = stg[si % 2]
                si += 1
                c0 = nb * 2048 + db * 128
                P.dma("sp", st[:], io["w_merge"][:, :, c0:c0 + 128], w=[st.k()])
                if nb % 2 == 0:
                    P.op("dve", lambda e: e.tensor_copy(out=wmb[:, :, nb * 128:(nb + 1) * 128], in_=st[:]), r=[st.k()], w=[wmb.k(nb)])
                else:
                    P.op("act", lambda e: e.activation(out=wmb[:, :, nb * 128:(nb + 1) * 128], in_=st[:], func=AF.Copy), r=[st.k()], w=[wmb.k(nb)])
            sb_ = stb[db % 2]
            P.dma("sp", sb_[:], io["w_branch"][:, :, db * 128:(db + 1) * 128], w=[sb_.k()])
            P.op("act", lambda e: e.activation(out=wbb[:], in_=sb_[:], func=AF.Copy), r=[sb_.k()], w=[wbb.k()])
            for (t0, n, isc) in chunks:
                for nb in range(3):
                    g_, b_ = pg[it % 2], pbr[it % 2]
                    sg = sig[it % 2]
                    it += 1
                    for kc in range(KC):
                        P.op("pe", lambda e: e.matmul(g_[:, 0:n], lhsT=wmb[:, kc, nb * 128:(nb + 1) * 128], rhs=hT[:, kc, t0:t0 + n],
                                                       start=(kc == 0), stop=(kc == KC - 1)), r=[wmb.k(nb), hT.k()], w=[g_.k()])
                    for cc in range(8):
                        P.op("pe", lambda e: e.matmul(b_[:, 0:n], lhsT=wbb[:, nb * 8 + cc, :], rhs=yT[:, nb * 8 + cc, t0:t0 + n],
                                                       start=(cc == 0), stop=(cc == 7)), r=[wbb.k(), yT.k(nb)], w=[b_.k()])
                    P.op("act", lambda e: e.activation(out=sg[:, 0:n], in_=g_[:, 0:n], func=AF.Sigmoid), r=[g_.k()], w=[sg.k()])
                    if nb == 0:
                        P.op("dve", lambda e: e.tensor_tensor(out=macc[:, 0:n], in0=sg[:, 0:n], in1=b_[:, 0:n], op=ALU.mult), r=[sg.k(), b_.k()], w=[macc.k()])
                    else:
                        P.op("dve", lambda e: e.tensor_tensor(out=prod[:, 0:n], in0=sg[:, 0:n], in1=b_[:, 0:n], op=ALU.mult), r=[sg.k(), b_.k()], w=[prod.k()])
                        if nb == 1:
                            P.op("dve", lambda e: e.tensor_tensor(out=macc[:, 0:n], in0=macc[:, 0:n], in1=prod[:, 0:n], op=ALU.add), r=[macc.k(), prod.k()], w=[macc.k()])
                        else:
                            P.op("dve", lambda e: e.tensor_tensor(out=mergedT[:, db, t0:t0 + n], in0=macc[:, 0:n], in1=prod[:, 0:n], op=ALU.add),
                                 r=[macc.k(), prod.k()], w=[mergedT.k(db, t0)])
        P.pop()

    P.push()
    znew = P.sbuf("c_znew", [128, KC, NTOK], F32)
    sq = [P.sbuf("c_sq%d" % i, [128, 512], F32) for i in range(2)]
    pss = [P.psum("c_pss%d" % i, [128, 512], F32) for i in range(len(chunks))]
    if do_merge:
        ze = [P.sbuf("c_ze%d" % i, [128, NTOK], F32) for i in range(2)]
        sto = [P.sbuf("c_sto%d" % i, [128, KC, 128], F32) for i in range(2)]
        wo = [P.sbuf("c_wo%d" % i, [128, KC, 128], BF16) for i in range(2)]
        po = [P.psum("c_po%d" % i, [128, 512], F32) for i in range(2)]
    it = 0
    for eb in range(KC):
        if do_merge:
            st, wob, z_e = sto[eb % 2], wo[eb % 2], ze[eb % 2]
            P.dma("sp", st[:], io["w_out"][:, :, eb * 128:(eb + 1) * 128], w=[st.k()])
            if eb % 2 == 0:
                P.op("act", lambda e: e.activation(out=wob[:], in_=st[:], func=AF.Copy), r=[st.k()], w=[wob.k()])
            else:
                P.op("dve", lambda e: e.tensor_copy(out=wob[:], in_=st[:]), r=[st.k()], w=[wob.k()])
            P.dma("sp", z_e[:], io["zT"][eb, :, :], w=[z_e.k()])
        else:
            P.dma("sp", znew[:, eb, :], io["zT"][eb, :, :], w=[znew.k(eb)])
        for ci, (t0, n, isc) in enumerate(chunks):
            if do_merge:
                p_ = po[it % 2]
                for db in range(KC):
                    P.op("pe", lambda e: e.matmul(p_[:, 0:n], lhsT=wob[:, db, :], rhs=mergedT[:, db, t0:t0 + n], start=(db == 0), stop=(db == KC - 1)),
                         r=[wob.k(), mergedT.k()], w=[p_.k()])
                gcol = modc[:, eb, 0:1] if isc else modc[:, eb, 1:2]
                P.op("dve", lambda e: e.scalar_tensor_tensor(out=znew[:, eb, t0:t0 + n], in0=p_[:, 0:n], scalar=gcol, in1=z_e[:, t0:t0 + n],
                                                              op0=ALU.mult, op1=ALU.add), r=[p_.k(), modc.k(), z_e.k()], w=[znew.k(eb, t0)])
            s_ = sq[it % 2]
            it += 1
            P.op("act", lambda e: e.activation(out=s_[:, 0:n], in_=znew[:, eb, t0:t0 + n], func=AF.Square), r=[znew.k(eb)], w=[s_.k()])
            P.op("pe", lambda e: e.matmul(pss[ci][:, 0:n], lhsT=ones[:, :], rhs=s_[:, 0:n], start=(eb == 0), stop=(eb == KC - 1)),
                 r=[ones.k(), s_.k()], w=[pss[ci].k()])
        if do_merge and mode == "mod":
            P.dma("pool", io["zTn"][eb, :, :], znew[:, eb, :], r=[znew.k(eb)], w=[io["zTn"].k(eb)])
    rstd = P.sbuf("c_rstd", [128, NTOK], F32)
    for ci, (t0, n, isc) in enumerate(chunks):
        P.op("act", lambda e: e.activation(out=rstd[:, t0:t0 + n], in_=pss[ci][:, 0:n], func=AF.Sqrt, scale=1.0 / 2048, bias=EPS), r=[pss[ci].k()], w=[rstd.k(ci)])
        P.op("dve", lambda e: e.reciprocal(out=rstd[:, t0:t0 + n], in_=rstd[:, t0:t0 + n]), r=[rstd.k(ci)], w=[rstd.k(ci)])
    odt = BF16 if mode == "mod" else F32
    hn = [P.sbuf("c_hn%d" % i, [128, NTOK], F32) for i in range(2)]
    ho = [P.sbuf("c_ho%d" % i, [128, NTOK], odt) for i in range(2)]
    oname = "hTn" if mode == "mod" else "oT"
    for eb in range(KC):
        h1, h2 = hn[eb % 2], ho[eb % 2]
        for ci, (t0, n, isc) in enumerate(chunks):
            j = 0 if isc else 1
            P.op("dve", lambda e: e.scalar_tensor_tensor(out=h1[:, t0:t0 + n], in0=znew[:, eb, t0:t0 + n], scalar=gsc[:, eb, j:j + 1], in1=rstd[:, t0:t0 + n],
                                                          op0=ALU.mult, op1=ALU.mult), r=[znew.k(eb), gsc.k(), rstd.k(ci)], w=[h1.k(ci)])
            P.op("dve", lambda e: e.tensor_scalar(out=h2[:, t0:t0 + n], in0=h1[:, t0:t0 + n], scalar1=modc[:, eb, 4 + j:5 + j], scalar2=None, op0=ALU.add),
                 r=[h1.k(ci), modc.k()], w=[h2.k(ci)])
        P.dma("pool", io[oname][eb, :, :], h2[:, :], r=[h2.k()], w=[io[oname].k(eb)])
    P.pop()


KC = 16


def mod_program(P, io):
    cT = P.sbuf("mm_cT", [128, KC, 3], F32)
    sT = P.sbuf("mm_sT", [128, KC, 3], F32)
    wa = [P.sbuf("mm_wa%d" % l, [128, KC, 768], F32) for l in range(2)]
    ba = P.sbuf("mm_ba", [3, 2, 768], F32)
    mo = P.sbuf("mm_mo", [3, 2, 768], F32)
    pm = [P.psum("mm_p%d" % i, [128, 512], F32) for i in range(2)]
    P.dma("sp", cT[:], io["cT"][:, :, :], w=[cT.k()])
    for l in range(2):
        P.dma("sp", wa[l][:], io["wa"][l, :, :, :], w=[wa[l].k()])
        P.dma("sp", ba[:, l, :], io["ba"][l, :, :], w=[ba.k(l)])
    P.op("act", lambda e: e.activation(out=sT[:], in_=cT[:], func=AF.Silu), r=[cT.k()], w=[sT.k()])
    it = 0
    for l in range(2):
        for hf in range(2):
            p_ = pm[it % 2]
            it += 1
            for kc in range(KC):
                P.op("pe", lambda e: e.matmul(p_[0:3, 0:384], lhsT=sT[:, kc, 0:3], rhs=wa[l][:, kc, hf * 384:(hf + 1) * 384],
                                               start=(kc == 0), stop=(kc == KC - 1)), r=[sT.k(), wa[l].k()], w=[p_.k()])
            P.op("dve", lambda e: e.tensor_tensor(out=mo[:, l, hf * 384:(hf + 1) * 384], in0=p_[0:3, 0:384], in1=ba[:, l, hf * 384:(hf + 1) * 384], op=ALU.add),
                 r=[p_.k(), ba.k(l)], w=[mo.k(l, hf)])
        P.dma("pool", io["mod"][l, :, :], mo[:, l, :], r=[mo.k(l)], w=[io["mod"].k(l)])

import ml_dtypes
from concourse.bass_utils import run_bass_kernel_spmd

NCORE = 8
DM = 2048
N_CTX = 256
N_LAT = 4096
TT = N_CTX + N_LAT
NTOK = TT // 4
CA_CHUNKS = [(0, 256, True), (256, 512, False), (768, 320, False)]
NPBF = ml_dtypes.bfloat16


def _wl(wc):
    return np.ascontiguousarray(wc.reshape(-1, 128, wc.shape[1]).transpose(1, 0, 2))


def _fm(a):
    return np.ascontiguousarray(a.T.reshape(-1, 128, a.shape[0]))


def _partner(d):
    return d + 32 if (d % 64) < 32 else d - 32


def _rope_tables(T, n_ctx):
    n_lat = T - n_ctx
    rows = n_lat // 64
    row = np.repeat(np.arange(rows), 64).astype(np.float32)
    col = np.tile(np.arange(64), rows).astype(np.float32)
    inv_freq = (np.float32(10000.0) ** (-np.arange(0, 64, 2, dtype=np.float32) / np.float32(64))).astype(np.float32)
    ang_lat = np.stack([row[:, None] * inv_freq, col[:, None] * inv_freq], axis=1)
    ang = np.concatenate([np.zeros((n_ctx, 2, 32), np.float32), ang_lat], axis=0)
    cos, sin = np.cos(ang).astype(np.float32), np.sin(ang).astype(np.float32)
    cosT = np.zeros((128, T), np.float32)
    sinT = np.zeros((128, T), np.float32)
    for d in range(128):
        a, half, i = d // 64, (d % 64) // 32, d % 32
        cosT[d] = cos[:, a, i]
        sinT[d] = sin[:, a, i] * (-1.0 if half == 0 else 1.0)
    return cosT, sinT


def _consts():
    c = {}
    pm = np.zeros((128, 128), np.float32)
    for d in range(128):
        pm[_partner(d), d] = 1.0
    c["pm"] = pm
    c["ones"] = np.ones((128, 128), np.float32)
    u = np.arange(128)
    c["tri_f"] = (u[:, None] <= u[None, :]).astype(np.float32)
    c["tri_b"] = (u[:, None] >= u[None, :]).astype(np.float32)
    c["mneg_f"] = np.where(u[:, None] <= u[None, :], 0.0, -30000.0).astype(np.float32)
    c["mneg_b"] = np.where(u[:, None] >= u[None, :], 0.0, -30000.0).astype(np.float32)
    i = np.arange(64)
    sT_f = (i[:, None] < i[None, :]).astype(np.float32)
    iT_f = (i[:, None] <= i[None, :]).astype(np.float32)
    st_f = (i[None, :] < i[:, None]).astype(np.float32)
    sT_b = (i[:, None] > i[None, :]).astype(np.float32)
    iT_b = (i[:, None] >= i[None, :]).astype(np.float32)
    st_b = (i[None, :] > i[:, None]).astype(np.float32)
    c["mask_f"] = np.tile(np.concatenate([sT_f, iT_f, sT_f, iT_f, st_f], axis=1), (2, 1))
    c["mask_b"] = np.tile(np.concatenate([sT_b, iT_b, sT_b, iT_b, st_b], axis=1), (2, 1))
    bones = np.zeros((128, 128), np.float32)
    bones[:64, :64] = 1
    bones[64:, 64:] = 1
    c["bones"] = bones
    c["ident"] = np.eye(128, dtype=np.float32)
    c["id64"] = np.tile(np.eye(64, dtype=np.float32), (2, 1))
    c["onesc"] = np.ones((128, 64), np.float32)
    c["cosT"], c["sinT"] = _rope_tables(TT, N_CTX)
    return c


_PROGS = {}


def _declare(P, specs):
    io = {}
    for name, (shape, dt, kind) in specs.items():
        io[name] = P.dram(name, list(shape), dt, kind=kind)
    return io


B_IN = {
    "hT": ([16, 128, TT], BF16),
    "g_w_fm": ([128, 16, 384], F32), "g_w_tm": ([128, 16, 384], F32), "cosT": ([128, TT], F32), "sinT": ([128, TT], F32),
    "pm": ([128, 128], F32), "ones": ([128, 128], F32), "gvec": ([128, 4], F32),
    "m_w_fm": ([128, 16, 256], F32), "m_w_tm": ([128, 16, 900], F32), "gbias": ([128, 4], F32), "normg": ([128, 256], F32),
    "tri_f": ([128, 128], F32), "tri_b": ([128, 128], F32), "mneg_f": ([128, 128], F32), "mneg_b": ([128, 128], F32),
    "ident": ([128, 128], F32), "id64": ([128, 64], F32), "bones": ([128, 128], F32), "onesc": ([128, 64], F32),
    "mask_f": ([128, 320], F32), "mask_b": ([128, 320], F32),
}
for _p in range(2):
    B_IN.update({"r%d_w_fm" % _p: ([128, 16, 640], F32), "r%d_w_g" % _p: ([128, 16, 128], F32), "r%d_mu" % _p: ([128, 10], F32),
                 "r%d_wup" % _p: ([128, 128], F32), "r%d_aup" % _p: ([128, 128], F32), "r%d_pcol" % _p: ([128, 8], F32),
                 "r%d_lnw" % _p: ([128, 64], F32), "r%d_lnb" % _p: ([128, 64], F32)})
B_OUT = {"yb": ([TT, 256], BF16), "yc": ([TT, 256], BF16), "ya0": ([TT, 128], BF16), "ya1": ([TT, 128], BF16)}


def _prog_B():
    if "B" in _PROGS:
        return _PROGS["B"]
    nc = bass.Bass("TRN2", target_bir_lowering=False)
    P = Prog(nc)
    specs = {k: (v[0], v[1], "ExternalInput") for k, v in B_IN.items()}
    specs.update({k: (v[0], v[1], "ExternalOutput") for k, v in B_OUT.items()})
    io = _declare(P, specs)
    P.push()
    gqa_program(P, TT, N_CTX, {"hT": io["hT"], "w_fm": io["g_w_fm"], "w_tm": io["g_w_tm"], "cosT": io["cosT"], "sinT": io["sinT"],
                               "pm": io["pm"], "ones": io["ones"], "gvec": io["gvec"], "yb": io["yb"]})
    P.pop()
    P.push()
    mlstm_program(P, TT, N_CTX, {"hT": io["hT"], "w_fm": io["m_w_fm"], "w_tm": io["m_w_tm"], "gbias": io["gbias"], "normg": io["normg"],
                                 "tri_f": io["tri_f"], "tri_b": io["tri_b"], "mneg_f": io["mneg_f"], "mneg_b": io["mneg_b"], "yc": io["yc"]})
    P.pop()
    for p in range(2):
        P.push()
        d = {"hT": io["hT"], "ya": io["ya%d" % p]}
        for k in ("ident", "id64", "bones", "onesc", "mask_f", "mask_b"):
            d[k] = io[k]
        for k in ("w_fm", "w_g", "mu", "wup", "aup", "pcol", "lnw", "lnb"):
            d[k] = io["r%d_%s" % (p, k)]
        rwkv_program(P, TT, N_CTX, d, tag="r%d_" % p)
        P.pop()
    P.finish()
    P.close()
    _PROGS["B"] = nc
    return nc


def _prog_CA(do_merge, mode):
    key = ("CA", do_merge, mode)
    if key in _PROGS:
        return _PROGS[key]
    nc = bass.Bass("TRN2", target_bir_lowering=False)
    P = Prog(nc)
    specs = {"zT": ([16, 128, NTOK], F32, "ExternalInput"), "modc": ([128, 16, 8], F32, "ExternalInput"), "ones": ([128, 128], F32, "ExternalInput")}
    if do_merge:
        specs.update({"hT": ([16, 128, NTOK], BF16, "ExternalInput"), "yT": ([24, 128, NTOK], BF16, "ExternalInput"),
                      "w_merge": ([128, 16, 6144], F32, "ExternalInput"), "w_branch": ([128, 24, 2048], F32, "ExternalInput"),
                      "w_out": ([128, 16, 2048], F32, "ExternalInput")})
    if mode == "mod":
        specs["hTn"] = ([16, 128, NTOK], BF16, "ExternalOutput")
        if do_merge:
            specs["zTn"] = ([16, 128, NTOK], F32, "ExternalOutput")
    else:
        specs["oT"] = ([16, 128, NTOK], F32, "ExternalOutput")
    io = _declare(P, specs)
    ca_program(P, NTOK, CA_CHUNKS, io, do_merge, mode)
    P.finish()
    P.close()
    _PROGS[key] = nc
    return nc


def _prog_M():
    if "M" in _PROGS:
        return _PROGS["M"]
    nc = bass.Bass("TRN2", target_bir_lowering=False)
    P = Prog(nc)
    io = _declare(P, {"cT": ([128, 16, 3], F32, "ExternalInput"), "wa": ([2, 128, 16, 768], F32, "ExternalInput"),
                      "ba": ([2, 3, 768], F32, "ExternalInput"), "mod": ([2, 3, 768], F32, "ExternalOutput")})
    mod_program(P, io)
    P.finish()
    P.close()
    _PROGS["M"] = nc
    return nc


def _run(nc, in_maps):
    res = run_bass_kernel_spmd(nc, in_maps, core_ids=list(range(NCORE)))
    return res.results


def _b_inputs(l, b, j, hT_full, consts, w_in, I):
    m = {"hT": hT_full[b]}
    for k in ("cosT", "sinT", "pm", "ones", "tri_f", "tri_b", "mneg_f", "mneg_b", "ident", "id64", "bones", "onesc", "mask_f", "mask_b"):
        m[k] = consts[k]
    W = w_in[l]
    kv = j // 2
    m["g_w_fm"] = _wl(np.concatenate([W[:, 4352 + 256 * j:4352 + 256 * j + 256], W[:, 5376 + 128 * kv:5376 + 128 * kv + 128]], axis=1))
    m["g_w_tm"] = _wl(np.concatenate([W[:, 5632 + 128 * kv:5632 + 128 * kv + 128], W[:, 5888 + 256 * j:5888 + 256 * j + 256]], axis=1))
    pidx = np.array([_partner(d) for d in range(128)])
    gq, gk = I["at_q_g"][l], I["at_k_g"][l]
    m["gvec"] = np.ascontiguousarray(np.stack([gq, gq[pidx], gk, gk[pidx]], axis=1).astype(np.float32))
    m["m_w_fm"] = _wl(np.concatenate([W[:, 6912 + 128 * j:6912 + 128 * j + 128], W[:, 7424 + 128 * j:7424 + 128 * j + 128]], axis=1))
    gcols = [9984 + 4 * t + j for t in range(4)]
    m["m_w_tm"] = _wl(np.concatenate([W[:, 7424 + 128 * j:7424 + 128 * j + 128], W[:, 7936 + 256 * j:7936 + 256 * j + 256], W[:, gcols],
                                      W[:, 8960 + 256 * j:8960 + 256 * j + 256], W[:, 10000 + 256 * j:10000 + 256 * j + 256]], axis=1))
    m["gbias"] = np.ascontiguousarray(np.tile(I["ml_gate_b"][l][:, j][None, :], (128, 1)).astype(np.float32))
    m["normg"] = np.ascontiguousarray(np.tile(I["ml_norm_g"][l][256 * j:256 * j + 256][None, :], (128, 1)).astype(np.float32))
    for p in range(2):
        ch0 = 256 * j + 128 * p
        cols = np.concatenate([np.arange(ch0, ch0 + 128), 1024 + np.arange(ch0, ch0 + 128), 2048 + np.arange(ch0, ch0 + 128),
                               np.arange(3072, 3200), np.arange(3200, 3328)])
        m["r%d_w_fm" % p] = _wl(W[:, cols])
        m["r%d_w_g" % p] = _wl(W[:, 3328 + ch0:3328 + ch0 + 128])
        mu = I["shift_mu"][l][:, cols]
        m["r%d_mu" % p] = np.ascontiguousarray(np.concatenate([mu[0].reshape(5, 128).T, mu[1].reshape(5, 128).T], axis=1).astype(np.float32))
        m["r%d_wup" % p] = np.ascontiguousarray(I["rw_w_up"][l][:, :, ch0:ch0 + 128].reshape(128, 128))
        m["r%d_aup" % p] = np.ascontiguousarray(I["rw_a_up"][l][:, :, ch0:ch0 + 128].reshape(128, 128))
        sl = slice(ch0, ch0 + 128)
        m["r%d_pcol" % p] = np.ascontiguousarray(np.stack([I["rw_w0"][l][0][sl], I["rw_w0"][l][1][sl], I["rw_a0"][l][0][sl], I["rw_a0"][l][1][sl],
                                                            I["rw_k_k"][l][sl], I["rw_k_a"][l][sl], I["rw_r_k"][l].reshape(1024)[sl],
                                                            np.zeros(128, np.float32)], axis=1).astype(np.float32))
        m["r%d_lnw" % p] = np.ascontiguousarray(np.repeat(I["rw_ln_w"][l][sl].reshape(2, 1, 64), 64, axis=1).reshape(128, 64))
        m["r%d_lnb" % p] = np.ascontiguousarray(np.repeat(I["rw_ln_b"][l][sl].reshape(2, 1, 64), 64, axis=1).reshape(128, 64))
    return m


def _modc(gt, sc, sh, g, b, j):
    z = np.zeros((3, DM), np.float32)
    gt = z if gt is None else gt
    sc = z if sc is None else sc
    sh = z if sh is None else sh
    rc = 2 if j == 0 else b
    cols = [gt[rc], gt[b], sc[rc], sc[b], sh[rc], sh[b], g, np.zeros(DM, np.float32)]
    out = np.zeros((128, 16, 8), np.float32)
    for i, c in enumerate(cols):
        out[:, :, i] = np.asarray(c, np.float32).reshape(16, 128).T
    return out


def kernel(**I):
    I = {k: np.asarray(v) for k, v in I.items()}
    consts = _consts()
    cores = [(c // 4, c % 4) for c in range(NCORE)]
    cstack = np.stack([I["c"][0], I["c"][1], I["c_ctx"]], axis=0).astype(np.float32)
    cT = np.ascontiguousarray(cstack.T.reshape(16, 128, 3).transpose(1, 0, 2))
    in_maps = []
    for c in range(NCORE):
        cs = slice(768 * c, 768 * (c + 1))
        wa = np.stack([_wl(I["w_ada"][l][:, cs]) for l in range(2)], axis=0)
        ba = np.stack([np.tile(I["b_ada"][l][cs][None, :], (3, 1)) for l in range(2)], axis=0).astype(np.float32)
        in_maps.append({"cT": cT, "wa": np.ascontiguousarray(wa), "ba": np.ascontiguousarray(ba)})
    r = _run(_prog_M(), in_maps)
    mod = np.concatenate([np.asarray(r[c]["mod"]) for c in range(NCORE)], axis=2)
    sh = [mod[l][:, 0:DM] for l in range(2)]
    sc = [mod[l][:, DM:2 * DM] for l in range(2)]
    gt = [mod[l][:, 2 * DM:3 * DM] for l in range(2)]
    zfull = [np.concatenate([I["ctx"][b], I["x"][b]], axis=0) for b in range(2)]
    zT = [_fm(zfull[b][NTOK * j:NTOK * (j + 1)]) for (b, j) in cores]
    in_maps = [{"zT": zT[c], "modc": _modc(None, sc[0], sh[0], I["norm_g"][0], b, j), "ones": consts["ones"]} for c, (b, j) in enumerate(cores)]
    r = _run(_prog_CA(False, "mod"), in_maps)
    hT = [np.asarray(r[c]["hTn"]) for c in range(NCORE)]
    out = None
    for l in range(2):
        hT_full = [np.ascontiguousarray(np.concatenate(hT[4 * b:4 * b + 4], axis=2)) for b in range(2)]
        in_maps = [_b_inputs(l, b, j, hT_full, consts, I["w_in"], I) for (b, j) in cores]
        r = _run(_prog_B(), in_maps)
        yT = []
        for b in range(2):
            y = np.zeros((TT, 3, 1024), NPBF)
            for j in range(4):
                rr = r[4 * b + j]
                for p in range(2):
                    y[:, 0, 256 * j + 128 * p:256 * j + 128 * p + 128] = np.asarray(rr["ya%d" % p])
                y[:, 1, 256 * j:256 * j + 256] = np.asarray(rr["yb"])
                y[:, 2, 256 * j:256 * j + 256] = np.asarray(rr["yc"])
            yT.append(_fm(y.reshape(TT, 3072)))
        last = (l == 1)
        w_merge = _wl(I["w_in"][l][:, 11024:17168])
        w_branch = _wl(I["w_branch"][l].reshape(3072, DM))
        w_out = _wl(I["w_out"][l])
        in_maps = []
        for c, (b, j) in enumerate(cores):
            ts = slice(NTOK * j, NTOK * (j + 1))
            if last:
                mc = _modc(gt[l], None, None, I["final_g"], b, j)
            else:
                mc = _modc(gt[l], sc[l + 1], sh[l + 1], I["norm_g"][l + 1], b, j)
            in_maps.append({"zT": zT[c], "modc": mc, "ones": consts["ones"], "hT": hT[c], "yT": np.ascontiguousarray(yT[b][:, :, ts]),
                            "w_merge": w_merge, "w_branch": w_branch, "w_out": w_out})
        r = _run(_prog_CA(True, "final" if last else "mod"), in_maps)
        if last:
            out = np.zeros((2, N_LAT, DM), np.float32)
            for c, (b, j) in enumerate(cores):
                o = np.asarray(r[c]["oT"]).reshape(DM, NTOK).T
                t0 = NTOK * j
                lo = max(t0, N_CTX)
                out[b, lo - N_CTX:t0 + NTOK - N_CTX] = o[lo - t0:]
        else:
            zT = [np.asarray(r[c]["zTn"]) for c in range(NCORE)]
            hT = [np.asarray(r[c]["hTn"]) for c in range(NCORE)]
    return out
```

```python
import numpy as np
from contextlib import ExitStack
import concourse.bass as bass
import concourse.mybir as mybir

F32 = mybir.dt.float32
BF16 = mybir.dt.bfloat16
AF = mybir.ActivationFunctionType
ALU = mybir.AluOpType
AX = mybir.AxisListType

NS_DMA = 8


class Tile:
    def __init__(self, t, name):
        self.t = t
        self.name = name

    def __getitem__(self, idx):
        return self.t[idx]

    def k(self, *sub):
        return (self.name,) + tuple(sub)


class Prog:
    def __init__(self, nc):
        self.nc = nc
        self.es = ExitStack()
        self.stacks = [self.es]
        self.eng = {"pe": nc.tensor, "act": nc.scalar, "dve": nc.vector, "pool": nc.gpsimd, "sp": nc.sync}
        self.sem = {}
        self.cnt = {}
        for e in ("pe", "act", "dve", "pool"):
            self.sem[e] = self.es.enter_context(nc.semaphore("c_" + e))
            self.cnt[e] = 0
        self.unit = {e: 1 for e in self.sem}
        self.dq_n = {}
        for q in ("sp", "pool", "act"):
            self.dq_n[q] = 0
            for s in range(NS_DMA):
                ch = ("dma", q, s)
                self.sem[ch] = self.es.enter_context(nc.semaphore("d_%s%d" % (q, s)))
                self.cnt[ch] = 0
                self.unit[ch] = 16
        self.seen = {e: {} for e in self.eng}
        self.lastw = {}
        self.readers = {}
        self.nuniq = 0
        self.n_inst = 0

    def sbuf(self, name, shape, dtype=F32):
        name = "s_" + name
        t = self.stacks[-1].enter_context(self.nc.sbuf_tensor(name, list(shape), dtype))
        return Tile(t, name)

    def psum(self, name, shape, dtype=F32):
        name = "p_" + name
        t = self.stacks[-1].enter_context(self.nc.psum_tensor(name, list(shape), dtype))
        return Tile(t, name)

    def dram(self, name, shape, dtype=F32, kind=None):
        if kind is None:
            t = self.nc.dram_tensor(name, list(shape), dtype)
        else:
            t = self.nc.dram_tensor(name, list(shape), dtype, kind=kind)
        return Tile(t.ap(), "D:" + name)

    @staticmethod
    def _overlap(a, b):
        n = min(len(a), len(b))
        return a[:n] == b[:n]

    def _deps(self, rkeys, wkeys):
        deps = set()
        for k in list(rkeys) + list(wkeys):
            d = self.lastw.get(k[0])
            if d:
                for sk, v in d.items():
                    if self._overlap(sk, k):
                        deps.add(v)
        for k in wkeys:
            d = self.readers.get(k[0])
            if d:
                for sk, lst in d.items():
                    if self._overlap(sk, k):
                        deps.update(lst)
        return deps

    def _record(self, rkeys, wkeys, me):
        for k in wkeys:
            d = self.lastw.setdefault(k[0], {})
            for sk in [sk for sk in d if len(sk) >= len(k) and sk[:len(k)] == k]:
                del d[sk]
            d[k] = me
            r = self.readers.get(k[0])
            if r:
                for sk in [sk for sk in r if self._overlap(sk, k)]:
                    if len(sk) >= len(k):
                        del r[sk]
        for k in rkeys:
            r = self.readers.setdefault(k[0], {})
            lst = r.setdefault(k, [])
            lst[:] = [x for x in lst if x[0] != me[0]]
            lst.append(me)

    def _wait(self, e, deps, skip_same=None):
        eng = self.eng[e]
        seen = self.seen[e]
        best = {}
        for ch, n in deps:
            if ch == skip_same:
                continue
            if seen.get(ch, 0) >= n:
                continue
            if best.get(ch, 0) < n:
                best[ch] = n
        for ch, n in best.items():
            eng.wait_ge(self.sem[ch], n * self.unit[ch])
            seen[ch] = n
            self.n_inst += 1

    def op(self, e, fn, r=(), w=()):
        if e != "pe":
            w = list(w) + [(k[0],) for k in r if k[0].startswith("p_")]
        deps = self._deps(r, w)
        self._wait(e, deps, skip_same=("pe" if e == "pe" else None))
        ins = fn(self.eng[e])
        self.cnt[e] += 1
        ins.then_inc(self.sem[e], 1)
        self.n_inst += 1
        me = (e, self.cnt[e])
        self._record(r, w, me)
        return ins

    def dma(self, q, out, in_, r=(), w=(), **kw):
        i = self.dq_n[q]
        self.dq_n[q] += 1
        ch = ("dma", q, i % NS_DMA)
        deps = self._deps(r, w)
        if self.cnt[ch] > 0:
            deps.add((ch, self.cnt[ch]))
        self._wait(q, deps)
        ins = self.eng[q].dma_start(out=out, in_=in_, **kw)
        self.cnt[ch] += 1
        ins.then_inc(self.sem[ch], 16)
        self.n_inst += 1
        self._record(r, w, (ch, self.cnt[ch]))
        return ins

    def barrier(self):
        for e in self.eng:
            deps = set()
            for ch, n in self.cnt.items():
                if n > 0:
                    deps.add((ch, n))
            self._wait(e, deps)
        self.lastw.clear()
        self.readers.clear()

    def push(self):
        self.stacks.append(ExitStack())

    def pop(self):
        self.barrier()
        self.stacks.pop().close()

    def finish(self):
        self.barrier()

    def close(self):
        self.es.close()


KC = 16
HD = 128
EPS = 1e-6


def chunks_of(T, n_ctx):
    out = []
    t = 0
    while t < n_ctx:
        n = min(512, n_ctx - t)
        out.append((t, n))
        t += n
    while t < T:
        n = min(512, T - t)
        out.append((t, n))
        t += n
    return out


def load_w_bf16(P, w_dram, dst_bf, ncols, stage, tag, blk=128):
    i = 0
    for c0 in range(0, ncols, blk):
        nb = min(blk, ncols - c0)
        st = stage[i % 2]
        P.dma("sp", st[:, :, 0:nb], w_dram[:, :, c0:c0 + nb], r=[], w=[st.k()])
        eng = "dve" if i % 2 == 0 else "act"
        if eng == "dve":
            P.op("dve", lambda e: e.tensor_copy(out=dst_bf[:, :, c0:c0 + nb], in_=st[:, :, 0:nb]), r=[st.k()], w=[dst_bf.k(c0)])
        else:
            P.op("act", lambda e: e.activation(out=dst_bf[:, :, c0:c0 + nb], in_=st[:, :, 0:nb], func=AF.Copy), r=[st.k()], w=[dst_bf.k(c0)])
        i += 1


def gqa_program(P, T, n_ctx, io):
    nc = P.nc
    NT = T // 128
    hT, w_fm_d, w_tm_d = io["hT"], io["w_fm"], io["w_tm"]
    qT = [P.sbuf("qT%d" % h, [128, T], BF16) for h in range(2)]
    kT = P.sbuf("kT", [128, T], BF16)
    Vaug = P.sbuf("Vaug", [128, NT, 132], BF16)
    gs = P.sbuf("gs", [128, NT, 256], F32)
    w_fm = P.sbuf("w_fm", [128, KC, 384], BF16)
    w_tm = P.sbuf("w_tm", [128, KC, 384], BF16)
    stage = [P.sbuf("wst%d" % i, [128, KC, 128], F32) for i in range(2)]
    hTc = [P.sbuf("hTc%d" % i, [128, KC, 512], BF16) for i in range(2)]
    cosc = [P.sbuf("cosc%d" % i, [128, 512], F32) for i in range(2)]
    sinc = [P.sbuf("sinc%d" % i, [128, 512], F32) for i in range(2)]
    pm = P.sbuf("pm", [128, 128], F32)
    ones = P.sbuf("ones", [128, 128], F32)
    gvec = P.sbuf("gvec", [128, 4], F32)
    raw = [P.sbuf("raw%d" % i, [128, 512], F32) for i in range(2)]
    sq = P.sbuf("sq", [128, 512], F32)
    rstd = P.sbuf("rstd", [128, 512], F32)
    t1 = P.sbuf("t1", [128, 512], F32)
    t2 = P.sbuf("t2", [128, 512], F32)
    ps_a = [P.psum("ps_a%d" % i, [128, 512], F32) for i in range(2)]
    ps_b = P.psum("ps_b", [128, 512], F32)
    ps_c = P.psum("ps_c", [128, 512], F32)

    P.dma("sp", pm[:], io["pm"][:, :], w=[pm.k()])
    P.dma("sp", ones[:], io["ones"][:, :], w=[ones.k()])
    P.dma("sp", gvec[:], io["gvec"][:, :], w=[gvec.k()])
    load_w_bf16(P, w_fm_d, w_fm, 384, stage, "fm")
    load_w_bf16(P, w_tm_d, w_tm, 384, stage, "tm")
    P.op("pool", lambda e: e.memset(Vaug[:, :, 128:132], 1.0), w=[Vaug.k("ones")])

    scale = HD ** -0.5
    for ci, (t0, n) in enumerate(chunks_of(T, n_ctx)):
        hc = hTc[ci % 2]
        P.dma("sp", hc[:, :, 0:n], hT[:, :, t0:t0 + n].rearrange("k p t -> p k t"), w=[hc.k()])
        cc, sc = cosc[ci % 2], sinc[ci % 2]
        P.dma("sp", cc[:, 0:n], io["cosT"][:, t0:t0 + n], w=[cc.k()])
        P.dma("sp", sc[:, 0:n], io["sinT"][:, t0:t0 + n], w=[sc.k()])
        for bi in range(3):
            ps = ps_a[bi % 2]
            for kc in range(KC):
                P.op("pe", lambda e: e.matmul(ps[:, 0:n], lhsT=w_fm[:, kc, bi * 128:(bi + 1) * 128], rhs=hc[:, kc, 0:n],
                                               start=(kc == 0), stop=(kc == KC - 1)),
                     r=[w_fm.k(bi * 128), hc.k()], w=[ps.k()])
            rw = raw[bi % 2]
            P.op("act", lambda e: e.activation(out=rw[:, 0:n], in_=ps[:, 0:n], func=AF.Copy), r=[ps.k()], w=[rw.k()])
            P.op("dve", lambda e: e.tensor_tensor(out=sq[:, 0:n], in0=rw[:, 0:n], in1=rw[:, 0:n], op=ALU.mult), r=[rw.k()], w=[sq.k()])
            P.op("pe", lambda e: e.matmul(ps_b[:, 0:n], lhsT=ones[:, :], rhs=sq[:, 0:n], start=True, stop=True),
                 r=[ones.k(), sq.k()], w=[ps_b.k()])
            P.op("pe", lambda e: e.matmul(ps_c[:, 0:n], lhsT=pm[:, :], rhs=rw[:, 0:n], start=True, stop=True),
                 r=[pm.k(), rw.k()], w=[ps_c.k()])
            P.op("act", lambda e: e.activation(out=rstd[:, 0:n], in_=ps_b[:, 0:n], func=AF.Sqrt, scale=1.0 / HD, bias=EPS),
                 r=[ps_b.k()], w=[rstd.k()])
            P.op("dve", lambda e: e.reciprocal(out=rstd[:, 0:n], in_=rstd[:, 0:n]), r=[rstd.k()], w=[rstd.k()])
            gi = 0 if bi < 2 else 2
            P.op("dve", lambda e: e.scalar_tensor_tensor(out=t1[:, 0:n], in0=rw[:, 0:n], scalar=gvec[:, gi:gi + 1], in1=cc[:, 0:n],
                                                          op0=ALU.mult, op1=ALU.mult), r=[rw.k(), gvec.k(), cc.k()], w=[t1.k()])
            P.op("dve", lambda e: e.scalar_tensor_tensor(out=t2[:, 0:n], in0=ps_c[:, 0:n], scalar=gvec[:, gi + 1:gi + 2], in1=sc[:, 0:n],
                                                          op0=ALU.mult, op1=ALU.mult), r=[ps_c.k(), gvec.k(), sc.k()], w=[t2.k()])
            P.op("dve", lambda e: e.tensor_tensor(out=t1[:, 0:n], in0=t1[:, 0:n], in1=t2[:, 0:n], op=ALU.add), r=[t1.k(), t2.k()], w=[t1.k()])
            dst = qT[bi] if bi < 2 else kT
            sc_f = scale if bi < 2 else 1.0
            P.op("dve", lambda e: e.scalar_tensor_tensor(out=dst[:, t0:t0 + n], in0=t1[:, 0:n], scalar=sc_f, in1=rstd[:, 0:n],
                                                          op0=ALU.mult, op1=ALU.mult), r=[t1.k(), rstd.k()], w=[dst.k(ci)])
        for ti in range(n // 128):
            tt = (t0 // 128) + ti
            ps = ps_a[ti % 2]
            for kc in range(KC):
                P.op("pe", lambda e: e.matmul(ps[:, 0:384], lhsT=hc[:, kc, ti * 128:(ti + 1) * 128], rhs=w_tm[:, kc, 0:384],
                                               start=(kc == 0), stop=(kc == KC - 1)),
                     r=[w_tm.k(), hc.k()], w=[ps.k()])
            P.op("dve", lambda e: e.tensor_copy(out=Vaug[:, tt, 0:128], in_=ps[:, 0:128]), r=[ps.k()], w=[Vaug.k("v", tt)])
            P.op("act", lambda e: e.activation(out=gs[:, tt, :], in_=ps[:, 128:384], func=AF.Silu), r=[ps.k()], w=[gs.k(tt)])

    pex = [P.sbuf("pex%d" % i, [128, 512], BF16) for i in range(3)]
    acc = [ps_b, ps_c] + [P.psum("acc%d" % i, [128, 512], F32) for i in range(2)]
    rec = P.sbuf("rec", [128, 4], F32)
    yt = [P.sbuf("yt%d" % i, [128, 128], F32) for i in range(2)]
    yo = [P.sbuf("yo%d" % i, [128, 128], BF16) for i in range(2)]
    nctx_t = n_ctx // 128
    it = 0
    for h in range(2):
        for ci, (t0, n) in enumerate(chunks_of(T, n_ctx)):
            nk = nctx_t if t0 < n_ctx else NT
            nsub = n // 128
            for kt in range(nk):
                ps = ps_a[it % 2]
                px = pex[it % 3]
                it += 1
                P.op("pe", lambda e: e.matmul(ps[:, 0:n], lhsT=kT[:, kt * 128:(kt + 1) * 128], rhs=qT[h][:, t0:t0 + n], start=True, stop=True),
                     r=[kT.k(), qT[h].k(ci)], w=[ps.k()])
                P.op("act", lambda e: e.activation(out=px[:, 0:n], in_=ps[:, 0:n], func=AF.Exp), r=[ps.k()], w=[px.k()])
                for s in range(nsub):
                    a = acc[s]
                    P.op("pe", lambda e: e.matmul(a[:, 0:129], lhsT=px[:, s * 128:(s + 1) * 128], rhs=Vaug[:, kt, 0:129],
                                                   start=(kt == 0), stop=(kt == nk - 1)),
                         r=[px.k(), Vaug.k()], w=[a.k()])
            for s in range(nsub):
                a = acc[s]
                tt = t0 // 128 + s
                y1 = yt[s % 2]
                y2 = yo[s % 2]
                P.op("dve", lambda e: e.reciprocal(out=rec[:, s:s + 1], in_=a[:, 128:129]), r=[a.k()], w=[rec.k(s)])
                P.op("dve", lambda e: e.scalar_tensor_tensor(out=y2[:, :], in0=a[:, 0:128], scalar=rec[:, s:s + 1],
                                                              in1=gs[:, tt, h * 128:(h + 1) * 128], op0=ALU.mult, op1=ALU.mult),
                     r=[a.k(), rec.k(s), gs.k(tt)], w=[y2.k()])
                P.dma("pool", io["yb"][tt * 128:(tt + 1) * 128, h * 128:(h + 1) * 128], y2[:, :], r=[y2.k()], w=[io["yb"].k(tt, h)])


GATE_CAP = 15.0
NEG = -30000.0


def mlstm_program(P, T, n_ctx, io):
    NCH = T // 128
    hT = io["hT"]
    qT = P.sbuf("m_qT", [128, T], F32)
    qTb = P.sbuf("m_qTb", [128, T], BF16)
    kTb = P.sbuf("m_kTb", [128, T], BF16)
    ktm = P.sbuf("m_ktm", [128, NCH, 128], F32)
    vaug = P.sbuf("m_vaug", [128, NCH, 260], BF16)
    og = P.sbuf("m_og", [128, NCH, 256], F32)
    gpre = P.sbuf("m_gpre", [128, NCH, 4], F32)
    lfn = P.sbuf("m_lfn", [128, NCH, 2], F32)
    gbias = P.sbuf("m_gbias", [128, 4], F32)
    normg = P.sbuf("m_normg", [128, 256], F32)
    tri = [P.sbuf("m_tri%d" % d, [128, 128], F32) for d in range(2)]
    mneg = [P.sbuf("m_mneg%d" % d, [128, 128], F32) for d in range(2)]
    onesf = P.sbuf("m_onesf", [128, 128], F32)
    pb = [P.psum("m_pb%d" % i, [128, 512], F32) for i in range(8)]
    P.push()
    w_fm = P.sbuf("m_wfm", [128, KC, 256], BF16)
    w_tm = P.sbuf("m_wtm", [128, KC, 900], BF16)
    stage = [P.sbuf("m_wst%d" % i, [128, KC, 128], F32) for i in range(2)]
    hTc = [P.sbuf("m_hTc%d" % i, [128, KC, 512], BF16) for i in range(2)]
    sig = P.sbuf("m_sig", [128, 256], F32)
    sil = P.sbuf("m_sil", [128, 256], F32)

    P.dma("sp", gbias[:], io["gbias"][:, :], w=[gbias.k()])
    P.dma("sp", normg[:], io["normg"][:, :], w=[normg.k()])
    for d, nm in enumerate(("tri_f", "tri_b")):
        P.dma("sp", tri[d][:], io[nm][:, :], w=[tri[d].k()])
    for d, nm in enumerate(("mneg_f", "mneg_b")):
        P.dma("sp", mneg[d][:], io[nm][:, :], w=[mneg[d].k()])
    P.op("pool", lambda e: e.memset(onesf[:, :], 1.0), w=[onesf.k()])
    P.op("pool", lambda e: e.memset(vaug[:, :, 256:260], 1.0), w=[vaug.k("ones")])
    load_w_bf16(P, io["w_fm"], w_fm, 256, stage, "fm")
    load_w_bf16(P, io["w_tm"], w_tm, 900, stage, "tm")

    qscale = 128 ** -0.5
    for ci, (t0, n) in enumerate(chunks_of(T, n_ctx)):
        hc = hTc[ci % 2]
        P.dma("sp", hc[:, :, 0:n], hT[:, :, t0:t0 + n].rearrange("k p t -> p k t"), w=[hc.k()])
        for bi in range(2):
            ps = pb[bi]
            for kc in range(KC):
                P.op("pe", lambda e: e.matmul(ps[:, 0:n], lhsT=w_fm[:, kc, bi * 128:(bi + 1) * 128], rhs=hc[:, kc, 0:n],
                                               start=(kc == 0), stop=(kc == KC - 1)), r=[w_fm.k(bi * 128), hc.k()], w=[ps.k()])
            if bi == 0:
                P.op("act", lambda e: e.activation(out=qT[:, t0:t0 + n], in_=ps[:, 0:n], func=AF.Copy, scale=qscale), r=[ps.k()], w=[qT.k(ci)])
                P.op("dve", lambda e: e.tensor_copy(out=qTb[:, t0:t0 + n], in_=qT[:, t0:t0 + n]), r=[qT.k(ci)], w=[qTb.k(ci)])
            else:
                P.op("act", lambda e: e.activation(out=kTb[:, t0:t0 + n], in_=ps[:, 0:n], func=AF.Copy), r=[ps.k()], w=[kTb.k(ci)])
        for ti in range(n // 128):
            c = t0 // 128 + ti
            p1, p2 = pb[2 + (ti % 2) * 2], pb[3 + (ti % 2) * 2]
            for kc in range(KC):
                P.op("pe", lambda e: e.matmul(p1[:, 0:388], lhsT=hc[:, kc, ti * 128:(ti + 1) * 128], rhs=w_tm[:, kc, 0:388],
                                               start=(kc == 0), stop=(kc == KC - 1)), r=[w_tm.k(), hc.k()], w=[p1.k()])
            for kc in range(KC):
                P.op("pe", lambda e: e.matmul(p2[:, 0:512], lhsT=hc[:, kc, ti * 128:(ti + 1) * 128], rhs=w_tm[:, kc, 388:900],
                                               start=(kc == 0), stop=(kc == KC - 1)), r=[w_tm.k(), hc.k()], w=[p2.k()])
            P.op("dve", lambda e: e.tensor_copy(out=ktm[:, c, :], in_=p1[:, 0:128]), r=[p1.k()], w=[ktm.k(c)])
            P.op("act", lambda e: e.activation(out=vaug[:, c, 0:256], in_=p1[:, 128:384], func=AF.Copy), r=[p1.k()], w=[vaug.k("v", c)])
            P.op("dve", lambda e: e.tensor_tensor(out=gpre[:, c, :], in0=p1[:, 384:388], in1=gbias[:, :], op=ALU.add), r=[p1.k(), gbias.k()], w=[gpre.k(c)])
            P.op("act", lambda e: e.activation(out=sig[:, :], in_=p2[:, 0:256], func=AF.Sigmoid), r=[p2.k()], w=[sig.k()])
            P.op("act", lambda e: e.activation(out=sil[:, :], in_=p2[:, 256:512], func=AF.Silu), r=[p2.k()], w=[sil.k()])
            P.op("dve", lambda e: e.tensor_tensor(out=og[:, c, :], in0=sig[:, :], in1=sil[:, :], op=ALU.mult), r=[sig.k(), sil.k()], w=[og.k(c)])

    P.pop()
    P.push()
    Hacc = P.sbuf("m_Hacc", [128, NCH, 256], F32)
    P.op("act", lambda e: e.activation(out=gpre[:, :, :], in_=gpre[:, :, :], func=AF.Tanh, scale=1.0 / GATE_CAP), r=[gpre.k()], w=[gpre.k()])
    P.op("dve", lambda e: e.tensor_scalar(out=gpre[:, :, :], in0=gpre[:, :, :], scalar1=GATE_CAP, scalar2=None, op0=ALU.mult), r=[gpre.k()], w=[gpre.k()])
    P.op("act", lambda e: e.activation(out=lfn[:, :, :], in_=gpre[:, :, 2:4], func=AF.Exp, scale=-1.0), r=[gpre.k()], w=[lfn.k()])
    P.op("act", lambda e: e.activation(out=lfn[:, :, :], in_=lfn[:, :, :], func=AF.Ln, bias=1.0), r=[lfn.k()], w=[lfn.k()])
    P.op("dve", lambda e: e.tensor_scalar(out=lfn[:, :, :], in0=lfn[:, :, :], scalar1=-1.0, scalar2=None, op0=ALU.mult), r=[lfn.k()], w=[lfn.k()])

    nctx_c = n_ctx // 128
    order_f = list(range(NCH))
    order_b = list(range(nctx_c - 1, -1, -1)) + list(range(NCH - 1, nctx_c - 1, -1))
    st = []
    for d in range(2):
        s = {}
        s["Cn"] = P.sbuf("m_Cn%d" % d, [128, 260], F32)
        s["Cnb"] = P.sbuf("m_Cnb%d" % d, [128, 260], BF16)
        for nm, shp, dt in (("LFbc", [128, 128], F32), ("tmp", [128, 128], F32), ("ET", [128, 128], F32)):
            s[nm] = P.sbuf("m_%s%d" % (nm, d), shp, dt)
        for q in range(2):
            for nm, shp, dt in (("expB", [128, 128], F32), ("qt", [128, 128], BF16), ("smT", [128, 128], BF16), ("kw", [128, 128], BF16), ("col", [128, 8], F32)):
                s[nm + str(q)] = P.sbuf("m_%s%d_%d" % (nm, d, q), shp, dt)
        P.op("pool", lambda e: e.memset(s["Cn"][:, :], 0.0), w=[s["Cn"].k()])
        P.op("pool", lambda e: e.memset(s["Cnb"][:, :], 0.0), w=[s["Cnb"].k()])
        st.append(s)

    def indep(d, c, q):
        s = st[d]
        p1, p2, p4 = pb[4 * d], pb[4 * d + 1], pb[4 * d + 3]
        col, expB, qt, smT, kw = (s[x + str(q)] for x in ("col", "expB", "qt", "smT", "kw"))
        gcol = 127 if d == 0 else 0
        tsl = slice(c * 128, (c + 1) * 128)
        P.op("dve", lambda e: e.tensor_scalar(out=s["LFbc"][:, :], in0=onesf[:, :], scalar1=lfn[:, c, d:d + 1], scalar2=None, op0=ALU.mult),
             r=[onesf.k(), lfn.k()], w=[s["LFbc"].k()])
        yield
        P.op("pe", lambda e: e.matmul(p1[:, 0:128], lhsT=s["LFbc"][:, :], rhs=tri[d][:, :], start=True, stop=True),
             r=[s["LFbc"].k(), tri[d].k()], w=[p1.k()])
        P.op("pe", lambda e: e.matmul(p1[:, 128:129], lhsT=tri[d][:, :], rhs=lfn[:, c, d:d + 1], start=True, stop=True),
             r=[tri[d].k(), lfn.k()], w=[p1.k()])
        P.op("pe", lambda e: e.matmul(p2[:, 0:128], lhsT=kTb[:, tsl], rhs=qTb[:, tsl], start=True, stop=True),
             r=[kTb.k(), qTb.k()], w=[p2.k()])
        yield
        P.op("dve", lambda e: e.tensor_tensor(out=col[:, 0:1], in0=gpre[:, c, d:d + 1], in1=p1[:, 128:129], op=ALU.subtract),
             r=[gpre.k(), p1.k()], w=[col.k(0)])
        P.op("dve", lambda e: e.tensor_tensor(out=s["tmp"][:, :], in0=p1[:, 0:128], in1=mneg[d][:, :], op=ALU.add),
             r=[p1.k(), mneg[d].k()], w=[s["tmp"].k()])
        yield
        P.op("act", lambda e: e.activation(out=s["ET"][:, :], in_=s["tmp"][:, :], func=AF.Exp, bias=col[:, 0:1]),
             r=[s["tmp"].k(), col.k(0)], w=[s["ET"].k()])
        P.op("act", lambda e: e.activation(out=expB[:, :], in_=p1[:, 0:128], func=AF.Exp), r=[p1.k()], w=[expB.k()])
        P.op("act", lambda e: e.activation(out=col[:, 1:2], in_=p1[:, gcol:gcol + 1], func=AF.Exp, bias=col[:, 0:1]),
             r=[p1.k(), col.k(0)], w=[col.k(1)])
        yield
        P.op("dve", lambda e: e.tensor_tensor(out=qt[:, :], in0=qT[:, tsl], in1=expB[:, :], op=ALU.mult),
             r=[qT.k(), expB.k()], w=[qt.k()])
        P.op("dve", lambda e: e.tensor_tensor(out=smT[:, :], in0=p2[:, 0:128], in1=s["ET"][:, :], op=ALU.mult),
             r=[p2.k(), s["ET"].k()], w=[smT.k()])
        P.op("dve", lambda e: e.tensor_scalar(out=kw[:, :], in0=ktm[:, c, :], scalar1=col[:, 1:2], scalar2=None, op0=ALU.mult),
             r=[ktm.k(c), col.k(1)], w=[kw.k()])
        yield
        P.op("pe", lambda e: e.matmul(p4[:, 0:257], lhsT=kw[:, :], rhs=vaug[:, c, 0:257], start=True, stop=True),
             r=[kw.k(), vaug.k()], w=[p4.k()])
        yield

    def dep(d, c, q, first):
        s = st[d]
        p3, p4 = pb[4 * d + 2], pb[4 * d + 3]
        col, expB, qt, smT = (s[x + str(q)] for x in ("col", "expB", "qt", "smT"))
        gcol = 127 if d == 0 else 0
        P.op("pe", lambda e: e.matmul(p3[:, 0:257], lhsT=smT[:, :], rhs=vaug[:, c, 0:257], start=True, stop=False),
             r=[smT.k(), vaug.k()], w=[p3.k()])
        P.op("pe", lambda e: e.matmul(p3[:, 0:257], lhsT=qt[:, :], rhs=s["Cnb"][:, 0:257], start=False, stop=True),
             r=[qt.k(), s["Cnb"].k()], w=[p3.k()])
        yield
        P.op("dve", lambda e: e.scalar_tensor_tensor(out=s["Cn"][:, 0:257], in0=s["Cn"][:, 0:257], scalar=expB[:, gcol:gcol + 1], in1=p4[:, 0:257],
                                                      op0=ALU.mult, op1=ALU.add), r=[s["Cn"].k(), expB.k(), p4.k()], w=[s["Cn"].k()])
        yield
        P.op("act", lambda e: e.activation(out=s["Cnb"][:, 0:257], in_=s["Cn"][:, 0:257], func=AF.Copy), r=[s["Cn"].k()], w=[s["Cnb"].k()])
        P.op("act", lambda e: e.activation(out=col[:, 2:3], in_=p3[:, 256:257], func=AF.Abs), r=[p3.k()], w=[col.k(2)])
        yield
        P.op("dve", lambda e: e.tensor_scalar(out=col[:, 2:3], in0=col[:, 2:3], scalar1=1.0, scalar2=None, op0=ALU.max),
             r=[col.k(2)], w=[col.k(2)])
        P.op("dve", lambda e: e.reciprocal(out=col[:, 3:4], in_=col[:, 2:3]), r=[col.k(2)], w=[col.k(3)])
        if first:
            P.op("dve", lambda e: e.tensor_scalar(out=Hacc[:, c, :], in0=p3[:, 0:256], scalar1=col[:, 3:4], scalar2=None, op0=ALU.mult),
                 r=[p3.k(), col.k(3)], w=[Hacc.k(c)])
        else:
            P.op("dve", lambda e: e.scalar_tensor_tensor(out=Hacc[:, c, :], in0=p3[:, 0:256], scalar=col[:, 3:4], in1=Hacc[:, c, :],
                                                          op0=ALU.mult, op1=ALU.add), r=[p3.k(), col.k(3), Hacc.k(c)], w=[Hacc.k(c)])
        yield

    def run_interleaved(gens):
        gens = list(gens)
        while gens:
            for g in list(gens):
                try:
                    next(g)
                except StopIteration:
                    gens.remove(g)

    orders = (order_f, order_b)
    done = set()
    run_interleaved([indep(d, orders[d][0], 0) for d in range(2)])
    for i in range(NCH):
        gens = []
        for d in range(2):
            c = orders[d][i]
            gens.append(dep(d, c, i % 2, c not in done))
            done.add(c)
        if i + 1 < NCH:
            for d in range(2):
                gens.append(indep(d, orders[d][i + 1], (i + 1) % 2))
        run_interleaved(gens)

    ssq = P.sbuf("m_ssq", [128, NCH], F32)
    junk = P.sbuf("m_junk", [128, 256], F32)
    yo = [P.sbuf("m_yo%d" % i, [128, 256], BF16) for i in range(2)]
    ytmp = [P.sbuf("m_ytmp%d" % i, [128, 256], F32) for i in range(2)]
    for c in range(NCH):
        P.op("dve", lambda e: e.tensor_tensor(out=junk[:, :], in0=Hacc[:, c, :], in1=Hacc[:, c, :], op=ALU.mult), r=[Hacc.k(c)], w=[junk.k()])
        P.op("dve", lambda e: e.reduce_sum(out=ssq[:, c:c + 1], in_=junk[:, :], axis=AX.X), r=[junk.k()], w=[ssq.k(c)])
    P.op("act", lambda e: e.activation(out=ssq[:, :], in_=ssq[:, :], func=AF.Sqrt, scale=1.0 / 256, bias=1e-6), r=[ssq.k()], w=[ssq.k()])
    P.op("dve", lambda e: e.reciprocal(out=ssq[:, :], in_=ssq[:, :]), r=[ssq.k()], w=[ssq.k()])
    for c in range(NCH):
        yt, y2 = ytmp[c % 2], yo[c % 2]
        P.op("dve", lambda e: e.scalar_tensor_tensor(out=yt[:, :], in0=Hacc[:, c, :], scalar=ssq[:, c:c + 1], in1=normg[:, :],
                                                      op0=ALU.mult, op1=ALU.mult), r=[Hacc.k(c), ssq.k(), normg.k()], w=[yt.k()])
        P.op("dve", lambda e: e.tensor_tensor(out=y2[:, :], in0=yt[:, :], in1=og[:, c, :], op=ALU.mult), r=[yt.k(), og.k(c)], w=[y2.k()])
        P.dma("pool", io["yc"][c * 128:(c + 1) * 128, :], y2[:, :], r=[y2.k()], w=[io["yc"].k(c)])
    P.pop()


RW_GN_EPS = 64e-5
C64 = 64


def seg_chunks(T, n_ctx, n=256):
    out = []
    for (a, b) in ((0, n_ctx), (n_ctx, T)):
        t = a
        while t < b:
            m = min(n, b - t)
            out.append((t, m, a, b))
            t += m
    return out


def rwkv_program(P, T, n_ctx, io, tag=""):
    NC64 = T // C64
    nctx_c = n_ctx // C64
    hT = io["hT"]
    R = P.sbuf(tag + "R", [128, T], F32)
    A = P.sbuf(tag + "A", [128, T], F32)
    KD = [P.sbuf(tag + "KD%d" % d, [128, T], F32) for d in range(2)]
    BB = [P.sbuf(tag + "BB%d" % d, [128, T], F32) for d in range(2)]
    LW = [P.sbuf(tag + "LW%d" % d, [128, T], F32) for d in range(2)]
    vtm = P.sbuf(tag + "vtm", [128, NC64, 64], F32)
    bon = P.sbuf(tag + "bon", [128, NC64], F32)
    ident = P.sbuf(tag + "ident", [128, 128], F32)
    id64 = P.sbuf(tag + "id64", [128, 64], F32)
    onesc = P.sbuf(tag + "onesc", [128, 64], F32)
    pcol = P.sbuf(tag + "pcol", [128, 8], F32)
    P.dma("sp", ident[:], io["ident"][:, :], w=[ident.k()])
    P.dma("sp", id64[:], io["id64"][:, :], w=[id64.k()])
    P.dma("sp", onesc[:], io["onesc"][:, :], w=[onesc.k()])
    P.dma("sp", pcol[:], io["pcol"][:, :], w=[pcol.k()])

    P.push()
    w_fm = P.sbuf(tag + "wfm", [128, KC, 640], BF16)
    stage = [P.sbuf(tag + "wst%d" % i, [128, KC, 64], F32) for i in range(2)]
    hTc = [P.sbuf(tag + "hTc0", [128, KC, 258], BF16)] * 2
    mu = P.sbuf(tag + "mu", [128, 16], F32)
    wup = P.sbuf(tag + "wup", [128, 128], F32)
    aup = P.sbuf(tag + "aup", [128, 128], F32)
    bones = P.sbuf(tag + "bones", [128, 128], F32)
    tmpn = {}
    for nm in ("K", "Vf", "WD", "AD", "tw", "as0", "as1", "kk0", "sq", "rn", "kka", "t1", "rkr"):
        tmpn[nm] = P.sbuf(tag + "t_" + nm, [128, 256], F32)
    ps = [P.psum(tag + "pp%d" % i, [128, 512], F32) for i in range(8)]
    P.dma("sp", mu[:, 0:10], io["mu"][:, :], w=[mu.k()])
    P.dma("sp", wup[:], io["wup"][:, :], w=[wup.k()])
    P.dma("sp", aup[:], io["aup"][:, :], w=[aup.k()])
    P.dma("sp", bones[:], io["bones"][:, :], w=[bones.k()])
    load_w_bf16(P, io["w_fm"], w_fm, 640, stage, "fm", blk=64)
    P.op("dve", lambda e: e.tensor_tensor(out=mu[:, 10:15], in0=mu[:, 0:5], in1=mu[:, 5:10], op=ALU.add), r=[mu.k()], w=[mu.k()])
    P.op("dve", lambda e: e.tensor_scalar(out=mu[:, 10:15], in0=mu[:, 10:15], scalar1=-1.0, scalar2=1.0, op0=ALU.mult, op1=ALU.add), r=[mu.k()], w=[mu.k()])

    for ci, (t0, n, sa, sb) in enumerate(seg_chunks(T, n_ctx)):
        lo, hi = max(sa, t0 - 1), min(sb, t0 + n + 1)
        nn = hi - lo
        o = t0 - lo
        hc = hTc[ci % 2]
        P.dma("sp", hc[:, :, 0:nn], hT[:, :, lo:hi].rearrange("k p t -> p k t"), w=[hc.k()])
        dsts = [R[:, t0:t0 + n], tmpn["K"][:, 0:n], tmpn["Vf"][:, 0:n], tmpn["WD"][:, 0:n], tmpn["AD"][:, 0:n]]
        dkeys = [R.k(ci), tmpn["K"].k(), tmpn["Vf"].k(), tmpn["WD"].k(), tmpn["AD"].k()]
        for bi in range(5):
            pp = ps[bi % 2]
            for kc in range(KC):
                P.op("pe", lambda e: e.matmul(pp[:, 0:nn], lhsT=w_fm[:, kc, bi * 128:(bi + 1) * 128], rhs=hc[:, kc, 0:nn],
                                               start=(kc == 0), stop=(kc == KC - 1)), r=[w_fm.k(), hc.k()], w=[pp.k()])
            dst, dk = dsts[bi], dkeys[bi]
            P.op("act", lambda e: e.activation(out=dst, in_=pp[:, o:o + n], func=AF.Copy, scale=mu[:, 10 + bi:11 + bi]), r=[pp.k(), mu.k()], w=[dk])
            j0 = 0 if o == 1 else 1
            P.op("dve", lambda e: e.scalar_tensor_tensor(out=dst[:, j0:n], in0=pp[:, o + j0 - 1:o + n - 1], scalar=mu[:, bi:bi + 1], in1=dst[:, j0:n],
                                                          op0=ALU.mult, op1=ALU.add), r=[pp.k(), mu.k(), dk], w=[dk])
            j1 = n if (o + n + 1 <= nn) else n - 1
            P.op("dve", lambda e: e.scalar_tensor_tensor(out=dst[:, 0:j1], in0=pp[:, o + 1:o + 1 + j1], scalar=mu[:, 5 + bi:6 + bi], in1=dst[:, 0:j1],
                                                          op0=ALU.mult, op1=ALU.add), r=[pp.k(), mu.k(), dk], w=[dk])
        K, Vf, WD, AD = tmpn["K"], tmpn["Vf"], tmpn["WD"], tmpn["AD"]
        tw, kk0, sq, rn, kka, t1, rkr = (tmpn[x] for x in ("tw", "kk0", "sq", "rn", "kka", "t1", "rkr"))
        asg = [tmpn["as0"], tmpn["as1"]]
        P.op("act", lambda e: e.activation(out=tw[:, 0:n], in_=WD[:, 0:n], func=AF.Tanh), r=[WD.k()], w=[tw.k()])
        for d in range(2):
            rows = slice(64 * d, 64 * d + 64)
            px = ps[2 + d]
            P.op("pe", lambda e: e.matmul(px[:, 0:n], lhsT=wup[rows, :], rhs=tw[rows, 0:n], start=True, stop=True), r=[wup.k(), tw.k()], w=[px.k()])
            P.op("act", lambda e: e.activation(out=LW[d][:, t0:t0 + n], in_=px[:, 0:n], func=AF.Sigmoid, bias=pcol[:, d:d + 1]), r=[px.k(), pcol.k()], w=[LW[d].k(ci)])
            P.op("dve", lambda e: e.tensor_scalar(out=LW[d][:, t0:t0 + n], in0=LW[d][:, t0:t0 + n], scalar1=-float(np.exp(-0.5)), scalar2=None, op0=ALU.mult),
                 r=[LW[d].k(ci)], w=[LW[d].k(ci)])
            pa = ps[4 + d]
            P.op("pe", lambda e: e.matmul(pa[:, 0:n], lhsT=aup[rows, :], rhs=AD[rows, 0:n], start=True, stop=True), r=[aup.k(), AD.k()], w=[pa.k()])
            P.op("act", lambda e: e.activation(out=asg[d][:, 0:n], in_=pa[:, 0:n], func=AF.Sigmoid, bias=pcol[:, 2 + d:3 + d]), r=[pa.k(), pcol.k()], w=[asg[d].k()])
        P.op("dve", lambda e: e.tensor_scalar(out=kk0[:, 0:n], in0=K[:, 0:n], scalar1=pcol[:, 4:5], scalar2=None, op0=ALU.mult), r=[K.k(), pcol.k()], w=[kk0.k()])
        P.op("dve", lambda e: e.tensor_tensor(out=sq[:, 0:n], in0=kk0[:, 0:n], in1=kk0[:, 0:n], op=ALU.mult), r=[kk0.k()], w=[sq.k()])
        pn = ps[6]
        P.op("pe", lambda e: e.matmul(pn[:, 0:n], lhsT=bones[:, :], rhs=sq[:, 0:n], start=True, stop=True), r=[bones.k(), sq.k()], w=[pn.k()])
        P.op("act", lambda e: e.activation(out=rn[:, 0:n], in_=pn[:, 0:n], func=AF.Sqrt), r=[pn.k()], w=[rn.k()])
        P.op("dve", lambda e: e.tensor_scalar(out=rn[:, 0:n], in0=rn[:, 0:n], scalar1=1e-12, scalar2=None, op0=ALU.max), r=[rn.k()], w=[rn.k()])
        P.op("dve", lambda e: e.reciprocal(out=rn[:, 0:n], in_=rn[:, 0:n]), r=[rn.k()], w=[rn.k()])
        P.op("dve", lambda e: e.scalar_tensor_tensor(out=A[:, t0:t0 + n], in0=kk0[:, 0:n], scalar=-1.0, in1=rn[:, 0:n], op0=ALU.mult, op1=ALU.mult),
             r=[kk0.k(), rn.k()], w=[A.k(ci)])
        P.op("dve", lambda e: e.tensor_scalar(out=kka[:, 0:n], in0=K[:, 0:n], scalar1=pcol[:, 5:6], scalar2=None, op0=ALU.mult), r=[K.k(), pcol.k()], w=[kka.k()])
        for d in range(2):
            P.op("dve", lambda e: e.scalar_tensor_tensor(out=BB[d][:, t0:t0 + n], in0=A[:, t0:t0 + n], scalar=-1.0, in1=asg[d][:, 0:n], op0=ALU.mult, op1=ALU.mult),
                 r=[A.k(ci), asg[d].k()], w=[BB[d].k(ci)])
            P.op("dve", lambda e: e.scalar_tensor_tensor(out=t1[:, 0:n], in0=asg[d][:, 0:n], scalar=-1.0, in1=kka[:, 0:n], op0=ALU.add, op1=ALU.mult),
                 r=[asg[d].k(), kka.k()], w=[t1.k()])
            P.op("dve", lambda e: e.tensor_tensor(out=KD[d][:, t0:t0 + n], in0=t1[:, 0:n], in1=K[:, 0:n], op=ALU.add), r=[t1.k(), K.k()], w=[KD[d].k(ci)])
        P.op("dve", lambda e: e.scalar_tensor_tensor(out=rkr[:, 0:n], in0=R[:, t0:t0 + n], scalar=pcol[:, 6:7], in1=K[:, 0:n], op0=ALU.mult, op1=ALU.mult),
             r=[R.k(ci), pcol.k(), K.k()], w=[rkr.k()])
        pbn = ps[7]
        nq = n // 64
        c0 = t0 // 64
        for hh in range(2):
            rows = slice(64 * hh, 64 * hh + 64)
            for q in range(nq):
                P.op("pe", lambda e: e.matmul(pbn[rows, q:q + 1], lhsT=rkr[rows, q * 64:(q + 1) * 64], rhs=onesc[rows, 0:1], start=True, stop=True),
                     r=[rkr.k(), onesc.k()], w=[pbn.k()])
                P.op("pe", lambda e: e.matmul(pbn[rows, 64 + q * 64:64 + (q + 1) * 64], lhsT=Vf[rows, q * 64:(q + 1) * 64], rhs=ident[rows, 64 * hh:64 * hh + 64],
                                               start=True, stop=True), r=[Vf.k(), ident.k()], w=[pbn.k()])
        P.op("dve", lambda e: e.tensor_copy(out=bon[:, c0:c0 + nq], in_=pbn[:, 0:nq]), r=[pbn.k()], w=[bon.k(ci)])
        P.op("act", lambda e: e.activation(out=vtm[:, c0:c0 + nq, :], in_=pbn[:, 64:64 + nq * 64].rearrange("p (q v) -> p q v", v=64), func=AF.Copy), r=[pbn.k()], w=[vtm.k(ci)])
    P.pop()

    P.push()
    yacc = P.sbuf(tag + "yacc", [128, NC64, 64], F32)
    P.push()
    masks = [P.sbuf(tag + "mask%d" % d, [128, 320], F32) for d in range(2)]
    for d, nm in enumerate(("mask_f", "mask_b")):
        P.dma("sp", masks[d][:], io[nm][:, :], w=[masks[d].k()])
    sts = []
    for d in range(2):
        s = {}
        for nm, w_ in (("G", 64), ("GE", 64), ("pre", 64), ("E1", 64), ("E2", 64), ("E3", 64), ("E4", 64), ("BK", 128), ("BKh", 128),
                       ("PP0", 128), ("PP1", 128), ("Wsb", 64), ("Usb", 64), ("ST", 64)):
            s[nm] = P.sbuf(tag + "s%d_%s" % (d, nm), [128, w_], F32)
        for q in range(2):
            for nm, w_ in (("AR", 128), ("AM", 320), ("Z", 64), ("BKT", 128), ("col", 4)):
                s[nm + str(q)] = P.sbuf(tag + "s%d_%s%d" % (d, nm, q), [128, w_], F32)
        s["ps"] = [P.psum(tag + "sp%d_%d" % (d, i), [128, 512], F32) for i in range(4)]
        P.op("pool", lambda e: e.memset(s["ST"][:, :], 0.0), w=[s["ST"].k()])
        sts.append(s)
    H = [slice(0, 64), slice(64, 128)]

    def indep(d, c, q):
        s = sts[d]
        pA, pL, pW, pT = s["ps"]
        tsl = slice(c * 64, (c + 1) * 64)
        G, GE, pre, E1, E2, E3, E4, BK, BKh = (s[x] for x in ("G", "GE", "pre", "E1", "E2", "E3", "E4", "BK", "BKh"))
        AR, AM, Z, BKT, col = (s[x + str(q)] for x in ("AR", "AM", "Z", "BKT", "col"))
        lw = LW[d][:, tsl]
        if d == 0:
            P.op("dve", lambda e: e.tensor_tensor_scan(out=G[:, :], data0=onesc[:, :], data1=lw, initial=0.0, op0=ALU.mult, op1=ALU.add),
                 r=[onesc.k(), LW[d].k()], w=[G.k()])
            gcol = G[:, 63:64]
        else:
            P.op("dve", lambda e: e.tensor_tensor_scan(out=pre[:, :], data0=onesc[:, :], data1=lw, initial=0.0, op0=ALU.mult, op1=ALU.add),
                 r=[onesc.k(), LW[d].k()], w=[pre.k()])
            P.op("dve", lambda e: e.tensor_tensor(out=G[:, :], in0=lw, in1=pre[:, :], op=ALU.subtract), r=[LW[d].k(), pre.k()], w=[G.k()])
            P.op("dve", lambda e: e.tensor_scalar(out=G[:, :], in0=G[:, :], scalar1=pre[:, 63:64], scalar2=None, op0=ALU.add), r=[G.k(), pre.k()], w=[G.k()])
            gcol = G[:, 0:1]
        P.op("dve", lambda e: e.tensor_tensor(out=GE[:, :], in0=G[:, :], in1=lw, op=ALU.subtract), r=[G.k(), LW[d].k()], w=[GE.k()])
        yield
        P.op("act", lambda e: e.activation(out=E1[:, :], in_=G[:, :], func=AF.Exp), r=[G.k()], w=[E1.k()])
        P.op("act", lambda e: e.activation(out=E2[:, :], in_=G[:, :], func=AF.Exp, scale=-1.0), r=[G.k()], w=[E2.k()])
        P.op("act", lambda e: e.activation(out=E3[:, :], in_=GE[:, :], func=AF.Exp), r=[GE.k()], w=[E3.k()])
        P.op("act", lambda e: e.activation(out=E4[:, :], in_=G[:, :], func=AF.Exp, scale=-1.0, bias=gcol), r=[G.k()], w=[E4.k()])
        P.op("act", lambda e: e.activation(out=col[:, 0:1], in_=gcol, func=AF.Exp), r=[G.k()], w=[col.k()])
        yield
        P.op("dve", lambda e: e.tensor_tensor(out=AR[:, 0:64], in0=A[:, tsl], in1=E3[:, :], op=ALU.mult), r=[A.k(), E3.k()], w=[AR.k(0)])
        P.op("dve", lambda e: e.tensor_tensor(out=AR[:, 64:128], in0=R[:, tsl], in1=E1[:, :], op=ALU.mult), r=[R.k(), E1.k()], w=[AR.k(1)])
        P.op("dve", lambda e: e.tensor_tensor(out=BK[:, 0:64], in0=BB[d][:, tsl], in1=E2[:, :], op=ALU.mult), r=[BB[d].k(), E2.k()], w=[BK.k(0)])
        P.op("dve", lambda e: e.tensor_tensor(out=BK[:, 64:128], in0=KD[d][:, tsl], in1=E2[:, :], op=ALU.mult), r=[KD[d].k(), E2.k()], w=[BK.k(1)])
        P.op("dve", lambda e: e.tensor_tensor(out=BKh[:, 0:64], in0=BB[d][:, tsl], in1=E4[:, :], op=ALU.mult), r=[BB[d].k(), E4.k()], w=[BKh.k(0)])
        P.op("dve", lambda e: e.tensor_tensor(out=BKh[:, 64:128], in0=KD[d][:, tsl], in1=E4[:, :], op=ALU.mult), r=[KD[d].k(), E4.k()], w=[BKh.k(1)])
        yield
        for hh in range(2):
            rw = H[hh]
            P.op("pe", lambda e: e.matmul(pA[rw, 0:128], lhsT=BK[rw, 0:64], rhs=AR[rw, 0:128], start=True, stop=True), r=[BK.k(), AR.k()], w=[pA.k()])
            P.op("pe", lambda e: e.matmul(pA[rw, 128:256], lhsT=BK[rw, 64:128], rhs=AR[rw, 0:128], start=True, stop=True), r=[BK.k(), AR.k()], w=[pA.k()])
            P.op("pe", lambda e: e.matmul(pA[rw, 256:320], lhsT=AR[rw, 0:64], rhs=BK[rw, 0:64], start=True, stop=True), r=[BK.k(), AR.k()], w=[pA.k()])
            P.op("pe", lambda e: e.matmul(pT[rw, 0:64], lhsT=BKh[rw, 0:64], rhs=ident[rw, 64 * hh:64 * hh + 64], start=True, stop=True), r=[BKh.k(), ident.k()], w=[pT.k()])
            P.op("pe", lambda e: e.matmul(pT[rw, 64:128], lhsT=BKh[rw, 64:128], rhs=ident[rw, 64 * hh:64 * hh + 64], start=True, stop=True), r=[BKh.k(), ident.k()], w=[pT.k()])
        yield
        P.op("dve", lambda e: e.tensor_tensor(out=AM[:, :], in0=pA[:, 0:320], in1=masks[d][:, :], op=ALU.mult), r=[pA.k(), masks[d].k()], w=[AM.k()])
        P.op("act", lambda e: e.activation(out=BKT[:, :], in_=pT[:, 0:128], func=AF.Copy), r=[pT.k()], w=[BKT.k()])
        P.op("dve", lambda e: e.tensor_tensor(out=Z[:, :], in0=AM[:, 0:64], in1=id64[:, :], op=ALU.add), r=[AM.k(), id64.k()], w=[Z.k()])
        yield
        Pc, PTc = AM[:, 0:64], AM[:, 256:320]
        Pk = AM.k()
        for lvl in range(1, 6):
            PPn = s["PP%d" % (lvl % 2)]
            for hh in range(2):
                rw = H[hh]
                P.op("pe", lambda e: e.matmul(pL[rw, 0:64], lhsT=Pc[rw, :], rhs=PTc[rw, :], start=True, stop=True), r=[Pk], w=[pL.k()])
                if lvl < 5:
                    P.op("pe", lambda e: e.matmul(pL[rw, 64:128], lhsT=PTc[rw, :], rhs=Pc[rw, :], start=True, stop=True), r=[Pk], w=[pL.k()])
            yield
            wcols = 128 if lvl < 5 else 64
            P.op("act", lambda e: e.activation(out=PPn[:, 0:wcols], in_=pL[:, 0:wcols], func=AF.Copy), r=[pL.k()], w=[PPn.k()])
            yield
            PTc, Pc, Pk = PPn[:, 0:64], PPn[:, 64:128], PPn.k()
            for hh in range(2):
                rw = H[hh]
                P.op("pe", lambda e: e.matmul(pL[rw, 128:192], lhsT=PTc[rw, :], rhs=Z[rw, :], start=True, stop=True), r=[Pk, Z.k()], w=[pL.k()])
            yield
            P.op("dve", lambda e: e.tensor_tensor(out=Z[:, :], in0=Z[:, :], in1=pL[:, 128:192], op=ALU.add), r=[Z.k(), pL.k()], w=[Z.k()])
            yield

    def dep(d, c, q, first):
        s = sts[d]
        pA, pL, pW, pT = s["ps"]
        Wsb, Usb, ST = s["Wsb"], s["Usb"], s["ST"]
        AR, AM, Z, BKT, col = (s[x + str(q)] for x in ("AR", "AM", "Z", "BKT", "col"))
        for hh in range(2):
            rw = H[hh]
            P.op("pe", lambda e: e.matmul(pW[rw, 0:64], lhsT=AR[rw, 0:64], rhs=ST[rw, :], start=True, stop=False), r=[AR.k(), ST.k()], w=[pW.k()])
            P.op("pe", lambda e: e.matmul(pW[rw, 0:64], lhsT=AM[rw, 128:192], rhs=vtm[rw, c, :], start=False, stop=True), r=[AM.k(), vtm.k()], w=[pW.k()])
        yield
        P.op("act", lambda e: e.activation(out=Wsb[:, :], in_=pW[:, 0:64], func=AF.Copy), r=[pW.k()], w=[Wsb.k()])
        yield
        for hh in range(2):
            rw = H[hh]
            P.op("pe", lambda e: e.matmul(pW[rw, 64:128], lhsT=Z[rw, :], rhs=Wsb[rw, :], start=True, stop=True), r=[Z.k(), Wsb.k()], w=[pW.k()])
        yield
        P.op("act", lambda e: e.activation(out=Usb[:, :], in_=pW[:, 64:128], func=AF.Copy), r=[pW.k()], w=[Usb.k()])
        yield
        for hh in range(2):
            rw = H[hh]
            P.op("pe", lambda e: e.matmul(pW[rw, 192:256], lhsT=BKT[rw, 0:64], rhs=Usb[rw, :], start=True, stop=False), r=[BKT.k(), Usb.k()], w=[pW.k()])
            P.op("pe", lambda e: e.matmul(pW[rw, 192:256], lhsT=BKT[rw, 64:128], rhs=vtm[rw, c, :], start=False, stop=True), r=[BKT.k(), vtm.k()], w=[pW.k()])
        for hh in range(2):
            rw = H[hh]
            P.op("pe", lambda e: e.matmul(pW[rw, 128:192], lhsT=AR[rw, 64:128], rhs=ST[rw, :], start=True, stop=False), r=[AR.k(), ST.k()], w=[pW.k()])
            P.op("pe", lambda e: e.matmul(pW[rw, 128:192], lhsT=AM[rw, 64:128], rhs=Usb[rw, :], start=False, stop=False), r=[AM.k(), Usb.k()], w=[pW.k()])
            P.op("pe", lambda e: e.matmul(pW[rw, 128:192], lhsT=AM[rw, 192:256], rhs=vtm[rw, c, :], start=False, stop=True), r=[AM.k(), vtm.k()], w=[pW.k()])
        yield
        P.op("dve", lambda e: e.scalar_tensor_tensor(out=ST[:, :], in0=ST[:, :], scalar=col[:, 0:1], in1=pW[:, 192:256], op0=ALU.mult, op1=ALU.add),
             r=[ST.k(), col.k(), pW.k()], w=[ST.k()])
        if first:
            P.op("dve", lambda e: e.tensor_copy(out=yacc[:, c, :], in_=pW[:, 128:192]), r=[pW.k()], w=[yacc.k(c)])
        else:
            P.op("dve", lambda e: e.tensor_tensor(out=yacc[:, c, :], in0=yacc[:, c, :], in1=pW[:, 128:192], op=ALU.add), r=[pW.k(), yacc.k(c)], w=[yacc.k(c)])
        yield

    def run_interleaved(gens):
        gens = list(gens)
        while gens:
            for g in list(gens):
                try:
                    next(g)
                except StopIteration:
                    gens.remove(g)

    order_f = list(range(NC64))
    order_b = list(range(nctx_c - 1, -1, -1)) + list(range(NC64 - 1, nctx_c - 1, -1))
    orders = (order_f, order_b)
    done = set()
    run_interleaved([indep(d, orders[d][0], 0) for d in range(2)])
    for i in range(NC64):
        gens = []
        for d in range(2):
            c = orders[d][i]
            gens.append(dep(d, c, i % 2, c not in done))
            done.add(c)
        if i + 1 < NC64:
            for d in range(2):
                gens.append(indep(d, orders[d][i + 1], (i + 1) % 2))
        run_interleaved(gens)
    P.pop()

    P.push()
    w_g = P.sbuf(tag + "wg", [128, KC, 128], BF16)
    stage = [P.sbuf(tag + "owst%d" % i, [128, KC, 64], F32) for i in range(2)]
    hTc = [P.sbuf(tag + "ohTc0", [128, KC, 256], BF16)] * 2
    lnw = P.sbuf(tag + "lnw", [128, 64], F32)
    lnb = P.sbuf(tag + "lnb", [128, 64], F32)
    gt = P.sbuf(tag + "gt", [128, 4, 64], F32)
    sc = P.sbuf(tag + "sc", [128, 8], F32)
    cen = P.sbuf(tag + "cen", [128, 64], F32)
    sq2 = P.sbuf(tag + "sq2", [128, 64], F32)
    yn = P.sbuf(tag + "yn", [128, 64], F32)
    yo4 = [P.sbuf(tag + "yo4_%d" % i, [128, 4, 64], BF16) for i in range(2)]
    sc4 = P.sbuf(tag + "sc4", [128, 5, 4], F32)
    cen4 = P.sbuf(tag + "cen4", [128, 4, 64], F32)
    sq4 = P.sbuf(tag + "sq4", [128, 4, 64], F32)
    pg = [P.psum(tag + "pg%d" % i, [128, 512], F32) for i in range(2)]
    P.dma("sp", lnw[:], io["lnw"][:, :], w=[lnw.k()])
    P.dma("sp", lnb[:], io["lnb"][:, :], w=[lnb.k()])
    load_w_bf16(P, io["w_g"], w_g, 128, stage, "g", blk=64)
    for ci, (t0, n, sa, sb) in enumerate(seg_chunks(T, n_ctx)):
        hc = hTc[ci % 2]
        P.dma("sp", hc[:, :, 0:n], hT[:, :, t0:t0 + n].rearrange("k p t -> p k t"), w=[hc.k()])
        pgc = pg[ci % 2]
        nq = n // 64
        for hh in range(2):
            rows = slice(64 * hh, 64 * hh + 64)
            for q in range(nq):
                for kc in range(KC):
                    P.op("pe", lambda e: e.matmul(pgc[rows, q * 64:(q + 1) * 64], lhsT=hc[:, kc, q * 64:(q + 1) * 64], rhs=w_g[:, kc, hh * 64:(hh + 1) * 64],
                                                   start=(kc == 0), stop=(kc == KC - 1)), r=[hc.k(), w_g.k()], w=[pgc.k()])
        P.op("act", lambda e: e.activation(out=gt[:, 0:nq, :], in_=pgc[:, 0:nq * 64].rearrange("p (q v) -> p q v", v=64), func=AF.Silu), r=[pgc.k()], w=[gt.k()])
        c0 = t0 // 64
        y4 = yacc[:, c0:c0 + nq, :]
        bshape = [128, nq, 64]
        P.op("dve", lambda e: e.reduce_sum(out=sc4[:, 0, 0:nq], in_=y4, axis=AX.X), r=[yacc.k()], w=[sc4.k(0)])
        P.op("dve", lambda e: e.tensor_scalar(out=sc4[:, 1, 0:nq], in0=sc4[:, 0, 0:nq], scalar1=-1.0 / 64, scalar2=None, op0=ALU.mult), r=[sc4.k(0)], w=[sc4.k(1)])
        P.op("dve", lambda e: e.tensor_tensor(out=cen4[:, 0:nq, :], in0=y4, in1=sc4[:, 1, 0:nq].unsqueeze(2).to_broadcast(bshape), op=ALU.add),
             r=[yacc.k(), sc4.k(1)], w=[cen4.k()])
        P.op("dve", lambda e: e.tensor_tensor(out=sq4[:, 0:nq, :], in0=cen4[:, 0:nq, :], in1=cen4[:, 0:nq, :], op=ALU.mult), r=[cen4.k()], w=[sq4.k()])
        P.op("dve", lambda e: e.reduce_sum(out=sc4[:, 2, 0:nq], in_=sq4[:, 0:nq, :], axis=AX.X), r=[sq4.k()], w=[sc4.k(2)])
        P.op("act", lambda e: e.activation(out=sc4[:, 3, 0:nq], in_=sc4[:, 2, 0:nq], func=AF.Sqrt, scale=1.0 / 64, bias=RW_GN_EPS), r=[sc4.k(2)], w=[sc4.k(3)])
        P.op("dve", lambda e: e.reciprocal(out=sc4[:, 4, 0:nq], in_=sc4[:, 3, 0:nq]), r=[sc4.k(3)], w=[sc4.k(4)])
        P.op("dve", lambda e: e.tensor_tensor(out=cen4[:, 0:nq, :], in0=cen4[:, 0:nq, :], in1=sc4[:, 4, 0:nq].unsqueeze(2).to_broadcast(bshape), op=ALU.mult),
             r=[cen4.k(), sc4.k(4)], w=[cen4.k()])
        P.op("dve", lambda e: e.tensor_tensor(out=cen4[:, 0:nq, :], in0=cen4[:, 0:nq, :], in1=lnw[:, :].unsqueeze(1).to_broadcast(bshape), op=ALU.mult),
             r=[cen4.k(), lnw.k()], w=[cen4.k()])
        P.op("dve", lambda e: e.tensor_tensor(out=cen4[:, 0:nq, :], in0=cen4[:, 0:nq, :], in1=lnb[:, :].unsqueeze(1).to_broadcast(bshape), op=ALU.add),
             r=[cen4.k(), lnb.k()], w=[cen4.k()])
        P.op("dve", lambda e: e.tensor_tensor(out=sq4[:, 0:nq, :], in0=vtm[:, c0:c0 + nq, :], in1=bon[:, c0:c0 + nq].unsqueeze(2).to_broadcast(bshape), op=ALU.mult),
             r=[vtm.k(), bon.k()], w=[sq4.k()])
        P.op("dve", lambda e: e.tensor_tensor(out=cen4[:, 0:nq, :], in0=cen4[:, 0:nq, :], in1=sq4[:, 0:nq, :], op=ALU.add), r=[cen4.k(), sq4.k()], w=[cen4.k()])
        y2 = yo4[ci % 2]
        P.op("dve", lambda e: e.tensor_tensor(out=y2[:, 0:nq, :], in0=cen4[:, 0:nq, :], in1=gt[:, 0:nq, :], op=ALU.mult), r=[cen4.k(), gt.k()], w=[y2.k()])
        for hh in range(2):
            P.dma("pool", io["ya"][t0:t0 + n, hh * 64:(hh + 1) * 64].rearrange("(q p) v -> p q v", p=64), y2[64 * hh:64 * hh + 64, 0:nq, :],
                  r=[y2.k()], w=[io["ya"].k(ci, hh)])
    P.pop()
    P.pop()


KC = 16
EPS = 1e-6


def ca_program(P, NTOK, chunks, io, do_merge, mode):
    modc = P.sbuf("c_modc", [128, KC, 8], F32)
    gsc = P.sbuf("c_gsc", [128, KC, 2], F32)
    ones = P.sbuf("c_ones", [128, 128], F32)
    P.dma("sp", modc[:], io["modc"][:, :, :], w=[modc.k()])
    P.dma("sp", ones[:], io["ones"][:, :], w=[ones.k()])
    for j in range(2):
        P.op("dve", lambda e: e.scalar_tensor_tensor(out=gsc[:, :, j:j + 1], in0=modc[:, :, 2 + j:3 + j], scalar=1.0, in1=modc[:, :, 6:7],
                                                      op0=ALU.add, op1=ALU.mult), r=[modc.k()], w=[gsc.k(j)])
    if do_merge:
        mergedT = P.sbuf("c_mergedT", [128, KC, NTOK], BF16)
        P.push()
        hT = P.sbuf("c_hT", [128, KC, NTOK], BF16)
        yT = P.sbuf("c_yT", [128, 24, NTOK], BF16)
        P.dma("sp", hT[:], io["hT"][:, :, :].rearrange("k p t -> p k t"), w=[hT.k()])
        for q in range(3):
            P.dma("sp", yT[:, q * 8:(q + 1) * 8, :], io["yT"][q * 8:(q + 1) * 8, :, :].rearrange("k p t -> p k t"), w=[yT.k(q)])
        stg = [P.sbuf("c_stg%d" % i, [128, KC, 128], F32) for i in range(2)]
        stb = [P.sbuf("c_stb0", [128, 24, 128], F32)] * 2
        wm = [P.sbuf("c_wm%d" % i, [128, KC, 384], BF16) for i in range(2)]
        wb = [P.sbuf("c_wb%d" % i, [128, 24, 128], BF16) for i in range(2)]
        sig = [P.sbuf("c_sig%d" % i, [128, 512], F32) for i in range(2)]
        macc = P.sbuf("c_macc", [128, 512], F32)
        prod = P.sbuf("c_prod", [128, 512], F32)
        pg = [P.psum("c_pg%d" % i, [128, 512], F32) for i in range(2)]
        pbr = [P.psum("c_pbr%d" % i, [128, 512], F32) for i in range(2)]
        si = 0
        it = 0
        for db in range(KC):
            wmb, wbb = wm[db % 2], wb[db % 2]
            for nb in range(3):
                st = stg[si % 2]
                si += 1
                c0 = nb * 2048 + db * 128
                P.dma("sp", st[:], io["w_merge"][:, :, c0:c0 + 128], w=[st.k()])
                if nb % 2 == 0:
                    P.op("dve", lambda e: e.tensor_copy(out=wmb[:, :, nb * 128:(nb + 1) * 128], in_=st[:]), r=[st.k()], w=[wmb.k(nb)])
                else:
                    P.op("act", lambda e: e.activation(out=wmb[:, :, nb * 128:(nb + 1) * 128], in_=st[:], func=AF.Copy), r=[st.k()], w=[wmb.k(nb)])
            sb_ = stb[db % 2]
            P.dma("sp", sb_[:], io["w_branch"][:, :, db * 128:(db + 1) * 128], w=[sb_.k()])
            P.op("act", lambda e: e.activation(out=wbb[:], in_=sb_[:], func=AF.Copy), r=[sb_.k()], w=[wbb.k()])
            for (t0, n, isc) in chunks:
                for nb in range(3):
                    g_, b_ = pg[it % 2], pbr[it % 2]
                    sg = sig[it % 2]
                    it += 1
                    for kc in range(KC):
                        P.op("pe", lambda e: e.matmul(g_[:, 0:n], lhsT=wmb[:, kc, nb * 128:(nb + 1) * 128], rhs=hT[:, kc, t0:t0 + n],
                                                       start=(kc == 0), stop=(kc == KC - 1)), r=[wmb.k(nb), hT.k()], w=[g_.k()])
                    for cc in range(8):
                        P.op("pe", lambda e: e.matmul(b_[:, 0:n], lhsT=wbb[:, nb * 8 + cc, :], rhs=yT[:, nb * 8 + cc, t0:t0 + n],
                                                       start=(cc == 0), stop=(cc == 7)), r=[wbb.k(), yT.k(nb)], w=[b_.k()])
                    P.op("act", lambda e: e.activation(out=sg[:, 0:n], in_=g_[:, 0:n], func=AF.Sigmoid), r=[g_.k()], w=[sg.k()])
                    if nb == 0:
                        P.op("dve", lambda e: e.tensor_tensor(out=macc[:, 0:n], in0=sg[:, 0:n], in1=b_[:, 0:n], op=ALU.mult), r=[sg.k(), b_.k()], w=[macc.k()])
                    else:
                        P.op("dve", lambda e: e.tensor_tensor(out=prod[:, 0:n], in0=sg[:, 0:n], in1=b_[:, 0:n], op=ALU.mult), r=[sg.k(), b_.k()], w=[prod.k()])
                        if nb == 1:
                            P.op("dve", lambda e: e.tensor_tensor(out=macc[:, 0:n], in0=macc[:, 0:n], in1=prod[:, 0:n], op=ALU.add), r=[macc.k(), prod.k()], w=[macc.k()])
                        else:
                            P.op("dve", lambda e: e.tensor_tensor(out=mergedT[:, db, t0:t0 + n], in0=macc[:, 0:n], in1=prod[:, 0:n], op=ALU.add),
                                 r=[macc.k(), prod.k()], w=[mergedT.k(db, t0)])
        P.pop()

    P.push()
    znew = P.sbuf("c_znew", [128, KC, NTOK], F32)
    sq = [P.sbuf("c_sq%d" % i, [128, 512], F32) for i in range(2)]
    pss = [P.psum("c_pss%d" % i, [128, 512], F32) for i in range(len(chunks))]
    if do_merge:
        ze = [P.sbuf("c_ze%d" % i, [128, NTOK], F32) for i in range(2)]
        sto = [P.sbuf("c_sto%d" % i, [128, KC, 128], F32) for i in range(2)]
        wo = [P.sbuf("c_wo%d" % i, [128, KC, 128], BF16) for i in range(2)]
        po = [P.psum("c_po%d" % i, [128, 512], F32) for i in range(2)]
    it = 0
    for eb in range(KC):
        if do_merge:
            st, wob, z_e = sto[eb % 2], wo[eb % 2], ze[eb % 2]
            P.dma("sp", st[:], io["w_out"][:, :, eb * 128:(eb + 1) * 128], w=[st.k()])
            if eb % 2 == 0:
                P.op("act", lambda e: e.activation(out=wob[:], in_=st[:], func=AF.Copy), r=[st.k()], w=[wob.k()])
            else:
                P.op("dve", lambda e: e.tensor_copy(out=wob[:], in_=st[:]), r=[st.k()], w=[wob.k()])
            P.dma("sp", z_e[:], io["zT"][eb, :, :], w=[z_e.k()])
        else:
            P.dma("sp", znew[:, eb, :], io["zT"][eb, :, :], w=[znew.k(eb)])
        for ci, (t0, n, isc) in enumerate(chunks):
            if do_merge:
                p_ = po[it % 2]
                for db in range(KC):
                    P.op("pe", lambda e: e.matmul(p_[:, 0:n], lhsT=wob[:, db, :], rhs=mergedT[:, db, t0:t0 + n], start=(db == 0), stop=(db == KC - 1)),
                         r=[wob.k(), mergedT.k()], w=[p_.k()])
                gcol = modc[:, eb, 0:1] if isc else modc[:, eb, 1:2]
                P.op("dve", lambda e: e.scalar_tensor_tensor(out=znew[:, eb, t0:t0 + n], in0=p_[:, 0:n], scalar=gcol, in1=z_e[:, t0:t0 + n],
                                                              op0=ALU.mult, op1=ALU.add), r=[p_.k(), modc.k(), z_e.k()], w=[znew.k(eb, t0)])
            s_ = sq[it % 2]
            it += 1
            P.op("act", lambda e: e.activation(out=s_[:, 0:n], in_=znew[:, eb, t0:t0 + n], func=AF.Square), r=[znew.k(eb)], w=[s_.k()])
            P.op("pe", lambda e: e.matmul(pss[ci][:, 0:n], lhsT=ones[:, :], rhs=s_[:, 0:n], start=(eb == 0), stop=(eb == KC - 1)),
                 r=[ones.k(), s_.k()], w=[pss[ci].k()])
        if do_merge and mode == "mod":
            P.dma("pool", io["zTn"][eb, :, :], znew[:, eb, :], r=[znew.k(eb)], w=[io["zTn"].k(eb)])
    rstd = P.sbuf("c_rstd", [128, NTOK], F32)
    for ci, (t0, n, isc) in enumerate(chunks):
        P.op("act", lambda e: e.activation(out=rstd[:, t0:t0 + n], in_=pss[ci][:, 0:n], func=AF.Sqrt, scale=1.0 / 2048, bias=EPS), r=[pss[ci].k()], w=[rstd.k(ci)])
        P.op("dve", lambda e: e.reciprocal(out=rstd[:, t0:t0 + n], in_=rstd[:, t0:t0 + n]), r=[rstd.k(ci)], w=[rstd.k(ci)])
    odt = BF16 if mode == "mod" else F32
    hn = [P.sbuf("c_hn%d" % i, [128, NTOK], F32) for i in range(2)]
    ho = [P.sbuf("c_ho%d" % i, [128, NTOK], odt) for i in range(2)]
    oname = "hTn" if mode == "mod" else "oT"
    for eb in range(KC):
        h1, h2 = hn[eb % 2], ho[eb % 2]
        for ci, (t0, n, isc) in enumerate(chunks):
            j = 0 if isc else 1
            P.op("dve", lambda e: e.scalar_tensor_tensor(out=h1[:, t0:t0 + n], in0=znew[:, eb, t0:t0 + n], scalar=gsc[:, eb, j:j + 1], in1=rstd[:, t0:t0 + n],
                                                          op0=ALU.mult, op1=ALU.mult), r=[znew.k(eb), gsc.k(), rstd.k(ci)], w=[h1.k(ci)])
            P.op("dve", lambda e: e.tensor_scalar(out=h2[:, t0:t0 + n], in0=h1[:, t0:t0 + n], scalar1=modc[:, eb, 4 + j:5 + j], scalar2=None, op0=ALU.add),
                 r=[h1.k(ci), modc.k()], w=[h2.k(ci)])
        P.dma("pool", io[oname][eb, :, :], h2[:, :], r=[h2.k()], w=[io[oname].k(eb)])
    P.pop()


KC = 16


def mod_program(P, io):
    cT = P.sbuf("mm_cT", [128, KC, 3], F32)
    sT = P.sbuf("mm_sT", [128, KC, 3], F32)
    wa = [P.sbuf("mm_wa%d" % l, [128, KC, 768], F32) for l in range(2)]
    ba = P.sbuf("mm_ba", [3, 2, 768], F32)
    mo = P.sbuf("mm_mo", [3, 2, 768], F32)
    pm = [P.psum("mm_p%d" % i, [128, 512], F32) for i in range(2)]
    P.dma("sp", cT[:], io["cT"][:, :, :], w=[cT.k()])
    for l in range(2):
        P.dma("sp", wa[l][:], io["wa"][l, :, :, :], w=[wa[l].k()])
        P.dma("sp", ba[:, l, :], io["ba"][l, :, :], w=[ba.k(l)])
    P.op("act", lambda e: e.activation(out=sT[:], in_=cT[:], func=AF.Silu), r=[cT.k()], w=[sT.k()])
    it = 0
    for l in range(2):
        for hf in range(2):
            p_ = pm[it % 2]
            it += 1
            for kc in range(KC):
                P.op("pe", lambda e: e.matmul(p_[0:3, 0:384], lhsT=sT[:, kc, 0:3], rhs=wa[l][:, kc, hf * 384:(hf + 1) * 384],
                                               start=(kc == 0), stop=(kc == KC - 1)), r=[sT.k(), wa[l].k()], w=[p_.k()])
            P.op("dve", lambda e: e.tensor_tensor(out=mo[:, l, hf * 384:(hf + 1) * 384], in0=p_[0:3, 0:384], in1=ba[:, l, hf * 384:(hf + 1) * 384], op=ALU.add),
                 r=[p_.k(), ba.k(l)], w=[mo.k(l, hf)])
        P.dma("pool", io["mod"][l, :, :], mo[:, l, :], r=[mo.k(l)], w=[io["mod"].k(l)])

import ml_dtypes
from concourse.bass_utils import run_bass_kernel_spmd

NCORE = 8
DM = 2048
N_CTX = 256
N_LAT = 4096
TT = N_CTX + N_LAT
NTOK = TT // 4
CA_CHUNKS = [(0, 256, True), (256, 512, False), (768, 320, False)]
NPBF = ml_dtypes.bfloat16


def _wl(wc):
    return np.ascontiguousarray(wc.reshape(-1, 128, wc.shape[1]).transpose(1, 0, 2))


def _fm(a):
    return np.ascontiguousarray(a.T.reshape(-1, 128, a.shape[0]))


def _partner(d):
    return d + 32 if (d % 64) < 32 else d - 32


def _rope_tables(T, n_ctx):
    n_lat = T - n_ctx
    rows = n_lat // 64
    row = np.repeat(np.arange(rows), 64).astype(np.float32)
    col = np.tile(np.arange(64), rows).astype(np.float32)
    inv_freq = (np.float32(10000.0) ** (-np.arange(0, 64, 2, dtype=np.float32) / np.float32(64))).astype(np.float32)
    ang_lat = np.stack([row[:, None] * inv_freq, col[:, None] * inv_freq], axis=1)
    ang = np.concatenate([np.zeros((n_ctx, 2, 32), np.float32), ang_lat], axis=0)
    cos, sin = np.cos(ang).astype(np.float32), np.sin(ang).astype(np.float32)
    cosT = np.zeros((128, T), np.float32)
    sinT = np.zeros((128, T), np.float32)
    for d in range(128):
        a, half, i = d // 64, (d % 64) // 32, d % 32
        cosT[d] = cos[:, a, i]
        sinT[d] = sin[:, a, i] * (-1.0 if half == 0 else 1.0)
    return cosT, sinT


def _consts():
    c = {}
    pm = np.zeros((128, 128), np.float32)
    for d in range(128):
        pm[_partner(d), d] = 1.0
    c["pm"] = pm
    c["ones"] = np.ones((128, 128), np.float32)
    u = np.arange(128)
    c["tri_f"] = (u[:, None] <= u[None, :]).astype(np.float32)
    c["tri_b"] = (u[:, None] >= u[None, :]).astype(np.float32)
    c["mneg_f"] = np.where(u[:, None] <= u[None, :], 0.0, -30000.0).astype(np.float32)
    c["mneg_b"] = np.where(u[:, None] >= u[None, :], 0.0, -30000.0).astype(np.float32)
    i = np.arange(64)
    sT_f = (i[:, None] < i[None, :]).astype(np.float32)
    iT_f = (i[:, None] <= i[None, :]).astype(np.float32)
    st_f = (i[None, :] < i[:, None]).astype(np.float32)
    sT_b = (i[:, None] > i[None, :]).astype(np.float32)
    iT_b = (i[:, None] >= i[None, :]).astype(np.float32)
    st_b = (i[None, :] > i[:, None]).astype(np.float32)
    c["mask_f"] = np.tile(np.concatenate([sT_f, iT_f, sT_f, iT_f, st_f], axis=1), (2, 1))
    c["mask_b"] = np.tile(np.concatenate([sT_b, iT_b, sT_b, iT_b, st_b], axis=1), (2, 1))
    bones = np.zeros((128, 128), np.float32)
    bones[:64, :64] = 1
    bones[64:, 64:] = 1
    c["bones"] = bones
    c["ident"] = np.eye(128, dtype=np.float32)
    c["id64"] = np.tile(np.eye(64, dtype=np.float32), (2, 1))
    c["onesc"] = np.ones((128, 64), np.float32)
    c["cosT"], c["sinT"] = _rope_tables(TT, N_CTX)
    return c


_PROGS = {}


def _declare(P, specs):
    io = {}
    for name, (shape, dt, kind) in specs.items():
        io[name] = P.dram(name, list(shape), dt, kind=kind)
    return io


B_IN = {
    "hT": ([16, 128, TT], BF16),
    "g_w_fm": ([128, 16, 384], F32), "g_w_tm": ([128, 16, 384], F32), "cosT": ([128, TT], F32), "sinT": ([128, TT], F32),
    "pm": ([128, 128], F32), "ones": ([128, 128], F32), "gvec": ([128, 4], F32),
    "m_w_fm": ([128, 16, 256], F32), "m_w_tm": ([128, 16, 900], F32), "gbias": ([128, 4], F32), "normg": ([128, 256], F32),
    "tri_f": ([128, 128], F32), "tri_b": ([128, 128], F32), "mneg_f": ([128, 128], F32), "mneg_b": ([128, 128], F32),
    "ident": ([128, 128], F32), "id64": ([128, 64], F32), "bones": ([128, 128], F32), "onesc": ([128, 64], F32),
    "mask_f": ([128, 320], F32), "mask_b": ([128, 320], F32),
}
for _p in range(2):
    B_IN.update({"r%d_w_fm" % _p: ([128, 16, 640], F32), "r%d_w_g" % _p: ([128, 16, 128], F32), "r%d_mu" % _p: ([128, 10], F32),
                 "r%d_wup" % _p: ([128, 128], F32), "r%d_aup" % _p: ([128, 128], F32), "r%d_pcol" % _p: ([128, 8], F32),
                 "r%d_lnw" % _p: ([128, 64], F32), "r%d_lnb" % _p: ([128, 64], F32)})
B_OUT = {"yb": ([TT, 256], BF16), "yc": ([TT, 256], BF16), "ya0": ([TT, 128], BF16), "ya1": ([TT, 128], BF16)}


def _prog_B():
    if "B" in _PROGS:
        return _PROGS["B"]
    nc = bass.Bass("TRN2", target_bir_lowering=False)
    P = Prog(nc)
    specs = {k: (v[0], v[1], "ExternalInput") for k, v in B_IN.items()}
    specs.update({k: (v[0], v[1], "ExternalOutput") for k, v in B_OUT.items()})
    io = _declare(P, specs)
    P.push()
    gqa_program(P, TT, N_CTX, {"hT": io["hT"], "w_fm": io["g_w_fm"], "w_tm": io["g_w_tm"], "cosT": io["cosT"], "sinT": io["sinT"],
                               "pm": io["pm"], "ones": io["ones"], "gvec": io["gvec"], "yb": io["yb"]})
    P.pop()
    P.push()
    mlstm_program(P, TT, N_CTX, {"hT": io["hT"], "w_fm": io["m_w_fm"], "w_tm": io["m_w_tm"], "gbias": io["gbias"], "normg": io["normg"],
                                 "tri_f": io["tri_f"], "tri_b": io["tri_b"], "mneg_f": io["mneg_f"], "mneg_b": io["mneg_b"], "yc": io["yc"]})
    P.pop()
    for p in range(2):
        P.push()
        d = {"hT": io["hT"], "ya": io["ya%d" % p]}
        for k in ("ident", "id64", "bones", "onesc", "mask_f", "mask_b"):
            d[k] = io[k]
        for k in ("w_fm", "w_g", "mu", "wup", "aup", "pcol", "lnw", "lnb"):
            d[k] = io["r%d_%s" % (p, k)]
        rwkv_program(P, TT, N_CTX, d, tag="r%d_" % p)
        P.pop()
    P.finish()
    P.close()
    _PROGS["B"] = nc
    return nc


def _prog_CA(do_merge, mode):
    key = ("CA", do_merge, mode)
    if key in _PROGS:
        return _PROGS[key]
    nc = bass.Bass("TRN2", target_bir_lowering=False)
    P = Prog(nc)
    specs = {"zT": ([16, 128, NTOK], F32, "ExternalInput"), "modc": ([128, 16, 8], F32, "ExternalInput"), "ones": ([128, 128], F32, "ExternalInput")}
    if do_merge:
        specs.update({"hT": ([16, 128, NTOK], BF16, "ExternalInput"), "yT": ([24, 128, NTOK], BF16, "ExternalInput"),
                      "w_merge": ([128, 16, 6144], F32, "ExternalInput"), "w_branch": ([128, 24, 2048], F32, "ExternalInput"),
                      "w_out": ([128, 16, 2048], F32, "ExternalInput")})
    if mode == "mod":
        specs["hTn"] = ([16, 128, NTOK], BF16, "ExternalOutput")
        if do_merge:
            specs["zTn"] = ([16, 128, NTOK], F32, "ExternalOutput")
    else:
        specs["oT"] = ([16, 128, NTOK], F32, "ExternalOutput")
    io = _declare(P, specs)
    ca_program(P, NTOK, CA_CHUNKS, io, do_merge, mode)
    P.finish()
    P.close()
    _PROGS[key] = nc
    return nc


def _prog_M():
    if "M" in _PROGS:
        return _PROGS["M"]
    nc = bass.Bass("TRN2", target_bir_lowering=False)
    P = Prog(nc)
    io = _declare(P, {"cT": ([128, 16, 3], F32, "ExternalInput"), "wa": ([2, 128, 16, 768], F32, "ExternalInput"),
                      "ba": ([2, 3, 768], F32, "ExternalInput"), "mod": ([2, 3, 768], F32, "ExternalOutput")})
    mod_program(P, io)
    P.finish()
    P.close()
    _PROGS["M"] = nc
    return nc


def _run(nc, in_maps):
    res = run_bass_kernel_spmd(nc, in_maps, core_ids=list(range(NCORE)))
    return res.results


def _b_inputs(l, b, j, hT_full, consts, w_in, I):
    m = {"hT": hT_full[b]}
    for k in ("cosT", "sinT", "pm", "ones", "tri_f", "tri_b", "mneg_f", "mneg_b", "ident", "id64", "bones", "onesc", "mask_f", "mask_b"):
        m[k] = consts[k]
    W = w_in[l]
    kv = j // 2
    m["g_w_fm"] = _wl(np.concatenate([W[:, 4352 + 256 * j:4352 + 256 * j + 256], W[:, 5376 + 128 * kv:5376 + 128 * kv + 128]], axis=1))
    m["g_w_tm"] = _wl(np.concatenate([W[:, 5632 + 128 * kv:5632 + 128 * kv + 128], W[:, 5888 + 256 * j:5888 + 256 * j + 256]], axis=1))
    pidx = np.array([_partner(d) for d in range(128)])
    gq, gk = I["at_q_g"][l], I["at_k_g"][l]
    m["gvec"] = np.ascontiguousarray(np.stack([gq, gq[pidx], gk, gk[pidx]], axis=1).astype(np.float32))
    m["m_w_fm"] = _wl(np.concatenate([W[:, 6912 + 128 * j:6912 + 128 * j + 128], W[:, 7424 + 128 * j:7424 + 128 * j + 128]], axis=1))
    gcols = [9984 + 4 * t + j for t in range(4)]
    m["m_w_tm"] = _wl(np.concatenate([W[:, 7424 + 128 * j:7424 + 128 * j + 128], W[:, 7936 + 256 * j:7936 + 256 * j + 256], W[:, gcols],
                                      W[:, 8960 + 256 * j:8960 + 256 * j + 256], W[:, 10000 + 256 * j:10000 + 256 * j + 256]], axis=1))
    m["gbias"] = np.ascontiguousarray(np.tile(I["ml_gate_b"][l][:, j][None, :], (128, 1)).astype(np.float32))
    m["normg"] = np.ascontiguousarray(np.tile(I["ml_norm_g"][l][256 * j:256 * j + 256][None, :], (128, 1)).astype(np.float32))
    for p in range(2):
        ch0 = 256 * j + 128 * p
        cols = np.concatenate([np.arange(ch0, ch0 + 128), 1024 + np.arange(ch0, ch0 + 128), 2048 + np.arange(ch0, ch0 + 128),
                               np.arange(3072, 3200), np.arange(3200, 3328)])
        m["r%d_w_fm" % p] = _wl(W[:, cols])
        m["r%d_w_g" % p] = _wl(W[:, 3328 + ch0:3328 + ch0 + 128])
        mu = I["shift_mu"][l][:, cols]
        m["r%d_mu" % p] = np.ascontiguousarray(np.concatenate([mu[0].reshape(5, 128).T, mu[1].reshape(5, 128).T], axis=1).astype(np.float32))
        m["r%d_wup" % p] = np.ascontiguousarray(I["rw_w_up"][l][:, :, ch0:ch0 + 128].reshape(128, 128))
        m["r%d_aup" % p] = np.ascontiguousarray(I["rw_a_up"][l][:, :, ch0:ch0 + 128].reshape(128, 128))
        sl = slice(ch0, ch0 + 128)
        m["r%d_pcol" % p] = np.ascontiguousarray(np.stack([I["rw_w0"][l][0][sl], I["rw_w0"][l][1][sl], I["rw_a0"][l][0][sl], I["rw_a0"][l][1][sl],
                                                            I["rw_k_k"][l][sl], I["rw_k_a"][l][sl], I["rw_r_k"][l].reshape(1024)[sl],
                                                            np.zeros(128, np.float32)], axis=1).astype(np.float32))
        m["r%d_lnw" % p] = np.ascontiguousarray(np.repeat(I["rw_ln_w"][l][sl].reshape(2, 1, 64), 64, axis=1).reshape(128, 64))
        m["r%d_lnb" % p] = np.ascontiguousarray(np.repeat(I["rw_ln_b"][l][sl].reshape(2, 1, 64), 64, axis=1).reshape(128, 64))
    return m


def _modc(gt, sc, sh, g, b, j):
    z = np.zeros((3, DM), np.float32)
    gt = z if gt is None else gt
    sc = z if sc is None else sc
    sh = z if sh is None else sh
    rc = 2 if j == 0 else b
    cols = [gt[rc], gt[b], sc[rc], sc[b], sh[rc], sh[b], g, np.zeros(DM, np.float32)]
    out = np.zeros((128, 16, 8), np.float32)
    for i, c in enumerate(cols):
        out[:, :, i] = np.asarray(c, np.float32).reshape(16, 128).T
    return out


def kernel(**I):
    I = {k: np.asarray(v) for k, v in I.items()}
    consts = _consts()
    cores = [(c // 4, c % 4) for c in range(NCORE)]
    cstack = np.stack([I["c"][0], I["c"][1], I["c_ctx"]], axis=0).astype(np.float32)
    cT = np.ascontiguousarray(cstack.T.reshape(16, 128, 3).transpose(1, 0, 2))
    in_maps = []
    for c in range(NCORE):
        cs = slice(768 * c, 768 * (c + 1))
        wa = np.stack([_wl(I["w_ada"][l][:, cs]) for l in range(2)], axis=0)
        ba = np.stack([np.tile(I["b_ada"][l][cs][None, :], (3, 1)) for l in range(2)], axis=0).astype(np.float32)
        in_maps.append({"cT": cT, "wa": np.ascontiguousarray(wa), "ba": np.ascontiguousarray(ba)})
    r = _run(_prog_M(), in_maps)
    mod = np.concatenate([np.asarray(r[c]["mod"]) for c in range(NCORE)], axis=2)
    sh = [mod[l][:, 0:DM] for l in range(2)]
    sc = [mod[l][:, DM:2 * DM] for l in range(2)]
    gt = [mod[l][:, 2 * DM:3 * DM] for l in range(2)]
    zfull = [np.concatenate([I["ctx"][b], I["x"][b]], axis=0) for b in range(2)]
    zT = [_fm(zfull[b][NTOK * j:NTOK * (j + 1)]) for (b, j) in cores]
    in_maps = [{"zT": zT[c], "modc": _modc(None, sc[0], sh[0], I["norm_g"][0], b, j), "ones": consts["ones"]} for c, (b, j) in enumerate(cores)]
    r = _run(_prog_CA(False, "mod"), in_maps)
    hT = [np.asarray(r[c]["hTn"]) for c in range(NCORE)]
    out = None
    for l in range(2):
        hT_full = [np.ascontiguousarray(np.concatenate(hT[4 * b:4 * b + 4], axis=2)) for b in range(2)]
        in_maps = [_b_inputs(l, b, j, hT_full, consts, I["w_in"], I) for (b, j) in cores]
        r = _run(_prog_B(), in_maps)
        yT = []
        for b in range(2):
            y = np.zeros((TT, 3, 1024), NPBF)
            for j in range(4):
                rr = r[4 * b + j]
                for p in range(2):
                    y[:, 0, 256 * j + 128 * p:256 * j + 128 * p + 128] = np.asarray(rr["ya%d" % p])
                y[:, 1, 256 * j:256 * j + 256] = np.asarray(rr["yb"])
                y[:, 2, 256 * j:256 * j + 256] = np.asarray(rr["yc"])
            yT.append(_fm(y.reshape(TT, 3072)))
        last = (l == 1)
        w_merge = _wl(I["w_in"][l][:, 11024:17168])
        w_branch = _wl(I["w_branch"][l].reshape(3072, DM))
        w_out = _wl(I["w_out"][l])
        in_maps = []
        for c, (b, j) in enumerate(cores):
            ts = slice(NTOK * j, NTOK * (j + 1))
            if last:
                mc = _modc(gt[l], None, None, I["final_g"], b, j)
            else:
                mc = _modc(gt[l], sc[l + 1], sh[l + 1], I["norm_g"][l + 1], b, j)
            in_maps.append({"zT": zT[c], "modc": mc, "ones": consts["ones"], "hT": hT[c], "yT": np.ascontiguousarray(yT[b][:, :, ts]),
                            "w_merge": w_merge, "w_branch": w_branch, "w_out": w_out})
        r = _run(_prog_CA(True, "final" if last else "mod"), in_maps)
        if last:
            out = np.zeros((2, N_LAT, DM), np.float32)
            for c, (b, j) in enumerate(cores):
                o = np.asarray(r[c]["oT"]).reshape(DM, NTOK).T
                t0 = NTOK * j
                lo = max(t0, N_CTX)
                out[b, lo - N_CTX:t0 + NTOK - N_CTX] = o[lo - t0:]
        else:
            zT = [np.asarray(r[c]["zTn"]) for c in range(NCORE)]
            hT = [np.asarray(r[c]["hTn"]) for c in range(NCORE)]
    return out
```

```python
import numpy as np
from contextlib import ExitStack
import concourse.bass as bass
import concourse.mybir as mybir

F32 = mybir.dt.float32
BF16 = mybir.dt.bfloat16
AF = mybir.ActivationFunctionType
ALU = mybir.AluOpType
AX = mybir.AxisListType

NS_DMA = 8


class Tile:
    def __init__(self, t, name):
        self.t = t
        self.name = name

    def __getitem__(self, idx):
        return self.t[idx]

    def k(self, *sub):
        return (self.name,) + tuple(sub)


class Prog:
    def __init__(self, nc):
        self.nc = nc
        self.es = ExitStack()
        self.stacks = [self.es]
        self.eng = {"pe": nc.tensor, "act": nc.scalar, "dve": nc.vector, "pool": nc.gpsimd, "sp": nc.sync}
        self.sem = {}
        self.cnt = {}
        for e in ("pe", "act", "dve", "pool"):
            self.sem[e] = self.es.enter_context(nc.semaphore("c_" + e))
            self.cnt[e] = 0
        self.unit = {e: 1 for e in self.sem}
        self.dq_n = {}
        for q in ("sp", "pool", "act"):
            self.dq_n[q] = 0
            for s in range(NS_DMA):
                ch = ("dma", q, s)
                self.sem[ch] = self.es.enter_context(nc.semaphore("d_%s%d" % (q, s)))
                self.cnt[ch] = 0
                self.unit[ch] = 16
        self.seen = {e: {} for e in self.eng}
        self.lastw = {}
        self.readers = {}
        self.nuniq = 0
        self.n_inst = 0

    def sbuf(self, name, shape, dtype=F32):
        name = "s_" + name
        t = self.stacks[-1].enter_context(self.nc.sbuf_tensor(name, list(shape), dtype))
        return Tile(t, name)

    def psum(self, name, shape, dtype=F32):
        name = "p_" + name
        t = self.stacks[-1].enter_context(self.nc.psum_tensor(name, list(shape), dtype))
        return Tile(t, name)

    def dram(self, name, shape, dtype=F32, kind=None):
        if kind is None:
            t = self.nc.dram_tensor(name, list(shape), dtype)
        else:
            t = self.nc.dram_tensor(name, list(shape), dtype, kind=kind)
        return Tile(t.ap(), "D:" + name)

    @staticmethod
    def _overlap(a, b):
        n = min(len(a), len(b))
        return a[:n] == b[:n]

    def _deps(self, rkeys, wkeys):
        deps = set()
        for k in list(rkeys) + list(wkeys):
            d = self.lastw.get(k[0])
            if d:
                for sk, v in d.items():
                    if self._overlap(sk, k):
                        deps.add(v)
        for k in wkeys:
            d = self.readers.get(k[0])
            if d:
                for sk, lst in d.items():
                    if self._overlap(sk, k):
                        deps.update(lst)
        return deps

    def _record(self, rkeys, wkeys, me):
        for k in wkeys:
            d = self.lastw.setdefault(k[0], {})
            for sk in [sk for sk in d if len(sk) >= len(k) and sk[:len(k)] == k]:
                del d[sk]
            d[k] = me
            r = self.readers.get(k[0])
            if r:
                for sk in [sk for sk in r if self._overlap(sk, k)]:
                    if len(sk) >= len(k):
                        del r[sk]
        for k in rkeys:
            r = self.readers.setdefault(k[0], {})
            lst = r.setdefault(k, [])
            lst[:] = [x for x in lst if x[0] != me[0]]
            lst.append(me)

    def _wait(self, e, deps, skip_same=None):
        eng = self.eng[e]
        seen = self.seen[e]
        best = {}
        for ch, n in deps:
            if ch == skip_same:
                continue
            if seen.get(ch, 0) >= n:
                continue
            if best.get(ch, 0) < n:
                best[ch] = n
        for ch, n in best.items():
            eng.wait_ge(self.sem[ch], n * self.unit[ch])
            seen[ch] = n
            self.n_inst += 1

    def op(self, e, fn, r=(), w=()):
        if e != "pe":
            w = list(w) + [(k[0],) for k in r if k[0].startswith("p_")]
        deps = self._deps(r, w)
        self._wait(e, deps, skip_same=("pe" if e == "pe" else None))
        ins = fn(self.eng[e])
        self.cnt[e] += 1
        ins.then_inc(self.sem[e], 1)
        self.n_inst += 1
        me = (e, self.cnt[e])
        self._record(r, w, me)
        return ins

    def dma(self, q, out, in_, r=(), w=(), **kw):
        i = self.dq_n[q]
        self.dq_n[q] += 1
        ch = ("dma", q, i % NS_DMA)
        deps = self._deps(r, w)
        if self.cnt[ch] > 0:
            deps.add((ch, self.cnt[ch]))
        self._wait(q, deps)
        ins = self.eng[q].dma_start(out=out, in_=in_, **kw)
        self.cnt[ch] += 1
        ins.then_inc(self.sem[ch], 16)
        self.n_inst += 1
        self._record(r, w, (ch, self.cnt[ch]))
        return ins

    def barrier(self):
        for e in self.eng:
            deps = set()
            for ch, n in self.cnt.items():
                if n > 0:
                    deps.add((ch, n))
            self._wait(e, deps)
        self.lastw.clear()
        self.readers.clear()

    def push(self):
        self.stacks.append(ExitStack())

    def pop(self):
        self.barrier()
        self.stacks.pop().close()

    def finish(self):
        self.barrier()

    def close(self):
        self.es.close()


KC = 16
HD = 128
EPS = 1e-6


def chunks_of(T, n_ctx):
    out = []
    t = 0
    while t < n_ctx:
        n = min(512, n_ctx - t)
        out.append((t, n))
        t += n
    while t < T:
        n = min(512, T - t)
        out.append((t, n))
        t += n
    return out


def load_w_bf16(P, w_dram, dst_bf, ncols, stage, tag, blk=128):
    i = 0
    for c0 in range(0, ncols, blk):
        nb = min(blk, ncols - c0)
        st = stage[i % 2]
        if len(w_dram.t.shape) == 4:
            P.dma("sp", st[:, :, 0:nb], w_dram[c0 // blk, :, :, 0:nb], r=[], w=[st.k()])
        else:
            P.dma("sp", st[:, :, 0:nb], w_dram[:, :, c0:c0 + nb], r=[], w=[st.k()])
        eng = "dve" if i % 2 == 0 else "act"
        if eng == "dve":
            P.op("dve", lambda e: e.tensor_copy(out=dst_bf[:, :, c0:c0 + nb], in_=st[:, :, 0:nb]), r=[st.k()], w=[dst_bf.k(c0)])
        else:
            P.op("act", lambda e: e.activation(out=dst_bf[:, :, c0:c0 + nb], in_=st[:, :, 0:nb], func=AF.Copy), r=[st.k()], w=[dst_bf.k(c0)])
        i += 1


def gqa_program(P, T, n_ctx, io):
    nc = P.nc
    NT = T // 128
    hT, w_fm_d, w_tm_d = io["hT"], io["w_fm"], io["w_tm"]
    qT = [P.sbuf("qT%d" % h, [128, T], BF16) for h in range(2)]
    kT = P.sbuf("kT", [128, T], BF16)
    Vaug = P.sbuf("Vaug", [128, NT, 132], BF16)
    gs = P.sbuf("gs", [128, NT, 256], F32)
    w_fm = P.sbuf("w_fm", [128, KC, 384], BF16)
    w_tm = P.sbuf("w_tm", [128, KC, 384], BF16)
    stage = [P.sbuf("wst%d" % i, [128, KC, 128], F32) for i in range(2)]
    hTc = [P.sbuf("hTc%d" % i, [128, KC, 512], BF16) for i in range(2)]
    cosc = [P.sbuf("cosc%d" % i, [128, 512], F32) for i in range(2)]
    sinc = [P.sbuf("sinc%d" % i, [128, 512], F32) for i in range(2)]
    pm = P.sbuf("pm", [128, 128], F32)
    ones = P.sbuf("ones", [128, 128], F32)
    gvec = P.sbuf("gvec", [128, 4], F32)
    raw = [P.sbuf("raw%d" % i, [128, 512], F32) for i in range(3)]
    sq = P.sbuf("sq", [128, 512], F32)
    rstd = P.sbuf("rstd", [128, 512], F32)
    t1 = P.sbuf("t1", [128, 512], F32)
    t2 = P.sbuf("t2", [128, 512], F32)
    ps_a = [P.psum("ps_a%d" % i, [128, 512], F32) for i in range(2)]
    ps_b = P.psum("ps_b", [128, 512], F32)
    ps_c = P.psum("ps_c", [128, 512], F32)
    ps_x = [P.psum("acc%d" % i, [128, 512], F32) for i in range(2)]
    psF = [ps_a[0], ps_a[1], ps_x[0]]

    P.dma("sp", pm[:], io["pm"][:, :], w=[pm.k()])
    P.dma("sp", ones[:], io["ones"][:, :], w=[ones.k()])
    P.dma("sp", gvec[:], io["gvec"][:, :], w=[gvec.k()])
    load_w_bf16(P, w_fm_d, w_fm, 384, stage, "fm")
    load_w_bf16(P, w_tm_d, w_tm, 384, stage, "tm")
    P.op("pool", lambda e: e.memset(Vaug[:, :, 128:132], 1.0), w=[Vaug.k("ones")])

    scale = HD ** -0.5
    for ci, (t0, n) in enumerate(chunks_of(T, n_ctx)):
        hc = hTc[ci % 2]
        P.dma("sp", hc[:, :, 0:n], hT[:, :, t0:t0 + n].rearrange("k p t -> p k t"), w=[hc.k()])
        cc, sc = cosc[ci % 2], sinc[ci % 2]
        P.dma("sp", cc[:, 0:n], io["cosT"][:, t0:t0 + n], w=[cc.k()])
        P.dma("sp", sc[:, 0:n], io["sinT"][:, t0:t0 + n], w=[sc.k()])
        for bi in range(3):
            ps = psF[bi]
            for kc in range(KC):
                P.op("pe", lambda e: e.matmul(ps[:, 0:n], lhsT=w_fm[:, kc, bi * 128:(bi + 1) * 128], rhs=hc[:, kc, 0:n],
                                               start=(kc == 0), stop=(kc == KC - 1)),
                     r=[w_fm.k(bi * 128), hc.k()], w=[ps.k()])
            rw = raw[bi]
            P.op("act", lambda e: e.activation(out=rw[:, 0:n], in_=ps[:, 0:n], func=AF.Copy), r=[ps.k()], w=[rw.k()])
        for bi in range(3):
            rw = raw[bi]
            P.op("dve", lambda e: e.tensor_tensor(out=sq[:, 0:n], in0=rw[:, 0:n], in1=rw[:, 0:n], op=ALU.mult), r=[rw.k()], w=[sq.k()])
            P.op("pe", lambda e: e.matmul(ps_b[:, 0:n], lhsT=ones[:, :], rhs=sq[:, 0:n], start=True, stop=True),
                 r=[ones.k(), sq.k()], w=[ps_b.k()])
            P.op("pe", lambda e: e.matmul(ps_c[:, 0:n], lhsT=pm[:, :], rhs=rw[:, 0:n], start=True, stop=True),
                 r=[pm.k(), rw.k()], w=[ps_c.k()])
            P.op("act", lambda e: e.activation(out=rstd[:, 0:n], in_=ps_b[:, 0:n], func=AF.Sqrt, scale=1.0 / HD, bias=EPS),
                 r=[ps_b.k()], w=[rstd.k()])
            P.op("dve", lambda e: e.reciprocal(out=rstd[:, 0:n], in_=rstd[:, 0:n]), r=[rstd.k()], w=[rstd.k()])
            gi = 0 if bi < 2 else 2
            P.op("dve", lambda e: e.scalar_tensor_tensor(out=t1[:, 0:n], in0=rw[:, 0:n], scalar=gvec[:, gi:gi + 1], in1=cc[:, 0:n],
                                                          op0=ALU.mult, op1=ALU.mult), r=[rw.k(), gvec.k(), cc.k()], w=[t1.k()])
            P.op("dve", lambda e: e.scalar_tensor_tensor(out=t2[:, 0:n], in0=ps_c[:, 0:n], scalar=gvec[:, gi + 1:gi + 2], in1=sc[:, 0:n],
                                                          op0=ALU.mult, op1=ALU.mult), r=[ps_c.k(), gvec.k(), sc.k()], w=[t2.k()])
            P.op("dve", lambda e: e.tensor_tensor(out=t1[:, 0:n], in0=t1[:, 0:n], in1=t2[:, 0:n], op=ALU.add), r=[t1.k(), t2.k()], w=[t1.k()])
            dst = qT[bi] if bi < 2 else kT
            sc_f = scale if bi < 2 else 1.0
            P.op("dve", lambda e: e.scalar_tensor_tensor(out=dst[:, t0:t0 + n], in0=t1[:, 0:n], scalar=sc_f, in1=rstd[:, 0:n],
                                                          op0=ALU.mult, op1=ALU.mult), r=[t1.k(), rstd.k()], w=[dst.k(ci)])
        for ti in range(n // 128):
            tt = (t0 // 128) + ti
            ps = ps_a[ti % 2]
            for kc in range(KC):
                P.op("pe", lambda e: e.matmul(ps[:, 0:384], lhsT=hc[:, kc, ti * 128:(ti + 1) * 128], rhs=w_tm[:, kc, 0:384],
                                               start=(kc == 0), stop=(kc == KC - 1)),
                     r=[w_tm.k(), hc.k()], w=[ps.k()])
            P.op("dve", lambda e: e.tensor_copy(out=Vaug[:, tt, 0:128], in_=ps[:, 0:128]), r=[ps.k()], w=[Vaug.k("v", tt)])
            P.op("act", lambda e: e.activation(out=gs[:, tt, :], in_=ps[:, 128:384], func=AF.Silu), r=[ps.k()], w=[gs.k(tt)])

    pex = [P.sbuf("pex%d" % i, [128, 512], BF16) for i in range(3)]
    acc = [ps_b, ps_c] + ps_x
    rec = P.sbuf("rec", [128, 4], F32)
    yt = [P.sbuf("yt%d" % i, [128, 128], F32) for i in range(2)]
    yo = [P.sbuf("yo%d" % i, [128, 128], BF16) for i in range(2)]
    nctx_t = n_ctx // 128
    it = 0
    for h in range(2):
        for ci, (t0, n) in enumerate(chunks_of(T, n_ctx)):
            nk = nctx_t if t0 < n_ctx else NT
            nsub = n // 128
            def emit_S(kt, slot):
                ps = ps_a[slot % 2]
                P.op("pe", lambda e: e.matmul(ps[:, 0:n], lhsT=kT[:, kt * 128:(kt + 1) * 128], rhs=qT[h][:, t0:t0 + n], start=True, stop=True),
                     r=[kT.k(), qT[h].k(ci)], w=[ps.k()])
            emit_S(0, it)
            for kt in range(nk):
                ps = ps_a[it % 2]
                px = pex[it % 3]
                if kt + 1 < nk:
                    emit_S(kt + 1, it + 1)
                it += 1
                P.op("act", lambda e: e.activation(out=px[:, 0:n], in_=ps[:, 0:n], func=AF.Exp), r=[ps.k()], w=[px.k()])
                for s in range(nsub):
                    a = acc[s]
                    P.op("pe", lambda e: e.matmul(a[:, 0:129], lhsT=px[:, s * 128:(s + 1) * 128], rhs=Vaug[:, kt, 0:129],
                                                   start=(kt == 0), stop=(kt == nk - 1)),
                         r=[px.k(), Vaug.k()], w=[a.k()])
            for s in range(nsub):
                a = acc[s]
                tt = t0 // 128 + s
                y1 = yt[s % 2]
                y2 = yo[s % 2]
                P.op("dve", lambda e: e.reciprocal(out=rec[:, s:s + 1], in_=a[:, 128:129]), r=[a.k()], w=[rec.k(s)])
                P.op("dve", lambda e: e.scalar_tensor_tensor(out=y2[:, :], in0=a[:, 0:128], scalar=rec[:, s:s + 1],
                                                              in1=gs[:, tt, h * 128:(h + 1) * 128], op0=ALU.mult, op1=ALU.mult),
                     r=[a.k(), rec.k(s), gs.k(tt)], w=[y2.k()])
                P.dma("pool", io["yb"][tt * 128:(tt + 1) * 128, h * 128:(h + 1) * 128], y2[:, :], r=[y2.k()], w=[io["yb"].k(tt, h)])


GATE_CAP = 15.0
NEG = -30000.0


def mlstm_program(P, T, n_ctx, io):
    NCH = T // 128
    hT = io["hT"]
    qT = P.sbuf("m_qT", [128, T], F32)
    qTb = P.sbuf("m_qTb", [128, T], BF16)
    kTb = P.sbuf("m_kTb", [128, T], BF16)
    ktm = P.sbuf("m_ktm", [128, NCH, 128], F32)
    vaug = P.sbuf("m_vaug", [128, NCH, 260], BF16)
    og = P.sbuf("m_og", [128, NCH, 256], F32)
    gpre = P.sbuf("m_gpre", [128, NCH, 4], F32)
    lfn = P.sbuf("m_lfn", [128, NCH, 2], F32)
    gbias = P.sbuf("m_gbias", [128, 4], F32)
    normg = P.sbuf("m_normg", [128, 256], F32)
    tri = [P.sbuf("m_tri%d" % d, [128, 128], F32) for d in range(2)]
    mneg = [P.sbuf("m_mneg%d" % d, [128, 128], F32) for d in range(2)]
    onesf = P.sbuf("m_onesf", [128, 128], F32)
    pb = [P.psum("m_pb%d" % i, [128, 512], F32) for i in range(8)]
    P.push()
    w_fm = P.sbuf("m_wfm", [128, KC, 256], BF16)
    w_tm = P.sbuf("m_wtm", [128, KC, 900], BF16)
    stage = [P.sbuf("m_wst%d" % i, [128, KC, 128], F32) for i in range(2)]
    hTc = [P.sbuf("m_hTc%d" % i, [128, KC, 512], BF16) for i in range(2)]
    sig = P.sbuf("m_sig", [128, 256], F32)
    sil = P.sbuf("m_sil", [128, 256], F32)

    P.dma("sp", gbias[:], io["gbias"][:, :], w=[gbias.k()])
    P.dma("sp", normg[:], io["normg"][:, :], w=[normg.k()])
    for d, nm in enumerate(("tri_f", "tri_b")):
        P.dma("sp", tri[d][:], io[nm][:, :], w=[tri[d].k()])
    for d, nm in enumerate(("mneg_f", "mneg_b")):
        P.dma("sp", mneg[d][:], io[nm][:, :], w=[mneg[d].k()])
    P.op("pool", lambda e: e.memset(onesf[:, :], 1.0), w=[onesf.k()])
    P.op("pool", lambda e: e.memset(vaug[:, :, 256:260], 1.0), w=[vaug.k("ones")])
    load_w_bf16(P, io["w_fm"], w_fm, 256, stage, "fm")
    load_w_bf16(P, io["w_tm"], w_tm, 900, stage, "tm")

    qscale = 128 ** -0.5
    for ci, (t0, n) in enumerate(chunks_of(T, n_ctx)):
        hc = hTc[ci % 2]
        P.dma("sp", hc[:, :, 0:n], hT[:, :, t0:t0 + n].rearrange("k p t -> p k t"), w=[hc.k()])
        for bi in range(2):
            ps = pb[bi]
            for kc in range(KC):
                P.op("pe", lambda e: e.matmul(ps[:, 0:n], lhsT=w_fm[:, kc, bi * 128:(bi + 1) * 128], rhs=hc[:, kc, 0:n],
                                               start=(kc == 0), stop=(kc == KC - 1)), r=[w_fm.k(bi * 128), hc.k()], w=[ps.k()])
            if bi == 0:
                P.op("act", lambda e: e.activation(out=qT[:, t0:t0 + n], in_=ps[:, 0:n], func=AF.Copy, scale=qscale), r=[ps.k()], w=[qT.k(ci)])
                P.op("dve", lambda e: e.tensor_copy(out=qTb[:, t0:t0 + n], in_=qT[:, t0:t0 + n]), r=[qT.k(ci)], w=[qTb.k(ci)])
            else:
                P.op("act", lambda e: e.activation(out=kTb[:, t0:t0 + n], in_=ps[:, 0:n], func=AF.Copy), r=[ps.k()], w=[kTb.k(ci)])
        for ti in range(n // 128):
            c = t0 // 128 + ti
            p1, p2 = pb[2 + (ti % 2) * 2], pb[3 + (ti % 2) * 2]
            for kc in range(KC):
                P.op("pe", lambda e: e.matmul(p1[:, 0:388], lhsT=hc[:, kc, ti * 128:(ti + 1) * 128], rhs=w_tm[:, kc, 0:388],
                                               start=(kc == 0), stop=(kc == KC - 1)), r=[w_tm.k(), hc.k()], w=[p1.k()])
            for kc in range(KC):
                P.op("pe", lambda e: e.matmul(p2[:, 0:512], lhsT=hc[:, kc, ti * 128:(ti + 1) * 128], rhs=w_tm[:, kc, 388:900],
                                               start=(kc == 0), stop=(kc == KC - 1)), r=[w_tm.k(), hc.k()], w=[p2.k()])
            P.op("dve", lambda e: e.tensor_copy(out=ktm[:, c, :], in_=p1[:, 0:128]), r=[p1.k()], w=[ktm.k(c)])
            P.op("act", lambda e: e.activation(out=vaug[:, c, 0:256], in_=p1[:, 128:384], func=AF.Copy), r=[p1.k()], w=[vaug.k("v", c)])
            P.op("dve", lambda e: e.tensor_tensor(out=gpre[:, c, :], in0=p1[:, 384:388], in1=gbias[:, :], op=ALU.add), r=[p1.k(), gbias.k()], w=[gpre.k(c)])
            P.op("act", lambda e: e.activation(out=sig[:, :], in_=p2[:, 0:256], func=AF.Sigmoid), r=[p2.k()], w=[sig.k()])
            P.op("act", lambda e: e.activation(out=sil[:, :], in_=p2[:, 256:512], func=AF.Silu), r=[p2.k()], w=[sil.k()])
            P.op("dve", lambda e: e.tensor_tensor(out=og[:, c, :], in0=sig[:, :], in1=sil[:, :], op=ALU.mult), r=[sig.k(), sil.k()], w=[og.k(c)])

    P.pop()
    P.push()
    Hacc = P.sbuf("m_Hacc", [128, NCH, 256], F32)
    P.op("act", lambda e: e.activation(out=gpre[:, :, :], in_=gpre[:, :, :], func=AF.Tanh, scale=1.0 / GATE_CAP), r=[gpre.k()], w=[gpre.k()])
    P.op("dve", lambda e: e.tensor_scalar(out=gpre[:, :, :], in0=gpre[:, :, :], scalar1=GATE_CAP, scalar2=None, op0=ALU.mult), r=[gpre.k()], w=[gpre.k()])
    P.op("act", lambda e: e.activation(out=lfn[:, :, :], in_=gpre[:, :, 2:4], func=AF.Exp, scale=-1.0), r=[gpre.k()], w=[lfn.k()])
    P.op("act", lambda e: e.activation(out=lfn[:, :, :], in_=lfn[:, :, :], func=AF.Ln, bias=1.0), r=[lfn.k()], w=[lfn.k()])
    P.op("dve", lambda e: e.tensor_scalar(out=lfn[:, :, :], in0=lfn[:, :, :], scalar1=-1.0, scalar2=None, op0=ALU.mult), r=[lfn.k()], w=[lfn.k()])

    nctx_c = n_ctx // 128
    order_f = list(range(NCH))
    order_b = list(range(nctx_c - 1, -1, -1)) + list(range(NCH - 1, nctx_c - 1, -1))
    st = []
    for d in range(2):
        s = {}
        s["Cn"] = P.sbuf("m_Cn%d" % d, [128, 260], F32)
        s["Cnb"] = P.sbuf("m_Cnb%d" % d, [128, 260], BF16)
        for nm, shp, dt in (("LFbc", [128, 128], F32), ("tmp", [128, 128], F32), ("ET", [128, 128], F32)):
            s[nm] = P.sbuf("m_%s%d" % (nm, d), shp, dt)
        for q in range(2):
            for nm, shp, dt in (("expB", [128, 128], F32), ("qt", [128, 128], BF16), ("smT", [128, 128], BF16), ("kw", [128, 128], BF16), ("col", [128, 8], F32)):
                s[nm + str(q)] = P.sbuf("m_%s%d_%d" % (nm, d, q), shp, dt)
        P.op("pool", lambda e: e.memset(s["Cn"][:, :], 0.0), w=[s["Cn"].k()])
        P.op("pool", lambda e: e.memset(s["Cnb"][:, :], 0.0), w=[s["Cnb"].k()])
        st.append(s)

    def indep(d, c, q):
        s = st[d]
        p1, p2, p4 = pb[4 * d], pb[4 * d + 1], pb[4 * d + 3]
        col, expB, qt, smT, kw = (s[x + str(q)] for x in ("col", "expB", "qt", "smT", "kw"))
        gcol = 127 if d == 0 else 0
        tsl = slice(c * 128, (c + 1) * 128)
        P.op("dve", lambda e: e.tensor_scalar(out=s["LFbc"][:, :], in0=onesf[:, :], scalar1=lfn[:, c, d:d + 1], scalar2=None, op0=ALU.mult),
             r=[onesf.k(), lfn.k()], w=[s["LFbc"].k()])
        yield
        P.op("pe", lambda e: e.matmul(p1[:, 0:128], lhsT=s["LFbc"][:, :], rhs=tri[d][:, :], start=True, stop=True),
             r=[s["LFbc"].k(), tri[d].k()], w=[p1.k()])
        P.op("pe", lambda e: e.matmul(p1[:, 128:129], lhsT=tri[d][:, :], rhs=lfn[:, c, d:d + 1], start=True, stop=True),
             r=[tri[d].k(), lfn.k()], w=[p1.k()])
        P.op("pe", lambda e: e.matmul(p2[:, 0:128], lhsT=kTb[:, tsl], rhs=qTb[:, tsl], start=True, stop=True),
             r=[kTb.k(), qTb.k()], w=[p2.k()])
        yield
        P.op("dve", lambda e: e.tensor_tensor(out=col[:, 0:1], in0=gpre[:, c, d:d + 1], in1=p1[:, 128:129], op=ALU.subtract),
             r=[gpre.k(), p1.k()], w=[col.k(0)])
        P.op("dve", lambda e: e.tensor_tensor(out=s["tmp"][:, :], in0=p1[:, 0:128], in1=mneg[d][:, :], op=ALU.add),
             r=[p1.k(), mneg[d].k()], w=[s["tmp"].k()])
        yield
        P.op("act", lambda e: e.activation(out=s["ET"][:, :], in_=s["tmp"][:, :], func=AF.Exp, bias=col[:, 0:1]),
             r=[s["tmp"].k(), col.k(0)], w=[s["ET"].k()])
        P.op("act", lambda e: e.activation(out=expB[:, :], in_=p1[:, 0:128], func=AF.Exp), r=[p1.k()], w=[expB.k()])
        P.op("act", lambda e: e.activation(out=col[:, 1:2], in_=p1[:, gcol:gcol + 1], func=AF.Exp, bias=col[:, 0:1]),
             r=[p1.k(), col.k(0)], w=[col.k(1)])
        yield
        P.op("dve", lambda e: e.tensor_tensor(out=qt[:, :], in0=qT[:, tsl], in1=expB[:, :], op=ALU.mult),
             r=[qT.k(), expB.k()], w=[qt.k()])
        P.op("dve", lambda e: e.tensor_tensor(out=smT[:, :], in0=p2[:, 0:128], in1=s["ET"][:, :], op=ALU.mult),
             r=[p2.k(), s["ET"].k()], w=[smT.k()])
        P.op("dve", lambda e: e.tensor_scalar(out=kw[:, :], in0=ktm[:, c, :], scalar1=col[:, 1:2], scalar2=None, op0=ALU.mult),
             r=[ktm.k(c), col.k(1)], w=[kw.k()])
        yield
        P.op("pe", lambda e: e.matmul(p4[:, 0:257], lhsT=kw[:, :], rhs=vaug[:, c, 0:257], start=True, stop=True),
             r=[kw.k(), vaug.k()], w=[p4.k()])
        yield

    def dep(d, c, q, first):
        s = st[d]
        p3, p4 = pb[4 * d + 2], pb[4 * d + 3]
        col, expB, qt, smT = (s[x + str(q)] for x in ("col", "expB", "qt", "smT"))
        gcol = 127 if d == 0 else 0
        P.op("pe", lambda e: e.matmul(p3[:, 0:257], lhsT=smT[:, :], rhs=vaug[:, c, 0:257], start=True, stop=False),
             r=[smT.k(), vaug.k()], w=[p3.k()])
        P.op("pe", lambda e: e.matmul(p3[:, 0:257], lhsT=qt[:, :], rhs=s["Cnb"][:, 0:257], start=False, stop=True),
             r=[qt.k(), s["Cnb"].k()], w=[p3.k()])
        yield
        P.op("dve", lambda e: e.scalar_tensor_tensor(out=s["Cn"][:, 0:257], in0=s["Cn"][:, 0:257], scalar=expB[:, gcol:gcol + 1], in1=p4[:, 0:257],
                                                      op0=ALU.mult, op1=ALU.add), r=[s["Cn"].k(), expB.k(), p4.k()], w=[s["Cn"].k()])
        yield
        P.op("act", lambda e: e.activation(out=s["Cnb"][:, 0:257], in_=s["Cn"][:, 0:257], func=AF.Copy), r=[s["Cn"].k()], w=[s["Cnb"].k()])
        P.op("act", lambda e: e.activation(out=col[:, 2:3], in_=p3[:, 256:257], func=AF.Abs), r=[p3.k()], w=[col.k(2)])
        yield
        P.op("dve", lambda e: e.tensor_scalar(out=col[:, 2:3], in0=col[:, 2:3], scalar1=1.0, scalar2=None, op0=ALU.max),
             r=[col.k(2)], w=[col.k(2)])
        P.op("dve", lambda e: e.reciprocal(out=col[:, 3:4], in_=col[:, 2:3]), r=[col.k(2)], w=[col.k(3)])
        if first:
            P.op("dve", lambda e: e.tensor_scalar(out=Hacc[:, c, :], in0=p3[:, 0:256], scalar1=col[:, 3:4], scalar2=None, op0=ALU.mult),
                 r=[p3.k(), col.k(3)], w=[Hacc.k(c)])
        else:
            P.op("dve", lambda e: e.scalar_tensor_tensor(out=Hacc[:, c, :], in0=p3[:, 0:256], scalar=col[:, 3:4], in1=Hacc[:, c, :],
                                                          op0=ALU.mult, op1=ALU.add), r=[p3.k(), col.k(3), Hacc.k(c)], w=[Hacc.k(c)])
        yield

    def run_interleaved(gens):
        gens = list(gens)
        while gens:
            for g in list(gens):
                try:
                    next(g)
                except StopIteration:
                    gens.remove(g)

    orders = (order_f, order_b)
    done = set()
    run_interleaved([indep(d, orders[d][0], 0) for d in range(2)])
    for i in range(NCH):
        gens = []
        for d in range(2):
            c = orders[d][i]
            gens.append(dep(d, c, i % 2, c not in done))
            done.add(c)
        if i + 1 < NCH:
            for d in range(2):
                gens.append(indep(d, orders[d][i + 1], (i + 1) % 2))
        run_interleaved(gens)

    ssq = P.sbuf("m_ssq", [128, NCH], F32)
    junk = P.sbuf("m_junk", [128, 256], F32)
    yo = [P.sbuf("m_yo%d" % i, [128, 256], BF16) for i in range(2)]
    ytmp = [P.sbuf("m_ytmp%d" % i, [128, 256], F32) for i in range(2)]
    for c in range(NCH):
        P.op("dve", lambda e: e.tensor_tensor(out=junk[:, :], in0=Hacc[:, c, :], in1=Hacc[:, c, :], op=ALU.mult), r=[Hacc.k(c)], w=[junk.k()])
        P.op("dve", lambda e: e.reduce_sum(out=ssq[:, c:c + 1], in_=junk[:, :], axis=AX.X), r=[junk.k()], w=[ssq.k(c)])
    P.op("act", lambda e: e.activation(out=ssq[:, :], in_=ssq[:, :], func=AF.Sqrt, scale=1.0 / 256, bias=1e-6), r=[ssq.k()], w=[ssq.k()])
    P.op("dve", lambda e: e.reciprocal(out=ssq[:, :], in_=ssq[:, :]), r=[ssq.k()], w=[ssq.k()])
    for c in range(NCH):
        yt, y2 = ytmp[c % 2], yo[c % 2]
        P.op("dve", lambda e: e.scalar_tensor_tensor(out=yt[:, :], in0=Hacc[:, c, :], scalar=ssq[:, c:c + 1], in1=normg[:, :],
                                                      op0=ALU.mult, op1=ALU.mult), r=[Hacc.k(c), ssq.k(), normg.k()], w=[yt.k()])
        P.op("dve", lambda e: e.tensor_tensor(out=y2[:, :], in0=yt[:, :], in1=og[:, c, :], op=ALU.mult), r=[yt.k(), og.k(c)], w=[y2.k()])
        P.dma("pool", io["yc"][c * 128:(c + 1) * 128, :], y2[:, :], r=[y2.k()], w=[io["yc"].k(c)])
    P.pop()


RW_GN_EPS = 64e-5
C64 = 64


def seg_chunks(T, n_ctx, n=256):
    out = []
    for (a, b) in ((0, n_ctx), (n_ctx, T)):
        t = a
        while t < b:
            m = min(n, b - t)
            out.append((t, m, a, b))
            t += m
    return out


def rwkv_program(P, T, n_ctx, io, tag=""):
    NC64 = T // C64
    nctx_c = n_ctx // C64
    hT = io["hT"]
    R = P.sbuf(tag + "R", [128, T], F32)
    A = P.sbuf(tag + "A", [128, T], F32)
    KD = [P.sbuf(tag + "KD%d" % d, [128, T], F32) for d in range(2)]
    BB = [P.sbuf(tag + "BB%d" % d, [128, T], F32) for d in range(2)]
    LW = [P.sbuf(tag + "LW%d" % d, [128, T], F32) for d in range(2)]
    vtm = P.sbuf(tag + "vtm", [128, NC64, 64], F32)
    bon = P.sbuf(tag + "bon", [128, NC64], F32)
    ident = P.sbuf(tag + "ident", [128, 128], F32)
    id64 = P.sbuf(tag + "id64", [128, 64], F32)
    onesc = P.sbuf(tag + "onesc", [128, 64], F32)
    pcol = P.sbuf(tag + "pcol", [128, 8], F32)
    P.dma("sp", ident[:], io["ident"][:, :], w=[ident.k()])
    P.dma("sp", id64[:], io["id64"][:, :], w=[id64.k()])
    P.dma("sp", onesc[:], io["onesc"][:, :], w=[onesc.k()])
    P.dma("sp", pcol[:], io["pcol"][:, :], w=[pcol.k()])

    P.push()
    w_fm = P.sbuf(tag + "wfm", [128, KC, 640], BF16)
    stage = [P.sbuf(tag + "wst%d" % i, [128, KC, 64], F32) for i in range(2)]
    hTc = [P.sbuf(tag + "hTc0", [128, KC, 258], BF16)] * 2
    mu = P.sbuf(tag + "mu", [128, 16], F32)
    wup = P.sbuf(tag + "wup", [128, 128], F32)
    aup = P.sbuf(tag + "aup", [128, 128], F32)
    bones = P.sbuf(tag + "bones", [128, 128], F32)
    tmpn = {}
    for nm in ("K", "Vf", "WD", "AD", "tw", "as0", "as1", "kk0", "sq", "rn", "kka", "t1", "rkr"):
        tmpn[nm] = P.sbuf(tag + "t_" + nm, [128, 256], F32)
    ps = [P.psum(tag + "pp%d" % i, [128, 512], F32) for i in range(8)]
    P.dma("sp", mu[:, 0:10], io["mu"][:, :], w=[mu.k()])
    P.dma("sp", wup[:], io["wup"][:, :], w=[wup.k()])
    P.dma("sp", aup[:], io["aup"][:, :], w=[aup.k()])
    P.dma("sp", bones[:], io["bones"][:, :], w=[bones.k()])
    load_w_bf16(P, io["w_fm"], w_fm, 640, stage, "fm", blk=64)
    P.op("dve", lambda e: e.tensor_tensor(out=mu[:, 10:15], in0=mu[:, 0:5], in1=mu[:, 5:10], op=ALU.add), r=[mu.k()], w=[mu.k()])
    P.op("dve", lambda e: e.tensor_scalar(out=mu[:, 10:15], in0=mu[:, 10:15], scalar1=-1.0, scalar2=1.0, op0=ALU.mult, op1=ALU.add), r=[mu.k()], w=[mu.k()])

    for ci, (t0, n, sa, sb) in enumerate(seg_chunks(T, n_ctx)):
        lo, hi = max(sa, t0 - 1), min(sb, t0 + n + 1)
        nn = hi - lo
        o = t0 - lo
        hc = hTc[ci % 2]
        P.dma("sp", hc[:, :, 0:nn], hT[:, :, lo:hi].rearrange("k p t -> p k t"), w=[hc.k()])
        dsts = [R[:, t0:t0 + n], tmpn["K"][:, 0:n], tmpn["Vf"][:, 0:n], tmpn["WD"][:, 0:n], tmpn["AD"][:, 0:n]]
        dkeys = [R.k(ci), tmpn["K"].k(), tmpn["Vf"].k(), tmpn["WD"].k(), tmpn["AD"].k()]
        for bi in range(5):
            pp = ps[bi % 2]
            for kc in range(KC):
                P.op("pe", lambda e: e.matmul(pp[:, 0:nn], lhsT=w_fm[:, kc, bi * 128:(bi + 1) * 128], rhs=hc[:, kc, 0:nn],
                                               start=(kc == 0), stop=(kc == KC - 1)), r=[w_fm.k(), hc.k()], w=[pp.k()])
            dst, dk = dsts[bi], dkeys[bi]
            P.op("act", lambda e: e.activation(out=dst, in_=pp[:, o:o + n], func=AF.Copy, scale=mu[:, 10 + bi:11 + bi]), r=[pp.k(), mu.k()], w=[dk])
            j0 = 0 if o == 1 else 1
            P.op("dve", lambda e: e.scalar_tensor_tensor(out=dst[:, j0:n], in0=pp[:, o + j0 - 1:o + n - 1], scalar=mu[:, bi:bi + 1], in1=dst[:, j0:n],
                                                          op0=ALU.mult, op1=ALU.add), r=[pp.k(), mu.k(), dk], w=[dk])
            j1 = n if (o + n + 1 <= nn) else n - 1
            P.op("dve", lambda e: e.scalar_tensor_tensor(out=dst[:, 0:j1], in0=pp[:, o + 1:o + 1 + j1], scalar=mu[:, 5 + bi:6 + bi], in1=dst[:, 0:j1],
                                                          op0=ALU.mult, op1=ALU.add), r=[pp.k(), mu.k(), dk], w=[dk])
        K, Vf, WD, AD = tmpn["K"], tmpn["Vf"], tmpn["WD"], tmpn["AD"]
        tw, kk0, sq, rn, kka, t1, rkr = (tmpn[x] for x in ("tw", "kk0", "sq", "rn", "kka", "t1", "rkr"))
        asg = [tmpn["as0"], tmpn["as1"]]
        P.op("act", lambda e: e.activation(out=tw[:, 0:n], in_=WD[:, 0:n], func=AF.Tanh), r=[WD.k()], w=[tw.k()])
        for d in range(2):
            rows = slice(64 * d, 64 * d + 64)
            px = ps[2 + d]
            P.op("pe", lambda e: e.matmul(px[:, 0:n], lhsT=wup[rows, :], rhs=tw[rows, 0:n], start=True, stop=True), r=[wup.k(), tw.k()], w=[px.k()])
            P.op("act", lambda e: e.activation(out=LW[d][:, t0:t0 + n], in_=px[:, 0:n], func=AF.Sigmoid, bias=pcol[:, d:d + 1]), r=[px.k(), pcol.k()], w=[LW[d].k(ci)])
            P.op("dve", lambda e: e.tensor_scalar(out=LW[d][:, t0:t0 + n], in0=LW[d][:, t0:t0 + n], scalar1=-float(np.exp(-0.5)), scalar2=None, op0=ALU.mult),
                 r=[LW[d].k(ci)], w=[LW[d].k(ci)])
            pa = ps[4 + d]
            P.op("pe", lambda e: e.matmul(pa[:, 0:n], lhsT=aup[rows, :], rhs=AD[rows, 0:n], start=True, stop=True), r=[aup.k(), AD.k()], w=[pa.k()])
            P.op("act", lambda e: e.activation(out=asg[d][:, 0:n], in_=pa[:, 0:n], func=AF.Sigmoid, bias=pcol[:, 2 + d:3 + d]), r=[pa.k(), pcol.k()], w=[asg[d].k()])
        P.op("dve", lambda e: e.tensor_scalar(out=kk0[:, 0:n], in0=K[:, 0:n], scalar1=pcol[:, 4:5], scalar2=None, op0=ALU.mult), r=[K.k(), pcol.k()], w=[kk0.k()])
        P.op("dve", lambda e: e.tensor_tensor(out=sq[:, 0:n], in0=kk0[:, 0:n], in1=kk0[:, 0:n], op=ALU.mult), r=[kk0.k()], w=[sq.k()])
        pn = ps[6]
        P.op("pe", lambda e: e.matmul(pn[:, 0:n], lhsT=bones[:, :], rhs=sq[:, 0:n], start=True, stop=True), r=[bones.k(), sq.k()], w=[pn.k()])
        P.op("act", lambda e: e.activation(out=rn[:, 0:n], in_=pn[:, 0:n], func=AF.Sqrt), r=[pn.k()], w=[rn.k()])
        P.op("dve", lambda e: e.tensor_scalar(out=rn[:, 0:n], in0=rn[:, 0:n], scalar1=1e-12, scalar2=None, op0=ALU.max), r=[rn.k()], w=[rn.k()])
        P.op("dve", lambda e: e.reciprocal(out=rn[:, 0:n], in_=rn[:, 0:n]), r=[rn.k()], w=[rn.k()])
        P.op("dve", lambda e: e.scalar_tensor_tensor(out=A[:, t0:t0 + n], in0=kk0[:, 0:n], scalar=-1.0, in1=rn[:, 0:n], op0=ALU.mult, op1=ALU.mult),
             r=[kk0.k(), rn.k()], w=[A.k(ci)])
        P.op("dve", lambda e: e.tensor_scalar(out=kka[:, 0:n], in0=K[:, 0:n], scalar1=pcol[:, 5:6], scalar2=None, op0=ALU.mult), r=[K.k(), pcol.k()], w=[kka.k()])
        for d in range(2):
            P.op("dve", lambda e: e.scalar_tensor_tensor(out=BB[d][:, t0:t0 + n], in0=A[:, t0:t0 + n], scalar=-1.0, in1=asg[d][:, 0:n], op0=ALU.mult, op1=ALU.mult),
                 r=[A.k(ci), asg[d].k()], w=[BB[d].k(ci)])
            P.op("dve", lambda e: e.scalar_tensor_tensor(out=t1[:, 0:n], in0=asg[d][:, 0:n], scalar=-1.0, in1=kka[:, 0:n], op0=ALU.add, op1=ALU.mult),
                 r=[asg[d].k(), kka.k()], w=[t1.k()])
            P.op("dve", lambda e: e.tensor_tensor(out=KD[d][:, t0:t0 + n], in0=t1[:, 0:n], in1=K[:, 0:n], op=ALU.add), r=[t1.k(), K.k()], w=[KD[d].k(ci)])
        P.op("dve", lambda e: e.scalar_tensor_tensor(out=rkr[:, 0:n], in0=R[:, t0:t0 + n], scalar=pcol[:, 6:7], in1=K[:, 0:n], op0=ALU.mult, op1=ALU.mult),
             r=[R.k(ci), pcol.k(), K.k()], w=[rkr.k()])
        pbn = ps[7]
        nq = n // 64
        c0 = t0 // 64
        for hh in range(2):
            rows = slice(64 * hh, 64 * hh + 64)
            for q in range(nq):
                P.op("pe", lambda e: e.matmul(pbn[rows, q:q + 1], lhsT=rkr[rows, q * 64:(q + 1) * 64], rhs=onesc[rows, 0:1], start=True, stop=True),
                     r=[rkr.k(), onesc.k()], w=[pbn.k()])
                P.op("pe", lambda e: e.matmul(pbn[rows, 64 + q * 64:64 + (q + 1) * 64], lhsT=Vf[rows, q * 64:(q + 1) * 64], rhs=ident[rows, 64 * hh:64 * hh + 64],
                                               start=True, stop=True), r=[Vf.k(), ident.k()], w=[pbn.k()])
        P.op("dve", lambda e: e.tensor_copy(out=bon[:, c0:c0 + nq], in_=pbn[:, 0:nq]), r=[pbn.k()], w=[bon.k(ci)])
        P.op("act", lambda e: e.activation(out=vtm[:, c0:c0 + nq, :], in_=pbn[:, 64:64 + nq * 64].rearrange("p (q v) -> p q v", v=64), func=AF.Copy), r=[pbn.k()], w=[vtm.k(ci)])
    P.pop()

    P.push()
    yacc = P.sbuf(tag + "yacc", [128, NC64, 64], F32)
    P.push()
    masks = [P.sbuf(tag + "mask%d" % d, [128, 320], F32) for d in range(2)]
    for d, nm in enumerate(("mask_f", "mask_b")):
        P.dma("sp", masks[d][:], io[nm][:, :], w=[masks[d].k()])
    sts = []
    for d in range(2):
        s = {}
        for nm, w_ in (("G", 64), ("GE", 64), ("pre", 64), ("E1", 64), ("E2", 64), ("E3", 64), ("E4", 64), ("BK", 128), ("BKh", 128),
                       ("PP0", 128), ("PP1", 128), ("Wsb", 64), ("Usb", 64), ("ST", 64)):
            s[nm] = P.sbuf(tag + "s%d_%s" % (d, nm), [128, w_], F32)
        for q in range(2):
            for nm, w_ in (("AR", 128), ("AM", 320), ("Z", 64), ("BKT", 128), ("col", 4)):
                s[nm + str(q)] = P.sbuf(tag + "s%d_%s%d" % (d, nm, q), [128, w_], F32)
        s["ps"] = [P.psum(tag + "sp%d_%d" % (d, i), [128, 512], F32) for i in range(4)]
        P.op("pool", lambda e: e.memset(s["ST"][:, :], 0.0), w=[s["ST"].k()])
        sts.append(s)
    H = [slice(0, 64), slice(64, 128)]

    def indep(d, c, q):
        s = sts[d]
        pA, pL, pW, pT = s["ps"]
        tsl = slice(c * 64, (c + 1) * 64)
        G, GE, pre, E1, E2, E3, E4, BK, BKh = (s[x] for x in ("G", "GE", "pre", "E1", "E2", "E3", "E4", "BK", "BKh"))
        AR, AM, Z, BKT, col = (s[x + str(q)] for x in ("AR", "AM", "Z", "BKT", "col"))
        lw = LW[d][:, tsl]
        if d == 0:
            P.op("dve", lambda e: e.tensor_tensor_scan(out=G[:, :], data0=onesc[:, :], data1=lw, initial=0.0, op0=ALU.mult, op1=ALU.add),
                 r=[onesc.k(), LW[d].k()], w=[G.k()])
            gcol = G[:, 63:64]
        else:
            P.op("dve", lambda e: e.tensor_tensor_scan(out=pre[:, :], data0=onesc[:, :], data1=lw, initial=0.0, op0=ALU.mult, op1=ALU.add),
                 r=[onesc.k(), LW[d].k()], w=[pre.k()])
            P.op("dve", lambda e: e.tensor_tensor(out=G[:, :], in0=lw, in1=pre[:, :], op=ALU.subtract), r=[LW[d].k(), pre.k()], w=[G.k()])
            P.op("dve", lambda e: e.tensor_scalar(out=G[:, :], in0=G[:, :], scalar1=pre[:, 63:64], scalar2=None, op0=ALU.add), r=[G.k(), pre.k()], w=[G.k()])
            gcol = G[:, 0:1]
        P.op("dve", lambda e: e.tensor_tensor(out=GE[:, :], in0=G[:, :], in1=lw, op=ALU.subtract), r=[G.k(), LW[d].k()], w=[GE.k()])
        yield
        P.op("act", lambda e: e.activation(out=E1[:, :], in_=G[:, :], func=AF.Exp), r=[G.k()], w=[E1.k()])
        P.op("act", lambda e: e.activation(out=E2[:, :], in_=G[:, :], func=AF.Exp, scale=-1.0), r=[G.k()], w=[E2.k()])
        P.op("act", lambda e: e.activation(out=E3[:, :], in_=GE[:, :], func=AF.Exp), r=[GE.k()], w=[E3.k()])
        P.op("act", lambda e: e.activation(out=E4[:, :], in_=G[:, :], func=AF.Exp, scale=-1.0, bias=gcol), r=[G.k()], w=[E4.k()])
        P.op("act", lambda e: e.activation(out=col[:, 0:1], in_=gcol, func=AF.Exp), r=[G.k()], w=[col.k()])
        yield
        P.op("dve", lambda e: e.tensor_tensor(out=AR[:, 0:64], in0=A[:, tsl], in1=E3[:, :], op=ALU.mult), r=[A.k(), E3.k()], w=[AR.k(0)])
        P.op("dve", lambda e: e.tensor_tensor(out=AR[:, 64:128], in0=R[:, tsl], in1=E1[:, :], op=ALU.mult), r=[R.k(), E1.k()], w=[AR.k(1)])
        P.op("dve", lambda e: e.tensor_tensor(out=BK[:, 0:64], in0=BB[d][:, tsl], in1=E2[:, :], op=ALU.mult), r=[BB[d].k(), E2.k()], w=[BK.k(0)])
        P.op("dve", lambda e: e.tensor_tensor(out=BK[:, 64:128], in0=KD[d][:, tsl], in1=E2[:, :], op=ALU.mult), r=[KD[d].k(), E2.k()], w=[BK.k(1)])
        P.op("dve", lambda e: e.tensor_tensor(out=BKh[:, 0:64], in0=BB[d][:, tsl], in1=E4[:, :], op=ALU.mult), r=[BB[d].k(), E4.k()], w=[BKh.k(0)])
        P.op("dve", lambda e: e.tensor_tensor(out=BKh[:, 64:128], in0=KD[d][:, tsl], in1=E4[:, :], op=ALU.mult), r=[KD[d].k(), E4.k()], w=[BKh.k(1)])
        yield
        for hh in range(2):
            rw = H[hh]
            P.op("pe", lambda e: e.matmul(pA[rw, 0:128], lhsT=BK[rw, 0:64], rhs=AR[rw, 0:128], start=True, stop=True), r=[BK.k(), AR.k()], w=[pA.k()])
            P.op("pe", lambda e: e.matmul(pA[rw, 128:256], lhsT=BK[rw, 64:128], rhs=AR[rw, 0:128], start=True, stop=True), r=[BK.k(), AR.k()], w=[pA.k()])
            P.op("pe", lambda e: e.matmul(pA[rw, 256:320], lhsT=AR[rw, 0:64], rhs=BK[rw, 0:64], start=True, stop=True), r=[BK.k(), AR.k()], w=[pA.k()])
            P.op("pe", lambda e: e.matmul(pT[rw, 0:64], lhsT=BKh[rw, 0:64], rhs=ident[rw, 64 * hh:64 * hh + 64], start=True, stop=True), r=[BKh.k(), ident.k()], w=[pT.k()])
            P.op("pe", lambda e: e.matmul(pT[rw, 64:128], lhsT=BKh[rw, 64:128], rhs=ident[rw, 64 * hh:64 * hh + 64], start=True, stop=True), r=[BKh.k(), ident.k()], w=[pT.k()])
        yield
        P.op("dve", lambda e: e.tensor_tensor(out=AM[:, :], in0=pA[:, 0:320], in1=masks[d][:, :], op=ALU.mult), r=[pA.k(), masks[d].k()], w=[AM.k()])
        P.op("act", lambda e: e.activation(out=BKT[:, :], in_=pT[:, 0:128], func=AF.Copy), r=[pT.k()], w=[BKT.k()])
        P.op("dve", lambda e: e.tensor_tensor(out=Z[:, :], in0=AM[:, 0:64], in1=id64[:, :], op=ALU.add), r=[AM.k(), id64.k()], w=[Z.k()])
        yield
        Pc, PTc = AM[:, 0:64], AM[:, 256:320]
        Pk = AM.k()
        for lvl in range(1, 6):
            PPn = s["PP%d" % (lvl % 2)]
            for hh in range(2):
                rw = H[hh]
                P.op("pe", lambda e: e.matmul(pL[rw, 0:64], lhsT=Pc[rw, :], rhs=PTc[rw, :], start=True, stop=True), r=[Pk], w=[pL.k()])
                if lvl < 5:
                    P.op("pe", lambda e: e.matmul(pL[rw, 64:128], lhsT=PTc[rw, :], rhs=Pc[rw, :], start=True, stop=True), r=[Pk], w=[pL.k()])
            yield
            wcols = 128 if lvl < 5 else 64
            P.op("act", lambda e: e.activation(out=PPn[:, 0:wcols], in_=pL[:, 0:wcols], func=AF.Copy), r=[pL.k()], w=[PPn.k()])
            yield
            PTc, Pc, Pk = PPn[:, 0:64], PPn[:, 64:128], PPn.k()
            for hh in range(2):
                rw = H[hh]
                P.op("pe", lambda e: e.matmul(pL[rw, 128:192], lhsT=PTc[rw, :], rhs=Z[rw, :], start=True, stop=True), r=[Pk, Z.k()], w=[pL.k()])
            yield
            P.op("dve", lambda e: e.tensor_tensor(out=Z[:, :], in0=Z[:, :], in1=pL[:, 128:192], op=ALU.add), r=[Z.k(), pL.k()], w=[Z.k()])
            yield

    def dep(d, c, q, first):
        s = sts[d]
        pA, pL, pW, pT = s["ps"]
        Wsb, Usb, ST = s["Wsb"], s["Usb"], s["ST"]
        AR, AM, Z, BKT, col = (s[x + str(q)] for x in ("AR", "AM", "Z", "BKT", "col"))
        for hh in range(2):
            rw = H[hh]
            P.op("pe", lambda e: e.matmul(pW[rw, 0:64], lhsT=AR[rw, 0:64], rhs=ST[rw, :], start=True, stop=False), r=[AR.k(), ST.k()], w=[pW.k()])
            P.op("pe", lambda e: e.matmul(pW[rw, 0:64], lhsT=AM[rw, 128:192], rhs=vtm[rw, c, :], start=False, stop=True), r=[AM.k(), vtm.k()], w=[pW.k()])
        yield
        P.op("act", lambda e: e.activation(out=Wsb[:, :], in_=pW[:, 0:64], func=AF.Copy), r=[pW.k()], w=[Wsb.k()])
        yield
        for hh in range(2):
            rw = H[hh]
            P.op("pe", lambda e: e.matmul(pW[rw, 64:128], lhsT=Z[rw, :], rhs=Wsb[rw, :], start=True, stop=True), r=[Z.k(), Wsb.k()], w=[pW.k()])
        yield
        P.op("act", lambda e: e.activation(out=Usb[:, :], in_=pW[:, 64:128], func=AF.Copy), r=[pW.k()], w=[Usb.k()])
        yield
        for hh in range(2):
            rw = H[hh]
            P.op("pe", lambda e: e.matmul(pW[rw, 192:256], lhsT=BKT[rw, 0:64], rhs=Usb[rw, :], start=True, stop=False), r=[BKT.k(), Usb.k()], w=[pW.k()])
            P.op("pe", lambda e: e.matmul(pW[rw, 192:256], lhsT=BKT[rw, 64:128], rhs=vtm[rw, c, :], start=False, stop=True), r=[BKT.k(), vtm.k()], w=[pW.k()])
        for hh in range(2):
            rw = H[hh]
            P.op("pe", lambda e: e.matmul(pW[rw, 128:192], lhsT=AR[rw, 64:128], rhs=ST[rw, :], start=True, stop=False), r=[AR.k(), ST.k()], w=[pW.k()])
            P.op("pe", lambda e: e.matmul(pW[rw, 128:192], lhsT=AM[rw, 64:128], rhs=Usb[rw, :], start=False, stop=False), r=[AM.k(), Usb.k()], w=[pW.k()])
            P.op("pe", lambda e: e.matmul(pW[rw, 128:192], lhsT=AM[rw, 192:256], rhs=vtm[rw, c, :], start=False, stop=True), r=[AM.k(), vtm.k()], w=[pW.k()])
        yield
        P.op("dve", lambda e: e.scalar_tensor_tensor(out=ST[:, :], in0=ST[:, :], scalar=col[:, 0:1], in1=pW[:, 192:256], op0=ALU.mult, op1=ALU.add),
             r=[ST.k(), col.k(), pW.k()], w=[ST.k()])
        if first:
            P.op("dve", lambda e: e.tensor_copy(out=yacc[:, c, :], in_=pW[:, 128:192]), r=[pW.k()], w=[yacc.k(c)])
        else:
            P.op("dve", lambda e: e.tensor_tensor(out=yacc[:, c, :], in0=yacc[:, c, :], in1=pW[:, 128:192], op=ALU.add), r=[pW.k(), yacc.k(c)], w=[yacc.k(c)])
        yield

    def run_interleaved(gens):
        gens = list(gens)
        while gens:
            for g in list(gens):
                try:
                    next(g)
                except StopIteration:
                    gens.remove(g)

    order_f = list(range(NC64))
    order_b = list(range(nctx_c - 1, -1, -1)) + list(range(NC64 - 1, nctx_c - 1, -1))
    orders = (order_f, order_b)
    done = set()
    run_interleaved([indep(d, orders[d][0], 0) for d in range(2)])
    for i in range(NC64):
        gens = []
        for d in range(2):
            c = orders[d][i]
            gens.append(dep(d, c, i % 2, c not in done))
            done.add(c)
        if i + 1 < NC64:
            for d in range(2):
                gens.append(indep(d, orders[d][i + 1], (i + 1) % 2))
        run_interleaved(gens)
    P.pop()

    P.push()
    w_g = P.sbuf(tag + "wg", [128, KC, 128], BF16)
    stage = [P.sbuf(tag + "owst%d" % i, [128, KC, 64], F32) for i in range(2)]
    hTc = [P.sbuf(tag + "ohTc0", [128, KC, 256], BF16)] * 2
    lnw = P.sbuf(tag + "lnw", [128, 64], F32)
    lnb = P.sbuf(tag + "lnb", [128, 64], F32)
    gt = P.sbuf(tag + "gt", [128, 4, 64], F32)
    sc = P.sbuf(tag + "sc", [128, 8], F32)
    cen = P.sbuf(tag + "cen", [128, 64], F32)
    sq2 = P.sbuf(tag + "sq2", [128, 64], F32)
    yn = P.sbuf(tag + "yn", [128, 64], F32)
    yo4 = [P.sbuf(tag + "yo4_%d" % i, [128, 4, 64], BF16) for i in range(2)]
    sc4 = P.sbuf(tag + "sc4", [128, 5, 4], F32)
    cen4 = P.sbuf(tag + "cen4", [128, 4, 64], F32)
    sq4 = P.sbuf(tag + "sq4", [128, 4, 64], F32)
    pg = [P.psum(tag + "pg%d" % i, [128, 512], F32) for i in range(2)]
    P.dma("sp", lnw[:], io["lnw"][:, :], w=[lnw.k()])
    P.dma("sp", lnb[:], io["lnb"][:, :], w=[lnb.k()])
    load_w_bf16(P, io["w_g"], w_g, 128, stage, "g", blk=64)
    for ci, (t0, n, sa, sb) in enumerate(seg_chunks(T, n_ctx)):
        hc = hTc[ci % 2]
        P.dma("sp", hc[:, :, 0:n], hT[:, :, t0:t0 + n].rearrange("k p t -> p k t"), w=[hc.k()])
        pgc = pg[ci % 2]
        nq = n // 64
        for hh in range(2):
            rows = slice(64 * hh, 64 * hh + 64)
            for q in range(nq):
                for kc in range(KC):
                    P.op("pe", lambda e: e.matmul(pgc[rows, q * 64:(q + 1) * 64], lhsT=hc[:, kc, q * 64:(q + 1) * 64], rhs=w_g[:, kc, hh * 64:(hh + 1) * 64],
                                                   start=(kc == 0), stop=(kc == KC - 1)), r=[hc.k(), w_g.k()], w=[pgc.k()])
        P.op("act", lambda e: e.activation(out=gt[:, 0:nq, :], in_=pgc[:, 0:nq * 64].rearrange("p (q v) -> p q v", v=64), func=AF.Silu), r=[pgc.k()], w=[gt.k()])
        c0 = t0 // 64
        y4 = yacc[:, c0:c0 + nq, :]
        bshape = [128, nq, 64]
        P.op("dve", lambda e: e.reduce_sum(out=sc4[:, 0, 0:nq], in_=y4, axis=AX.X), r=[yacc.k()], w=[sc4.k(0)])
        P.op("dve", lambda e: e.tensor_scalar(out=sc4[:, 1, 0:nq], in0=sc4[:, 0, 0:nq], scalar1=-1.0 / 64, scalar2=None, op0=ALU.mult), r=[sc4.k(0)], w=[sc4.k(1)])
        P.op("dve", lambda e: e.tensor_tensor(out=cen4[:, 0:nq, :], in0=y4, in1=sc4[:, 1, 0:nq].unsqueeze(2).to_broadcast(bshape), op=ALU.add),
             r=[yacc.k(), sc4.k(1)], w=[cen4.k()])
        P.op("dve", lambda e: e.tensor_tensor(out=sq4[:, 0:nq, :], in0=cen4[:, 0:nq, :], in1=cen4[:, 0:nq, :], op=ALU.mult), r=[cen4.k()], w=[sq4.k()])
        P.op("dve", lambda e: e.reduce_sum(out=sc4[:, 2, 0:nq], in_=sq4[:, 0:nq, :], axis=AX.X), r=[sq4.k()], w=[sc4.k(2)])
        P.op("act", lambda e: e.activation(out=sc4[:, 3, 0:nq], in_=sc4[:, 2, 0:nq], func=AF.Sqrt, scale=1.0 / 64, bias=RW_GN_EPS), r=[sc4.k(2)], w=[sc4.k(3)])
        P.op("dve", lambda e: e.reciprocal(out=sc4[:, 4, 0:nq], in_=sc4[:, 3, 0:nq]), r=[sc4.k(3)], w=[sc4.k(4)])
        P.op("dve", lambda e: e.tensor_tensor(out=cen4[:, 0:nq, :], in0=cen4[:, 0:nq, :], in1=sc4[:, 4, 0:nq].unsqueeze(2).to_broadcast(bshape), op=ALU.mult),
             r=[cen4.k(), sc4.k(4)], w=[cen4.k()])
        P.op("dve", lambda e: e.tensor_tensor(out=cen4[:, 0:nq, :], in0=cen4[:, 0:nq, :], in1=lnw[:, :].unsqueeze(1).to_broadcast(bshape), op=ALU.mult),
             r=[cen4.k(), lnw.k()], w=[cen4.k()])
        P.op("dve", lambda e: e.tensor_tensor(out=cen4[:, 0:nq, :], in0=cen4[:, 0:nq, :], in1=lnb[:, :].unsqueeze(1).to_broadcast(bshape), op=ALU.add),
             r=[cen4.k(), lnb.k()], w=[cen4.k()])
        P.op("dve", lambda e: e.tensor_tensor(out=sq4[:, 0:nq, :], in0=vtm[:, c0:c0 + nq, :], in1=bon[:, c0:c0 + nq].unsqueeze(2).to_broadcast(bshape), op=ALU.mult),
             r=[vtm.k(), bon.k()], w=[sq4.k()])
        P.op("dve", lambda e: e.tensor_tensor(out=cen4[:, 0:nq, :], in0=cen4[:, 0:nq, :], in1=sq4[:, 0:nq, :], op=ALU.add), r=[cen4.k(), sq4.k()], w=[cen4.k()])
        y2 = yo4[ci % 2]
        P.op("dve", lambda e: e.tensor_tensor(out=y2[:, 0:nq, :], in0=cen4[:, 0:nq, :], in1=gt[:, 0:nq, :], op=ALU.mult), r=[cen4.k(), gt.k()], w=[y2.k()])
        for hh in range(2):
            P.dma("pool", io["ya"][t0:t0 + n, hh * 64:(hh + 1) * 64].rearrange("(q p) v -> p q v", p=64), y2[64 * hh:64 * hh + 64, 0:nq, :],
                  r=[y2.k()], w=[io["ya"].k(ci, hh)])
    P.pop()
    P.pop()


KC = 16
EPS = 1e-6


def ca_program(P, NTOK, chunks, io, do_merge, mode):
    modc = P.sbuf("c_modc", [128, KC, 8], F32)
    gsc = P.sbuf("c_gsc", [128, KC, 2], F32)
    ones = P.sbuf("c_ones", [128, 128], F32)
    P.dma("sp", modc[:], io["modc"][:, :, :], w=[modc.k()])
    P.dma("sp", ones[:], io["ones"][:, :], w=[ones.k()])
    for j in range(2):
        P.op("dve", lambda e: e.scalar_tensor_tensor(out=gsc[:, :, j:j + 1], in0=modc[:, :, 2 + j:3 + j], scalar=1.0, in1=modc[:, :, 6:7],
                                                      op0=ALU.add, op1=ALU.mult), r=[modc.k()], w=[gsc.k(j)])
    if do_merge:
        mergedT = P.sbuf("c_mergedT", [128, KC, NTOK], BF16)
        P.push()
        hT = P.sbuf("c_hT", [128, KC, NTOK], BF16)
        yT = P.sbuf("c_yT", [128, 24, NTOK], BF16)
        P.dma("sp", hT[:], io["hT"][:, :, :].rearrange("k p t -> p k t"), w=[hT.k()])
        for q in range(3):
            P.dma("sp", yT[:, q * 8:(q + 1) * 8, :], io["yT"][q * 8:(q + 1) * 8, :, :].rearrange("k p t -> p k t"), w=[yT.k(q)])
        stg = [P.sbuf("c_stg%d" % i, [128, KC, 128], F32) for i in range(2)]
        stb = [P.sbuf("c_stb0", [128, 24, 128], F32)] * 2
        wm = [P.sbuf("c_wm%d" % i, [128, KC, 384], BF16) for i in range(2)]
        wb = [P.sbuf("c_wb%d" % i, [128, 24, 128], BF16) for i in range(2)]
        sig = [P.sbuf("c_sig%d" % i, [128, 512], F32) for i in range(2)]
        macc = P.sbuf("c_macc", [128, 512], F32)
        prod = P.sbuf("c_prod", [128, 512], F32)
        pg = [P.psum("c_pg%d" % i, [128, 512], F32) for i in range(2)]
        pbr = [P.psum("c_pbr%d" % i, [128, 512], F32) for i in range(2)]
        si = 0
        it = 0
        for db in range(KC):
            wmb, wbb = wm[db % 2], wb[db % 2]
            for nb in range(3):
                st = stg[si % 2]
                si += 1
                c0 = nb * 2048 + db * 128
                P.dma("sp", st[:], io["w_merge"][db, nb, :, :, :], w=[st.k()])
                if nb % 2 == 0:
                    P.op("dve", lambda e: e.tensor_copy(out=wmb[:, :, nb * 128:(nb + 1) * 128], in_=st[:]), r=[st.k()], w=[wmb.k(nb)])
                else:
                    P.op("act", lambda e: e.activation(out=wmb[:, :, nb * 128:(nb + 1) * 128], in_=st[:], func=AF.Copy), r=[st.k()], w=[wmb.k(nb)])
            sb_ = stb[db % 2]
            P.dma("sp", sb_[:], io["w_branch"][db, :, :, :], w=[sb_.k()])
            P.op("act", lambda e: e.activation(out=wbb[:], in_=sb_[:], func=AF.Copy), r=[sb_.k()], w=[wbb.k()])
            for (t0, n, isc) in chunks:
                for nb in range(3):
                    g_, b_ = pg[it % 2], pbr[it % 2]
                    sg = sig[it % 2]
                    it += 1
                    for kc in range(KC):
                        P.op("pe", lambda e: e.matmul(g_[:, 0:n], lhsT=wmb[:, kc, nb * 128:(nb + 1) * 128], rhs=hT[:, kc, t0:t0 + n],
                                                       start=(kc == 0), stop=(kc == KC - 1)), r=[wmb.k(nb), hT.k()], w=[g_.k()])
                    for cc in range(8):
                        P.op("pe", lambda e: e.matmul(b_[:, 0:n], lhsT=wbb[:, nb * 8 + cc, :], rhs=yT[:, nb * 8 + cc, t0:t0 + n],
                                                       start=(cc == 0), stop=(cc == 7)), r=[wbb.k(), yT.k(nb)], w=[b_.k()])
                    P.op("act", lambda e: e.activation(out=sg[:, 0:n], in_=g_[:, 0:n], func=AF.Sigmoid), r=[g_.k()], w=[sg.k()])
                    if nb == 0:
                        P.op("dve", lambda e: e.tensor_tensor(out=macc[:, 0:n], in0=sg[:, 0:n], in1=b_[:, 0:n], op=ALU.mult), r=[sg.k(), b_.k()], w=[macc.k()])
                    else:
                        P.op("dve", lambda e: e.tensor_tensor(out=prod[:, 0:n], in0=sg[:, 0:n], in1=b_[:, 0:n], op=ALU.mult), r=[sg.k(), b_.k()], w=[prod.k()])
                        if nb == 1:
                            P.op("dve", lambda e: e.tensor_tensor(out=macc[:, 0:n], in0=macc[:, 0:n], in1=prod[:, 0:n], op=ALU.add), r=[macc.k(), prod.k()], w=[macc.k()])
                        else:
                            P.op("dve", lambda e: e.tensor_tensor(out=mergedT[:, db, t0:t0 + n], in0=macc[:, 0:n], in1=prod[:, 0:n], op=ALU.add),
                                 r=[macc.k(), prod.k()], w=[mergedT.k(db, t0)])
        P.pop()

    P.push()
    znew = P.sbuf("c_znew", [128, KC, NTOK], F32)
    sq = [P.sbuf("c_sq%d" % i, [128, 512], F32) for i in range(2)]
    pss = [P.psum("c_pss%d" % i, [128, 512], F32) for i in range(len(chunks))]
    if do_merge:
        ze = [P.sbuf("c_ze%d" % i, [128, NTOK], F32) for i in range(2)]
        sto = [P.sbuf("c_sto%d" % i, [128, KC, 128], F32) for i in range(2)]
        wo = [P.sbuf("c_wo%d" % i, [128, KC, 128], BF16) for i in range(2)]
        po = [P.psum("c_po%d" % i, [128, 512], F32) for i in range(2)]
    it = 0
    for eb in range(KC):
        if do_merge:
            st, wob, z_e = sto[eb % 2], wo[eb % 2], ze[eb % 2]
            P.dma("sp", st[:], io["w_out"][eb, :, :, :], w=[st.k()])
            if eb % 2 == 0:
                P.op("act", lambda e: e.activation(out=wob[:], in_=st[:], func=AF.Copy), r=[st.k()], w=[wob.k()])
            else:
                P.op("dve", lambda e: e.tensor_copy(out=wob[:], in_=st[:]), r=[st.k()], w=[wob.k()])
            P.dma("sp", z_e[:], io["zT"][eb, :, :], w=[z_e.k()])
        else:
            P.dma("sp", znew[:, eb, :], io["zT"][eb, :, :], w=[znew.k(eb)])
        for ci, (t0, n, isc) in enumerate(chunks):
            if do_merge:
                p_ = po[it % 2]
                for db in range(KC):
                    P.op("pe", lambda e: e.matmul(p_[:, 0:n], lhsT=wob[:, db, :], rhs=mergedT[:, db, t0:t0 + n], start=(db == 0), stop=(db == KC - 1)),
                         r=[wob.k(), mergedT.k()], w=[p_.k()])
                gcol = modc[:, eb, 0:1] if isc else modc[:, eb, 1:2]
                P.op("dve", lambda e: e.scalar_tensor_tensor(out=znew[:, eb, t0:t0 + n], in0=p_[:, 0:n], scalar=gcol, in1=z_e[:, t0:t0 + n],
                                                              op0=ALU.mult, op1=ALU.add), r=[p_.k(), modc.k(), z_e.k()], w=[znew.k(eb, t0)])
            s_ = sq[it % 2]
            it += 1
            P.op("act", lambda e: e.activation(out=s_[:, 0:n], in_=znew[:, eb, t0:t0 + n], func=AF.Square), r=[znew.k(eb)], w=[s_.k()])
            P.op("pe", lambda e: e.matmul(pss[ci][:, 0:n], lhsT=ones[:, :], rhs=s_[:, 0:n], start=(eb == 0), stop=(eb == KC - 1)),
                 r=[ones.k(), s_.k()], w=[pss[ci].k()])
        if do_merge and mode == "mod":
            P.dma("pool", io["zTn"][eb, :, :], znew[:, eb, :], r=[znew.k(eb)], w=[io["zTn"].k(eb)])
    rstd = P.sbuf("c_rstd", [128, NTOK], F32)
    for ci, (t0, n, isc) in enumerate(chunks):
        P.op("act", lambda e: e.activation(out=rstd[:, t0:t0 + n], in_=pss[ci][:, 0:n], func=AF.Sqrt, scale=1.0 / 2048, bias=EPS), r=[pss[ci].k()], w=[rstd.k(ci)])
        P.op("dve", lambda e: e.reciprocal(out=rstd[:, t0:t0 + n], in_=rstd[:, t0:t0 + n]), r=[rstd.k(ci)], w=[rstd.k(ci)])
    odt = BF16 if mode == "mod" else F32
    hn = [P.sbuf("c_hn%d" % i, [128, NTOK], F32) for i in range(2)]
    ho = [P.sbuf("c_ho%d" % i, [128, NTOK], odt) for i in range(2)]
    oname = "hTn" if mode == "mod" else "oT"
    for eb in range(KC):
        h1, h2 = hn[eb % 2], ho[eb % 2]
        for ci, (t0, n, isc) in enumerate(chunks):
            j = 0 if isc else 1
            P.op("dve", lambda e: e.scalar_tensor_tensor(out=h1[:, t0:t0 + n], in0=znew[:, eb, t0:t0 + n], scalar=gsc[:, eb, j:j + 1], in1=rstd[:, t0:t0 + n],
                                                          op0=ALU.mult, op1=ALU.mult), r=[znew.k(eb), gsc.k(), rstd.k(ci)], w=[h1.k(ci)])
            P.op("dve", lambda e: e.tensor_scalar(out=h2[:, t0:t0 + n], in0=h1[:, t0:t0 + n], scalar1=modc[:, eb, 4 + j:5 + j], scalar2=None, op0=ALU.add),
                 r=[h1.k(ci), modc.k()], w=[h2.k(ci)])
        P.dma("pool", io[oname][eb, :, :], h2[:, :], r=[h2.k()], w=[io[oname].k(eb)])
    P.pop()


KC = 16


def mod_program(P, io):
    cT = P.sbuf("mm_cT", [128, KC, 3], F32)
    sT = P.sbuf("mm_sT", [128, KC, 3], F32)
    wa = [P.sbuf("mm_wa%d" % l, [128, KC, 768], F32) for l in range(2)]
    ba = P.sbuf("mm_ba", [3, 2, 768], F32)
    mo = P.sbuf("mm_mo", [3, 2, 768], F32)
    pm = [P.psum("mm_p%d" % i, [128, 512], F32) for i in range(2)]
    P.dma("sp", cT[:], io["cT"][:, :, :], w=[cT.k()])
    for l in range(2):
        P.dma("sp", wa[l][:], io["wa"][l, :, :, :], w=[wa[l].k()])
        P.dma("sp", ba[:, l, :], io["ba"][l, :, :], w=[ba.k(l)])
    P.op("act", lambda e: e.activation(out=sT[:], in_=cT[:], func=AF.Silu), r=[cT.k()], w=[sT.k()])
    it = 0
    for l in range(2):
        for hf in range(2):
            p_ = pm[it % 2]
            it += 1
            for kc in range(KC):
                P.op("pe", lambda e: e.matmul(p_[0:3, 0:384], lhsT=sT[:, kc, 0:3], rhs=wa[l][:, kc, hf * 384:(hf + 1) * 384],
                                               start=(kc == 0), stop=(kc == KC - 1)), r=[sT.k(), wa[l].k()], w=[p_.k()])
            P.op("dve", lambda e: e.tensor_tensor(out=mo[:, l, hf * 384:(hf + 1) * 384], in0=p_[0:3, 0:384], in1=ba[:, l, hf * 384:(hf + 1) * 384], op=ALU.add),
                 r=[p_.k(), ba.k(l)], w=[mo.k(l, hf)])
        P.dma("pool", io["mod"][l, :, :], mo[:, l, :], r=[mo.k(l)], w=[io["mod"].k(l)])

import ml_dtypes
from concourse.bass_utils import run_bass_kernel_spmd

NCORE = 8
DM = 2048
N_CTX = 256
N_LAT = 4096
TT = N_CTX + N_LAT
NTOK = TT // 4
CA_CHUNKS = [(0, 256, True), (256, 512, False), (768, 320, False)]
NPBF = ml_dtypes.bfloat16


def _wl(wc):
    return np.ascontiguousarray(wc.reshape(-1, 128, wc.shape[1]).transpose(1, 0, 2))


def _wlb(wc, blk):
    K, n = wc.shape
    nb = -(-n // blk)
    if nb * blk != n:
        wc = np.concatenate([wc, np.zeros((K, nb * blk - n), wc.dtype)], axis=1)
    return np.ascontiguousarray(wc.reshape(K // 128, 128, nb, blk).transpose(2, 1, 0, 3))


def _fm(a):
    return np.ascontiguousarray(a.T.reshape(-1, 128, a.shape[0]))


def _partner(d):
    return d + 32 if (d % 64) < 32 else d - 32


def _rope_tables(T, n_ctx):
    n_lat = T - n_ctx
    rows = n_lat // 64
    row = np.repeat(np.arange(rows), 64).astype(np.float32)
    col = np.tile(np.arange(64), rows).astype(np.float32)
    inv_freq = (np.float32(10000.0) ** (-np.arange(0, 64, 2, dtype=np.float32) / np.float32(64))).astype(np.float32)
    ang_lat = np.stack([row[:, None] * inv_freq, col[:, None] * inv_freq], axis=1)
    ang = np.concatenate([np.zeros((n_ctx, 2, 32), np.float32), ang_lat], axis=0)
    cos, sin = np.cos(ang).astype(np.float32), np.sin(ang).astype(np.float32)
    cosT = np.zeros((128, T), np.float32)
    sinT = np.zeros((128, T), np.float32)
    for d in range(128):
        a, half, i = d // 64, (d % 64) // 32, d % 32
        cosT[d] = cos[:, a, i]
        sinT[d] = sin[:, a, i] * (-1.0 if half == 0 else 1.0)
    return cosT, sinT


def _consts():
    c = {}
    pm = np.zeros((128, 128), np.float32)
    for d in range(128):
        pm[_partner(d), d] = 1.0
    c["pm"] = pm
    c["ones"] = np.ones((128, 128), np.float32)
    u = np.arange(128)
    c["tri_f"] = (u[:, None] <= u[None, :]).astype(np.float32)
    c["tri_b"] = (u[:, None] >= u[None, :]).astype(np.float32)
    c["mneg_f"] = np.where(u[:, None] <= u[None, :], 0.0, -30000.0).astype(np.float32)
    c["mneg_b"] = np.where(u[:, None] >= u[None, :], 0.0, -30000.0).astype(np.float32)
    i = np.arange(64)
    sT_f = (i[:, None] < i[None, :]).astype(np.float32)
    iT_f = (i[:, None] <= i[None, :]).astype(np.float32)
    st_f = (i[None, :] < i[:, None]).astype(np.float32)
    sT_b = (i[:, None] > i[None, :]).astype(np.float32)
    iT_b = (i[:, None] >= i[None, :]).astype(np.float32)
    st_b = (i[None, :] > i[:, None]).astype(np.float32)
    c["mask_f"] = np.tile(np.concatenate([sT_f, iT_f, sT_f, iT_f, st_f], axis=1), (2, 1))
    c["mask_b"] = np.tile(np.concatenate([sT_b, iT_b, sT_b, iT_b, st_b], axis=1), (2, 1))
    bones = np.zeros((128, 128), np.float32)
    bones[:64, :64] = 1
    bones[64:, 64:] = 1
    c["bones"] = bones
    c["ident"] = np.eye(128, dtype=np.float32)
    c["id64"] = np.tile(np.eye(64, dtype=np.float32), (2, 1))
    c["onesc"] = np.ones((128, 64), np.float32)
    c["cosT"], c["sinT"] = _rope_tables(TT, N_CTX)
    return c


_PROGS = {}


def _declare(P, specs):
    io = {}
    for name, (shape, dt, kind) in specs.items():
        io[name] = P.dram(name, list(shape), dt, kind=kind)
    return io


B_IN = {
    "hT": ([16, 128, TT], BF16),
    "g_w_fm": ([3, 128, 16, 128], F32), "g_w_tm": ([3, 128, 16, 128], F32), "cosT": ([128, TT], F32), "sinT": ([128, TT], F32),
    "pm": ([128, 128], F32), "ones": ([128, 128], F32), "gvec": ([128, 4], F32),
    "m_w_fm": ([2, 128, 16, 128], F32), "m_w_tm": ([8, 128, 16, 128], F32), "gbias": ([128, 4], F32), "normg": ([128, 256], F32),
    "tri_f": ([128, 128], F32), "tri_b": ([128, 128], F32), "mneg_f": ([128, 128], F32), "mneg_b": ([128, 128], F32),
    "ident": ([128, 128], F32), "id64": ([128, 64], F32), "bones": ([128, 128], F32), "onesc": ([128, 64], F32),
    "mask_f": ([128, 320], F32), "mask_b": ([128, 320], F32),
}
for _p in range(2):
    B_IN.update({"r%d_w_fm" % _p: ([10, 128, 16, 64], F32), "r%d_w_g" % _p: ([2, 128, 16, 64], F32), "r%d_mu" % _p: ([128, 10], F32),
                 "r%d_wup" % _p: ([128, 128], F32), "r%d_aup" % _p: ([128, 128], F32), "r%d_pcol" % _p: ([128, 8], F32),
                 "r%d_lnw" % _p: ([128, 64], F32), "r%d_lnb" % _p: ([128, 64], F32)})
B_OUT = {"yb": ([TT, 256], BF16), "yc": ([TT, 256], BF16), "ya0": ([TT, 128], BF16), "ya1": ([TT, 128], BF16)}


def _prog_B():
    if "B" in _PROGS:
        return _PROGS["B"]
    nc = bass.Bass("TRN2", target_bir_lowering=False)
    P = Prog(nc)
    specs = {k: (v[0], v[1], "ExternalInput") for k, v in B_IN.items()}
    specs.update({k: (v[0], v[1], "ExternalOutput") for k, v in B_OUT.items()})
    io = _declare(P, specs)
    P.push()
    gqa_program(P, TT, N_CTX, {"hT": io["hT"], "w_fm": io["g_w_fm"], "w_tm": io["g_w_tm"], "cosT": io["cosT"], "sinT": io["sinT"],
                               "pm": io["pm"], "ones": io["ones"], "gvec": io["gvec"], "yb": io["yb"]})
    P.pop()
    P.push()
    mlstm_program(P, TT, N_CTX, {"hT": io["hT"], "w_fm": io["m_w_fm"], "w_tm": io["m_w_tm"], "gbias": io["gbias"], "normg": io["normg"],
                                 "tri_f": io["tri_f"], "tri_b": io["tri_b"], "mneg_f": io["mneg_f"], "mneg_b": io["mneg_b"], "yc": io["yc"]})
    P.pop()
    for p in range(2):
        P.push()
        d = {"hT": io["hT"], "ya": io["ya%d" % p]}
        for k in ("ident", "id64", "bones", "onesc", "mask_f", "mask_b"):
            d[k] = io[k]
        for k in ("w_fm", "w_g", "mu", "wup", "aup", "pcol", "lnw", "lnb"):
            d[k] = io["r%d_%s" % (p, k)]
        rwkv_program(P, TT, N_CTX, d, tag="r%d_" % p)
        P.pop()
    P.finish()
    P.close()
    _PROGS["B"] = nc
    return nc


def _prog_CA(do_merge, mode):
    key = ("CA", do_merge, mode)
    if key in _PROGS:
        return _PROGS[key]
    nc = bass.Bass("TRN2", target_bir_lowering=False)
    P = Prog(nc)
    specs = {"zT": ([16, 128, NTOK], F32, "ExternalInput"), "modc": ([128, 16, 8], F32, "ExternalInput"), "ones": ([128, 128], F32, "ExternalInput")}
    if do_merge:
        specs.update({"hT": ([16, 128, NTOK], BF16, "ExternalInput"), "yT": ([24, 128, NTOK], BF16, "ExternalInput"),
                      "w_merge": ([16, 3, 128, 16, 128], F32, "ExternalInput"), "w_branch": ([16, 128, 24, 128], F32, "ExternalInput"),
                      "w_out": ([16, 128, 16, 128], F32, "ExternalInput")})
    if mode == "mod":
        specs["hTn"] = ([16, 128, NTOK], BF16, "ExternalOutput")
        if do_merge:
            specs["zTn"] = ([16, 128, NTOK], F32, "ExternalOutput")
    else:
        specs["oT"] = ([16, 128, NTOK], F32, "ExternalOutput")
    io = _declare(P, specs)
    ca_program(P, NTOK, CA_CHUNKS, io, do_merge, mode)
    P.finish()
    P.close()
    _PROGS[key] = nc
    return nc


def _prog_M():
    if "M" in _PROGS:
        return _PROGS["M"]
    nc = bass.Bass("TRN2", target_bir_lowering=False)
    P = Prog(nc)
    io = _declare(P, {"cT": ([128, 16, 3], F32, "ExternalInput"), "wa": ([2, 128, 16, 768], F32, "ExternalInput"),
                      "ba": ([2, 3, 768], F32, "ExternalInput"), "mod": ([2, 3, 768], F32, "ExternalOutput")})
    mod_program(P, io)
    P.finish()
    P.close()
    _PROGS["M"] = nc
    return nc


def _run(nc, in_maps):
    res = run_bass_kernel_spmd(nc, in_maps, core_ids=list(range(NCORE)))
    return res.results


def _b_inputs(l, b, j, hT_full, consts, w_in, I):
    m = {"hT": hT_full[b]}
    for k in ("cosT", "sinT", "pm", "ones", "tri_f", "tri_b", "mneg_f", "mneg_b", "ident", "id64", "bones", "onesc", "mask_f", "mask_b"):
        m[k] = consts[k]
    W = w_in[l]
    kv = j // 2
    m["g_w_fm"] = _wlb(np.concatenate([W[:, 4352 + 256 * j:4352 + 256 * j + 256], W[:, 5376 + 128 * kv:5376 + 128 * kv + 128]], axis=1), 128)
    m["g_w_tm"] = _wlb(np.concatenate([W[:, 5632 + 128 * kv:5632 + 128 * kv + 128], W[:, 5888 + 256 * j:5888 + 256 * j + 256]], axis=1), 128)
    pidx = np.array([_partner(d) for d in range(128)])
    gq, gk = I["at_q_g"][l], I["at_k_g"][l]
    m["gvec"] = np.ascontiguousarray(np.stack([gq, gq[pidx], gk, gk[pidx]], axis=1).astype(np.float32))
    m["m_w_fm"] = _wlb(np.concatenate([W[:, 6912 + 128 * j:6912 + 128 * j + 128], W[:, 7424 + 128 * j:7424 + 128 * j + 128]], axis=1), 128)
    gcols = [9984 + 4 * t + j for t in range(4)]
    m["m_w_tm"] = _wlb(np.concatenate([W[:, 7424 + 128 * j:7424 + 128 * j + 128], W[:, 7936 + 256 * j:7936 + 256 * j + 256], W[:, gcols],
                                      W[:, 8960 + 256 * j:8960 + 256 * j + 256], W[:, 10000 + 256 * j:10000 + 256 * j + 256]], axis=1), 128)
    m["gbias"] = np.ascontiguousarray(np.tile(I["ml_gate_b"][l][:, j][None, :], (128, 1)).astype(np.float32))
    m["normg"] = np.ascontiguousarray(np.tile(I["ml_norm_g"][l][256 * j:256 * j + 256][None, :], (128, 1)).astype(np.float32))
    for p in range(2):
        ch0 = 256 * j + 128 * p
        cols = np.concatenate([np.arange(ch0, ch0 + 128), 1024 + np.arange(ch0, ch0 + 128), 2048 + np.arange(ch0, ch0 + 128),
                               np.arange(3072, 3200), np.arange(3200, 3328)])
        m["r%d_w_fm" % p] = _wlb(W[:, cols], 64)
        m["r%d_w_g" % p] = _wlb(W[:, 3328 + ch0:3328 + ch0 + 128], 64)
        mu = I["shift_mu"][l][:, cols]
        m["r%d_mu" % p] = np.ascontiguousarray(np.concatenate([mu[0].reshape(5, 128).T, mu[1].reshape(5, 128).T], axis=1).astype(np.float32))
        m["r%d_wup" % p] = np.ascontiguousarray(I["rw_w_up"][l][:, :, ch0:ch0 + 128].reshape(128, 128))
        m["r%d_aup" % p] = np.ascontiguousarray(I["rw_a_up"][l][:, :, ch0:ch0 + 128].reshape(128, 128))
        sl = slice(ch0, ch0 + 128)
        m["r%d_pcol" % p] = np.ascontiguousarray(np.stack([I["rw_w0"][l][0][sl], I["rw_w0"][l][1][sl], I["rw_a0"][l][0][sl], I["rw_a0"][l][1][sl],
                                                            I["rw_k_k"][l][sl], I["rw_k_a"][l][sl], I["rw_r_k"][l].reshape(1024)[sl],
                                                            np.zeros(128, np.float32)], axis=1).astype(np.float32))
        m["r%d_lnw" % p] = np.ascontiguousarray(np.repeat(I["rw_ln_w"][l][sl].reshape(2, 1, 64), 64, axis=1).reshape(128, 64))
        m["r%d_lnb" % p] = np.ascontiguousarray(np.repeat(I["rw_ln_b"][l][sl].reshape(2, 1, 64), 64, axis=1).reshape(128, 64))
    return m


def _modc(gt, sc, sh, g, b, j):
    z = np.zeros((3, DM), np.float32)
    gt = z if gt is None else gt
    sc = z if sc is None else sc
    sh = z if sh is None else sh
    rc = 2 if j == 0 else b
    cols = [gt[rc], gt[b], sc[rc], sc[b], sh[rc], sh[b], g, np.zeros(DM, np.float32)]
    out = np.zeros((128, 16, 8), np.float32)
    for i, c in enumerate(cols):
        out[:, :, i] = np.asarray(c, np.float32).reshape(16, 128).T
    return out


def kernel(**I):
    I = {k: np.asarray(v) for k, v in I.items()}
    consts = _consts()
    cores = [(c // 4, c % 4) for c in range(NCORE)]
    cstack = np.stack([I["c"][0], I["c"][1], I["c_ctx"]], axis=0).astype(np.float32)
    cT = np.ascontiguousarray(cstack.T.reshape(16, 128, 3).transpose(1, 0, 2))
    in_maps = []
    for c in range(NCORE):
        cs = slice(768 * c, 768 * (c + 1))
        wa = np.stack([_wl(I["w_ada"][l][:, cs]) for l in range(2)], axis=0)
        ba = np.stack([np.tile(I["b_ada"][l][cs][None, :], (3, 1)) for l in range(2)], axis=0).astype(np.float32)
        in_maps.append({"cT": cT, "wa": np.ascontiguousarray(wa), "ba": np.ascontiguousarray(ba)})
    r = _run(_prog_M(), in_maps)
    mod = np.concatenate([np.asarray(r[c]["mod"]) for c in range(NCORE)], axis=2)
    sh = [mod[l][:, 0:DM] for l in range(2)]
    sc = [mod[l][:, DM:2 * DM] for l in range(2)]
    gt = [mod[l][:, 2 * DM:3 * DM] for l in range(2)]
    zfull = [np.concatenate([I["ctx"][b], I["x"][b]], axis=0) for b in range(2)]
    zT = [_fm(zfull[b][NTOK * j:NTOK * (j + 1)]) for (b, j) in cores]
    in_maps = [{"zT": zT[c], "modc": _modc(None, sc[0], sh[0], I["norm_g"][0], b, j), "ones": consts["ones"]} for c, (b, j) in enumerate(cores)]
    r = _run(_prog_CA(False, "mod"), in_maps)
    hT = [np.asarray(r[c]["hTn"]) for c in range(NCORE)]
    out = None
    for l in range(2):
        hT_full = [np.ascontiguousarray(np.concatenate(hT[4 * b:4 * b + 4], axis=2)) for b in range(2)]
        in_maps = [_b_inputs(l, b, j, hT_full, consts, I["w_in"], I) for (b, j) in cores]
        r = _run(_prog_B(), in_maps)
        yT = []
        for b in range(2):
            y = np.zeros((TT, 3, 1024), NPBF)
            for j in range(4):
                rr = r[4 * b + j]
                for p in range(2):
                    y[:, 0, 256 * j + 128 * p:256 * j + 128 * p + 128] = np.asarray(rr["ya%d" % p])
                y[:, 1, 256 * j:256 * j + 256] = np.asarray(rr["yb"])
                y[:, 2, 256 * j:256 * j + 256] = np.asarray(rr["yc"])
            yT.append(_fm(y.reshape(TT, 3072)))
        last = (l == 1)
        w_merge = np.ascontiguousarray(I["w_in"][l][:, 11024:17168].reshape(16, 128, 3, 16, 128).transpose(3, 2, 1, 0, 4))
        w_branch = _wlb(I["w_branch"][l].reshape(3072, DM), 128)
        w_out = _wlb(I["w_out"][l], 128)
        in_maps = []
        for c, (b, j) in enumerate(cores):
            ts = slice(NTOK * j, NTOK * (j + 1))
            if last:
                mc = _modc(gt[l], None, None, I["final_g"], b, j)
            else:
                mc = _modc(gt[l], sc[l + 1], sh[l + 1], I["norm_g"][l + 1], b, j)
            in_maps.append({"zT": zT[c], "modc": mc, "ones": consts["ones"], "hT": hT[c], "yT": np.ascontiguousarray(yT[b][:, :, ts]),
                            "w_merge": w_merge, "w_branch": w_branch, "w_out": w_out})
        r = _run(_prog_CA(True, "final" if last else "mod"), in_maps)
        if last:
            out = np.zeros((2, N_LAT, DM), np.float32)
            for c, (b, j) in enumerate(cores):
                o = np.asarray(r[c]["oT"]).reshape(DM, NTOK).T
                t0 = NTOK * j
                lo = max(t0, N_CTX)
                out[b, lo - N_CTX:t0 + NTOK - N_CTX] = o[lo - t0:]
        else:
            zT = [np.asarray(r[c]["zTn"]) for c in range(NCORE)]
            hT = [np.asarray(r[c]["hTn"]) for c in range(NCORE)]
    return out
```
